# Optimizing a Trainium2 kernel written in Bass

```python
import jax, jax.numpy as jnp
from jax import lax
import numpy as np


D_MODEL = 1024
BATCH = 8
SEQ = 4096
DEPTH = 2

GRID_W = 64
CTX_LEN = 256
W_A = 256
H_A = 4
DH_A = 64
HGRN_CHUNK = 16
W_B = 384
H_B = 4
DH_B = 96
MLSTM_CHUNK = 64
W_C = 384
H_C = 6
DH_C = 64
WIN_ROWS = 8
WIN_COLS = 16
W_MIX = W_A + W_B + W_C
ROPE_BASE = 10000.0
EPS = 1e-6
SPLITS = (('a_q', W_A), ('a_ff', W_A), ('a_fb', W_A), ('a_i', W_A), ('a_z', W_A),
          ('b_q', W_B), ('b_k', W_B), ('b_v', W_B), ('b_o', W_B), ('b_z', W_B), ('b_g', 4 * H_B),
          ('c_q', W_C), ('c_k', W_C), ('c_v', W_C), ('c_z', W_C))
P_IN = 5 * W_A + 5 * W_B + 4 * H_B + 4 * W_C

kernel_name = 'hybrid_hgrn2_mlstm_natten_block'


def rms(x, g):
    x32 = x.astype(jnp.float32)
    return x32 * lax.rsqrt(jnp.mean(x32 * x32, axis=-1, keepdims=True) + EPS) * g.astype(jnp.float32)


def heads(a, h):
    return a.reshape(*a.shape[:-1], h, a.shape[-1] // h)


def head_rms(o, g):
    o = o * lax.rsqrt(jnp.mean(o * o, axis=-1, keepdims=True) + EPS)
    return o.reshape(*o.shape[:-2], -1) * g.astype(jnp.float32)


def split_proj(p):
    out = {}
    off = 0
    for name, w in SPLITS:
        out[name] = p[..., off:off + w]
        off += w
    return out


def rope_2d(x):
    T, d = x.shape[1], x.shape[-1]
    t = jnp.arange(T)
    half = d // 2

    def rot(xp, pos):
        dp = xp.shape[-1]
        inv = ROPE_BASE ** (-jnp.arange(0, dp, 2, dtype=jnp.float32) / dp)
        ang = pos.astype(jnp.float32)[:, None] * inv[None, :]
        cos = jnp.cos(ang)[None, :, None, :]
        sin = jnp.sin(ang)[None, :, None, :]
        x1, x2 = xp[..., :dp // 2], xp[..., dp // 2:]
        return jnp.concatenate([x1 * cos - x2 * sin, x2 * cos + x1 * sin], axis=-1)

    return jnp.concatenate([rot(x[..., :half], t // GRID_W), rot(x[..., half:], t % GRID_W)], axis=-1)


def gla_chunked(q, k, v, logf, s0):
    B, T, H, dk = q.shape
    dv = v.shape[-1]
    L = HGRN_CHUNK
    N = T // L
    q = q.reshape(B, N, L, H, dk)
    k = k.reshape(B, N, L, H, dk)
    logf = logf.reshape(B, N, L, H, dk)
    v = v.reshape(B, N, L, H, dv)
    b = jnp.cumsum(logf, axis=2)
    mask = np.tril(np.ones((L, L), dtype=bool))[None, None, :, :, None, None]
    decay = jnp.exp(jnp.where(mask, b[:, :, :, None] - b[:, :, None, :], -jnp.inf))
    A = jnp.einsum('bnthk,bnshk,bntshk->bntsh', q, k, decay)
    o_intra = jnp.einsum('bntsh,bnshv->bnthv', A, v)
    bL = b[:, :, -1]
    U = jnp.einsum('bnshk,bnshv->bnhkv', k * jnp.exp(bL[:, :, None] - b), v)

    def step(S, inp):
        a, u = inp
        return a[..., None] * S + u, S

    s_fin, s_prev = lax.scan(step, s0, (jnp.moveaxis(jnp.exp(bL), 1, 0), jnp.moveaxis(U, 1, 0)))
    s_prev = jnp.moveaxis(s_prev, 0, 1)
    o_inter = jnp.einsum('bnthk,bnhkv->bnthv', q * jnp.exp(b), s_prev)
    return (o_intra + o_inter).reshape(B, T, H, dv), s_fin


def mlstm_chunked(q, k, v, ig, lf, state0):
    B, T, H, dk = q.shape
    dv = v.shape[-1]
    L = MLSTM_CHUNK
    N = T // L
    q = q.reshape(B, N, L, H, dk)
    k = k.reshape(B, N, L, H, dk)
    v = v.reshape(B, N, L, H, dv)
    ig = ig.reshape(B, N, L, H)
    lf = lf.reshape(B, N, L, H)
    b = jnp.cumsum(lf, axis=2)
    bL = b[:, :, -1]
    w = bL[:, :, None] + ig - b
    m_loc = jnp.max(w, axis=2)
    ew = jnp.exp(w - m_loc[:, :, None])
    C_loc = jnp.einsum('bnsh,bnshk,bnshv->bnhkv', ew, k, v)
    n_loc = jnp.einsum('bnsh,bnshk->bnhk', ew, k)

    def step(carry, inp):
        C, n, m = carry
        bl, ml, Cl, nl = inp
        m_new = jnp.maximum(bl + m, ml)
        a = jnp.exp(bl + m - m_new)
        e = jnp.exp(ml - m_new)
        C_new = a[..., None, None] * C + e[..., None, None] * Cl
        n_new = a[..., None] * n + e[..., None] * nl
        return (C_new, n_new, m_new), (C, n, m)

    xs = tuple(jnp.moveaxis(a, 1, 0) for a in (bL, m_loc, C_loc, n_loc))
    final, (C_prev, n_prev, m_prev) = lax.scan(step, state0, xs)
    C_prev = jnp.moveaxis(C_prev, 0, 1)
    n_prev = jnp.moveaxis(n_prev, 0, 1)
    m_prev = jnp.moveaxis(m_prev, 0, 1)
    mask = np.tril(np.ones((L, L), dtype=bool))[None, None, :, :, None]
    D = jnp.where(mask, b[:, :, :, None] - b[:, :, None, :] + ig[:, :, None], -jnp.inf)
    inter = b + m_prev[:, :, None]
    m_t = jnp.maximum(jnp.max(D, axis=3), inter)
    S = jnp.einsum('bnthk,bnshk->bntsh', q, k) * jnp.exp(D - m_t[:, :, :, None])
    e_in = jnp.exp(inter - m_t)
    num = jnp.einsum('bntsh,bnshv->bnthv', S, v) + e_in[..., None] * jnp.einsum('bnthk,bnhkv->bnthv', q, C_prev)
    den = jnp.sum(S, axis=3) + e_in * jnp.einsum('bnthk,bnhk->bnth', q, n_prev)
    h = num / jnp.maximum(jnp.abs(den), jnp.exp(-m_t))[..., None]
    return h.reshape(B, T, H, dv), final


def hgrn2_mixer(px, pc, lb, gn, need_ctx):
    lb = lb.reshape(2, H_A, DH_A)

    def direction(p, d, s0, reverse):
        q = heads(jax.nn.silu(p['a_q']), H_A) * (DH_A ** -0.5)
        v = heads(p['a_i'], H_A)
        fl = heads(p[('a_ff', 'a_fb')[d]], H_A)
        lbd = lb[d]
        logf = jnp.logaddexp(jnp.log(lbd), jnp.log1p(-lbd) + jax.nn.log_sigmoid(fl))
        k = (1.0 - lbd) * jax.nn.sigmoid(-fl)
        if reverse:
            q, k, v, logf = (jnp.flip(a, axis=1) for a in (q, k, v, logf))
        o, s = gla_chunked(q, k, v, logf, s0)
        if reverse:
            o = jnp.flip(o, axis=1)
        return o, s

    B = px['a_q'].shape[0]
    s0 = jnp.zeros((B, H_A, DH_A, DH_A), jnp.float32)
    oc_f, sc_f = direction(pc, 0, s0, False)
    oc_b, sc_b = direction(pc, 1, s0, True)
    ol_f, _ = direction(px, 0, sc_f, False)
    ol_b, _ = direction(px, 1, sc_b, True)
    y = head_rms(ol_f + ol_b, gn) * jax.nn.silu(px['a_z'])
    yc = head_rms(oc_f + oc_b, gn) * jax.nn.silu(pc['a_z']) if need_ctx else None
    return y, yc


def mlstm_mixer(px, pc, gate_b, gn, need_ctx):
    def prep(p, rotary):
        q = heads(p['b_q'], H_B) * (DH_B ** -0.5)
        k = heads(p['b_k'], H_B)
        v = heads(p['b_v'], H_B)
        if rotary:
            q, k = rope_2d(q), rope_2d(k)
        g = p['b_g'].reshape(*p['b_g'].shape[:-1], 4, H_B) + gate_b.astype(jnp.float32)
        return q, k, v, g

    def direction(q, k, v, g, d, state0, reverse):
        ig = g[..., d, :]
        lf = jax.nn.log_sigmoid(g[..., 2 + d, :])
        if reverse:
            q, k, v, ig, lf = (jnp.flip(a, axis=1) for a in (q, k, v, ig, lf))
        h, st = mlstm_chunked(q, k, v, ig, lf, state0)
        if reverse:
            h = jnp.flip(h, axis=1)
        return h, st

    B = px['b_q'].shape[0]
    state0 = (jnp.zeros((B, H_B, DH_B, DH_B), jnp.float32), jnp.zeros((B, H_B, DH_B), jnp.float32),
              jnp.zeros((B, H_B), jnp.float32))
    qc, kc, vc, gc = prep(pc, False)
    ql, kl, vl, gl = prep(px, True)
    hc_f, st_f = direction(qc, kc, vc, gc, 0, state0, False)
    hc_b, st_b = direction(qc, kc, vc, gc, 1, state0, True)
    hl_f, _ = direction(ql, kl, vl, gl, 0, st_f, False)
    hl_b, _ = direction(ql, kl, vl, gl, 1, st_b, True)
    y = jax.nn.sigmoid(px['b_o']) * head_rms(hl_f + hl_b, gn) * jax.nn.silu(px['b_z'])
    yc = (jax.nn.sigmoid(pc['b_o']) * head_rms(hc_f + hc_b, gn) * jax.nn.silu(pc['b_z'])) if need_ctx else None
    return y, yc


def na_mixer(px, pc, rpb, need_ctx):
    scale = DH_C ** -0.5
    q = heads(px['c_q'], H_C) * scale
    k = heads(px['c_k'], H_C)
    v = heads(px['c_v'], H_C)
    qc = heads(pc['c_q'], H_C) * scale
    kc = heads(pc['c_k'], H_C)
    vc = heads(pc['c_v'], H_C)
    B, T = q.shape[0], q.shape[1]
    rows = T // GRID_W
    win_r = min(WIN_ROWS, rows)
    qg = q.reshape(B, rows, GRID_W, H_C, DH_C)
    kg = k.reshape(B, rows, GRID_W, H_C, DH_C)
    vg = v.reshape(B, rows, GRID_W, H_C, DH_C)
    col = np.arange(GRID_W)
    cs = np.clip(col - WIN_COLS // 2, 0, GRID_W - WIN_COLS)
    band = (col[None, :] >= cs[:, None]) & (col[None, :] < cs[:, None] + WIN_COLS)
    dc_idx = np.clip(col[None, :] - col[:, None] + WIN_COLS - 1, 0, 2 * WIN_COLS - 2)
    rpb_c = rpb.astype(jnp.float32)[:, :, dc_idx]
    band_b = band[:, None, :]

    def row_fn(r):
        rs = jnp.clip(r - win_r // 2, 0, rows - win_r)
        q_r = lax.dynamic_index_in_dim(qg, r, axis=1, keepdims=False)
        k_b = lax.dynamic_slice_in_dim(kg, rs, win_r, axis=1)
        v_b = lax.dynamic_slice_in_dim(vg, rs, win_r, axis=1)
        dr = rs - r + jnp.arange(win_r) + WIN_ROWS - 1
        bias = jnp.transpose(jnp.take(rpb_c, dr, axis=1), (0, 2, 1, 3))
        s_w = jnp.einsum('bchd,bjkhd->bhcjk', q_r, k_b) + bias[None]
        s_w = jnp.where(band_b, s_w, -jnp.inf)
        s_c = jnp.einsum('bchd,bnhd->bhcn', q_r, kc)
        logits = jnp.concatenate([s_w.reshape(B, H_C, GRID_W, win_r * GRID_W), s_c], axis=-1).astype(jnp.float32)
        p = jax.nn.softmax(logits, axis=-1)
        p_w = p[..., :win_r * GRID_W].reshape(B, H_C, GRID_W, win_r, GRID_W)
        p_c = p[..., win_r * GRID_W:]
        return jnp.einsum('bhcjk,bjkhd->bchd', p_w, v_b) + jnp.einsum('bhcn,bnhd->bchd', p_c, vc)

    o = lax.map(row_fn, jnp.arange(rows))
    o = jnp.moveaxis(o, 0, 1).reshape(B, T, W_C)
    y = o * jax.nn.silu(px['c_z'])
    yc = None
    if need_ctx:
        s = jnp.einsum('bnhd,bmhd->bhnm', qc, kc).astype(jnp.float32)
        oc = jnp.einsum('bhnm,bmhd->bnhd', jax.nn.softmax(s, axis=-1), vc)
        yc = oc.reshape(B, oc.shape[1], W_C) * jax.nn.silu(pc['c_z'])
    return y, yc


def setup_inputs(seed: int = 0) -> dict:
    key = jax.random.key(seed)
    ks = jax.random.split(key, 16)
    nrm = jax.random.normal
    f32 = jnp.float32
    gate_b = jnp.concatenate([0.1 * nrm(ks[12], (DEPTH, 2, H_B), f32),
                              3.0 + 0.5 * nrm(ks[13], (DEPTH, 2, H_B), f32)], axis=1)
    return {
        'x': nrm(ks[0], (BATCH, SEQ, D_MODEL), f32),
        'c': nrm(ks[1], (BATCH, D_MODEL), f32),
        'ctx': nrm(ks[2], (BATCH, CTX_LEN, D_MODEL), f32),
        'c_ctx': nrm(ks[3], (D_MODEL,), f32),
        'w_mod': 0.5 * D_MODEL ** -0.5 * nrm(ks[4], (DEPTH, D_MODEL, 3 * D_MODEL), f32),
        'b_mod': 0.02 * nrm(ks[5], (DEPTH, 3 * D_MODEL), f32),
        'g_pre': 1.0 + 0.02 * nrm(ks[6], (DEPTH, D_MODEL), f32),
        'g_post': 1.0 + 0.02 * nrm(ks[7], (DEPTH, D_MODEL), f32),
        'w_in': D_MODEL ** -0.5 * nrm(ks[8], (DEPTH, D_MODEL, P_IN), f32),
        'w_out': W_MIX ** -0.5 * nrm(ks[9], (DEPTH, W_MIX, D_MODEL), f32),
        'hgrn_lb': 0.5 * nrm(ks[10], (DEPTH, 2, W_A), f32),
        'hgrn_gn': 1.0 + 0.02 * nrm(ks[11], (DEPTH, W_A), f32),
        'mlstm_gate_b': gate_b,
        'mlstm_gn': 1.0 + 0.02 * nrm(ks[14], (DEPTH, W_B), f32),
        'na_rpb': 0.02 * nrm(ks[15], (DEPTH, H_C, 2 * WIN_ROWS - 1, 2 * WIN_COLS - 1), f32),
    }


def reference(x, c, ctx, c_ctx, w_mod, b_mod, g_pre, g_post, w_in, w_out, hgrn_lb, hgrn_gn, mlstm_gate_b, mlstm_gn, na_rpb):
    lb_cum = jnp.cumsum(jax.nn.softmax(hgrn_lb.astype(jnp.float32), axis=0), axis=0)
    lb_all = lb_cum - lb_cum[0]
    for l in range(DEPTH):
        need_ctx = l < DEPTH - 1
        mod = jax.nn.silu(c.astype(jnp.float32)) @ w_mod[l] + b_mod[l]
        mod_c = jax.nn.silu(c_ctx.astype(jnp.float32)) @ w_mod[l] + b_mod[l]
        sh, sc, gt = jnp.split(mod, 3, axis=-1)
        shc, scc, gtc = jnp.split(mod_c, 3, axis=-1)
        hx = rms(x, g_pre[l]) * (1.0 + sc[:, None]) + sh[:, None]
        hc = rms(ctx, g_pre[l]) * (1.0 + scc) + shc
        px = split_proj(hx @ w_in[l])
        pc = split_proj(hc @ w_in[l])
        ya, yac = hgrn2_mixer(px, pc, lb_all[l], hgrn_gn[l], need_ctx)
        yb, ybc = mlstm_mixer(px, pc, mlstm_gate_b[l], mlstm_gn[l], need_ctx)
        yc, ycc = na_mixer(px, pc, na_rpb[l], need_ctx)
        ux = jnp.concatenate([ya, yb, yc], axis=-1) @ w_out[l]
        x = x + (gt[:, None] * rms(ux, g_post[l])).astype(x.dtype)
        if need_ctx:
            uc = jnp.concatenate([yac, ybc, ycc], axis=-1) @ w_out[l]
            ctx = ctx + (gtc * rms(uc, g_post[l])).astype(ctx.dtype)
    return x
```

```python
import os
import numpy as np
from contextlib import ExitStack
import concourse.bass as bass
import concourse.mybir as mybir
from concourse.bass_utils import run_bass_kernel_spmd

F32 = mybir.dt.float32
BF16 = mybir.dt.bfloat16
AF = mybir.ActivationFunctionType
ALU = mybir.AluOpType
AX = mybir.AxisListType

D = 1024
NT = 34
NCTX = 2
TOK = NT * 128
PIN = 4752
DEPTH = 2
EPS = 1e-6
NTB = 2944
NTBS = 4352
NTF = 528
NFM = 20
NEG = -30000.0
BWD_ORDER = [1, 0] + list(range(NT - 1, 1, -1))

SEM_CHUNK = 24000


class Buf:
    def __init__(self, t=None, name=""):
        self.t = t
        self.name = name
        self.w = None
        self.r = {}
        self.dsem = None

    def __getitem__(self, k):
        return self.t[k]


class DmaSem:
    def __init__(self, sem):
        self.sem = sem
        self.n = 0


class FW:
    def __init__(self, nc, es):
        self.nc = nc
        self.es = es
        self.E = {"pe": nc.tensor, "act": nc.scalar, "dve": nc.vector, "pool": nc.gpsimd, "sp": nc.sync}
        self.cnt = {e: 0 for e in self.E}
        self.sems = {e: [] for e in self.E}
        self.seen = {e: {} for e in self.E}
        self.dsems = []
        self.free_dsems = []
        self.bar_sem = es.enter_context(nc.semaphore("barsem"))
        self.bar_n = 0
        self.uid = 0

    def sb(self, es, name, shape, dt):
        self.uid += 1
        t = es.enter_context(self.nc.sbuf_tensor("%s_%d" % (name, self.uid), list(shape), dt))
        return Buf(t, name)

    def ps(self, es, name, shape, dt):
        self.uid += 1
        t = es.enter_context(self.nc.psum_tensor("%s_%d" % (name, self.uid), list(shape), dt))
        return Buf(t, name)

    def get_dsem(self):
        if self.free_dsems:
            return self.free_dsems.pop()
        s = self.es.enter_context(self.nc.semaphore("dsem%d" % len(self.dsems)))
        d = DmaSem(s)
        self.dsems.append(d)
        return d

    def release(self, bufs):
        for b in bufs:
            if b.dsem is not None:
                self.free_dsems.append(b.dsem)
                b.dsem = None

    def _esem(self, e, n):
        k = (n - 1) // SEM_CHUNK
        while len(self.sems[e]) <= k:
            self.sems[e].append(self.es.enter_context(self.nc.semaphore("es_%s_%d" % (e, len(self.sems[e])))))
        return self.sems[e][k], (n - 1) % SEM_CHUNK + 1

    def _wait_tok(self, e, tok):
        key, n = tok
        if key == e and e == "pe":
            return
        if self.seen[e].get(key, 0) >= n:
            return
        self.seen[e][key] = n
        eng = self.E[e]
        if isinstance(key, str):
            sem, val = self._esem(key, n)
            eng.wait_ge(sem, val)
        else:
            eng.wait_ge(key.sem, n)

    def _deps(self, e, reads, writes):
        for b in reads:
            if b.w is not None:
                self._wait_tok(e, b.w)
        for b in writes:
            if b.w is not None:
                self._wait_tok(e, b.w)
            for key, n in list(b.r.items()):
                self._wait_tok(e, (key, n))

    def _mark(self, tok, reads, writes):
        key, n = tok
        for b in writes:
            b.w = tok
            b.r = {}
        for b in reads:
            if b in writes:
                continue
            if b.r.get(key, 0) < n:
                b.r[key] = n

    def op(self, e, reads, writes, fn):
        self._deps(e, reads, writes)
        ins = fn(self.E[e])
        self.cnt[e] += 1
        n = self.cnt[e]
        sem, _ = self._esem(e, n)
        ins.then_inc(sem, 1)
        self._mark((e, n), reads, writes)
        return ins

    def dma(self, q, out_buf, in_buf, out_ap, in_ap, sem_buf=None, **kw):
        sb_ = sem_buf if sem_buf is not None else out_buf
        if sb_.dsem is None:
            sb_.dsem = self.get_dsem()
        ds = sb_.dsem
        self._deps(q, [in_buf], [out_buf])
        ins = self.E[q].dma_start(out=out_ap, in_=in_ap, **kw)
        ds.n += 16
        ins.then_inc(ds.sem, 16)
        self._mark((ds, ds.n), [in_buf], [out_buf])
        return ins

    def barrier(self):
        sp = self.E["sp"]
        for e in self.E:
            if e != "sp" and self.cnt[e] > 0:
                self._wait_tok("sp", (e, self.cnt[e]))
        for d in self.dsems:
            if d.n > 0:
                self._wait_tok("sp", (d, d.n))
        self.bar_n += 1
        sp.sem_inc(self.bar_sem, 1)
        for e in self.E:
            if e != "sp":
                self.E[e].wait_ge(self.bar_sem, self.bar_n)
        for e in self.E:
            for e2 in self.E:
                if e2 != e:
                    self.seen[e][e2] = self.cnt[e2]
            for d in self.dsems:
                self.seen[e][d] = d.n


class Ring:
    def __init__(self, bufs):
        self.bufs = bufs
        self.i = 0

    def next(self):
        b = self.bufs[self.i % len(self.bufs)]
        self.i += 1
        return b


def _consts():
    s = np.arange(128)[:, None]
    t = np.arange(128)[None, :]
    c = np.zeros((128, 6, 128), np.float32)
    c[:, 0] = (s == t)
    c[:, 1] = (s <= t)
    c[:, 2] = (s >= t)
    c[:, 3] = (s > t)
    c[:, 4] = (s < t)
    c[:, 5] = 1.0
    return c.reshape(128, 768)


def _rope_tables():
    tab = np.zeros((NT, 128, 4, 96), np.float32)
    qs = np.float32(96 ** -0.5)
    tab[:NCTX, :, 0, :] = qs
    tab[:NCTX, :, 2, :] = 1.0
    inv = (np.float32(10000.0) ** (-np.arange(0, 48, 2, dtype=np.float32) / np.float32(48))).astype(np.float32)
    T = np.arange(4096)
    pos = [(T // 64).astype(np.float32), (T % 64).astype(np.float32)]
    cos = np.zeros((4096, 2, 2, 24), np.float32)
    sin = np.zeros((4096, 2, 2, 24), np.float32)
    for p in range(2):
        ang = (pos[p][:, None] * inv[None, :]).astype(np.float32)
        cs, sn = np.cos(ang).astype(np.float32), np.sin(ang).astype(np.float32)
        cos[:, p, 0], cos[:, p, 1] = cs, cs
        sin[:, p, 0], sin[:, p, 1] = -sn, sn
    cos = cos.reshape(32, 128, 96)
    sin = sin.reshape(32, 128, 96)
    tab[NCTX:, :, 0] = cos * qs
    tab[NCTX:, :, 1] = sin * qs
    tab[NCTX:, :, 2] = cos
    tab[NCTX:, :, 3] = sin
    return tab


def _rpb_tables(na_rpb):
    L, H = na_rpb.shape[0], na_rpb.shape[1]
    s = np.arange(128)[:, None]
    t = np.arange(128)[None, :]
    kc, qc = s % 64, t % 64
    qr = t // 64
    cs = np.clip(qc - 8, 0, 48)
    band = (kc >= cs) & (kc < cs + 16)
    dc = np.clip(kc - qc + 15, 0, 30)
    out = np.full((L, H, 9, 128, 128), NEG, np.float32)
    for k in range(9):
        o = [-3, -2, -1, 0, 1, 2, 3, -2, 2][k]
        kr = 2 * o + s // 64
        diff = kr - qr
        ok = band & (diff >= -7) & (diff <= 7)
        if k >= 7:
            ok = ok & (diff >= -4) & (diff <= 3)
        dr = np.clip(diff + 7, 0, 14)
        for l in range(L):
            for h in range(H):
                g = na_rpb[l, h][dr, dc]
                out[l, h, k] = np.where(ok, g, np.float32(NEG))
    return out.reshape(L, H * 9, 128, 128)


def _key_tiles(lt):
    if lt <= 1:
        return [(kt, kt - lt + 3) for kt in range(0, 4)]
    if lt >= 30:
        return [(kt, kt - lt + 3) for kt in range(28, 32)]
    res = []
    for o in range(-2, 3):
        k = o + 3
        if o == -2:
            k = 7
        if o == 2:
            k = 8
        res.append((lt + o, k))
    return res


def build(debug=(), nlayers=DEPTH, phases="PABCO"):
    nc = bass.Bass("TRN2", target_bir_lowering=False)

    def din(name, shape, dt=F32):
        return nc.dram_tensor(name, list(shape), dt, kind="ExternalInput").ap()

    def dscr(name, shape, dt=F32):
        kind = "ExternalOutput" if name in debug else "Internal"
        return nc.dram_tensor(name, list(shape), dt, kind=kind).ap()

    x_in = din("x", [4096, D])
    ctx_in = din("ctx", [256, D])
    cT_in = din("cT", [128, 16])
    w_mod = din("w_mod", [DEPTH, D, 3 * D])
    b_mod = din("b_mod", [DEPTH, 3 * D])
    g_pre = din("g_pre", [DEPTH, D])
    g_post = din("g_post", [DEPTH, D])
    w_in = din("w_in", [DEPTH, D, PIN])
    w_out = din("w_out", [DEPTH, D, D])
    hgrn_lb = din("hgrn_lb", [DEPTH, 512])
    hgrn_gn = din("hgrn_gn", [DEPTH, 256])
    gate_b = din("gate_b", [DEPTH, 16])
    mlstm_gn = din("mlstm_gn", [DEPTH, 384])
    rpbt = din("rpbt", [DEPTH, 54, 128, 128])
    rope = din("rope", [NT, 128, 4 * 96])
    consts = din("consts", [128, 768])
    y_out = nc.dram_tensor("y", [4096, D], F32, kind="ExternalOutput").ap()

    XC = dscr("XC", [TOK, D])
    TB = dscr("TB", [TOK, NTB], BF16)
    TF = dscr("TF", [TOK, NTF])
    FM = dscr("FM", [NT, 128, NFM, 128], BF16)
    OA = [dscr("OA%d" % d, [TOK, 256]) for d in range(2)]
    OB = [dscr("OB%d" % d, [TOK, 384]) for d in range(2)]
    OC = dscr("OC", [TOK, 384])
    dr = Buf(None, "dram")

    with ExitStack() as es:
        fw = FW(nc, es)
        op, dma = fw.op, fw.dma

        CST = fw.sb(es, "cst", [128, 768], F32)
        dma("sp", CST, dr, CST[:], consts[:, :])
        IDENTF = CST.t[:, 0:128]
        TRI = [CST.t[:, 128:256], CST.t[:, 256:384]]
        TRIS = [CST.t[:, 384:512], CST.t[:, 512:640]]
        ONESF = CST.t[:, 640:768]
        CB = fw.sb(es, "cstb", [128, 384], BF16)
        op("dve", [CST], [CB], lambda e: e.tensor_copy(out=CB[:], in_=CST[:, 0:384]))
        IDENT = CB.t[:, 0:128]
        MASK = [CB.t[:, 128:256], CB.t[:, 256:384]]
        CT = fw.sb(es, "cT", [128, 16], F32)
        dma("sp", CT, dr, CT[:], cT_in[:, :])
        SC = fw.sb(es, "sc", [128, 16], F32)
        op("act", [CT], [SC], lambda e: e.activation(out=SC[:], in_=CT[:], func=AF.Silu))
        fw.barrier()

        def x_src(l, ti):
            if l == 0:
                if ti < NCTX:
                    return ctx_in[ti * 128:(ti + 1) * 128, :]
                return x_in[(ti - NCTX) * 128:(ti - NCTX + 1) * 128, :]
            return XC[ti * 128:(ti + 1) * 128, :]

        for l in range(nlayers):
            last = (l == DEPTH - 1)
            with ExitStack() as les:
                MODS = [fw.sb(les, "mod%d" % i, [128, 3 * D], F32) for i in range(2)]
                BM = fw.sb(les, "bm", [128, 3 * D], F32)
                GPRE = fw.sb(les, "gpre", [128, D], F32)
                GPOST = fw.sb(les, "gpost", [128, D], F32)
                dma("sp", BM, dr, BM[:], b_mod[l:l + 1, :].partition_broadcast(128))
                dma("sp", GPRE, dr, GPRE[:], g_pre[l:l + 1, :].partition_broadcast(128))
                dma("sp", GPOST, dr, GPOST[:], g_post[l:l + 1, :].partition_broadcast(128))
                LB = fw.sb(les, "lb", [128, 512], F32)
                OML = fw.sb(les, "oml", [128, 512], F32)
                GNA = fw.sb(les, "gna", [128, 256], F32)
                GNB = fw.sb(les, "gnb", [128, 384], F32)
                GB = fw.sb(les, "gb", [128, 16], F32)
                dma("sp", GNA, dr, GNA[:], hgrn_gn[l:l + 1, :].partition_broadcast(128))
                dma("sp", GNB, dr, GNB[:], mlstm_gn[l:l + 1, :].partition_broadcast(128))
                dma("sp", GB, dr, GB[:], gate_b[l:l + 1, :].partition_broadcast(128))
                if l == 0:
                    op("dve", [], [LB], lambda e: e.memset(LB[:], 0.0))
                else:
                    dma("sp", LB, dr, LB[:], hgrn_lb[1:2, :].partition_broadcast(128))
                    dma("sp", OML, dr, OML[:], hgrn_lb[0:1, :].partition_broadcast(128))
                    op("dve", [LB, OML], [LB], lambda e: e.tensor_sub(out=LB[:], in0=LB[:], in1=OML[:]))
                    op("act", [LB], [LB], lambda e: e.activation(out=LB[:], in_=LB[:], func=AF.Sigmoid))
                op("dve", [LB], [OML], lambda e: e.tensor_scalar(out=OML[:], in0=LB[:], scalar1=-1.0, scalar2=1.0,
                                                                  op0=ALU.mult, op1=ALU.add))
                with ExitStack() as ms:
                    wm = Ring([fw.sb(ms, "wm%d" % i, [128, 8, 512], F32) for i in range(2)])
                    pmod = Ring([fw.ps(ms, "pmod%d" % i, [128, 512], F32) for i in range(4)])
                    for nb in range(6):
                        w = wm.next()
                        dma("sp", w, dr, w[:], w_mod[l, :, nb * 512:(nb + 1) * 512].rearrange("(kc p) n -> p kc n", p=128))
                        for st in range(2):
                            pm = pmod.next()
                            for kc in range(8):
                                op("pe", [SC, w], [pm], lambda e: e.matmul(
                                    pm[:], lhsT=SC[:, st * 8 + kc:st * 8 + kc + 1].to_broadcast([128, 128]),
                                    rhs=w[:, kc, :], start=(kc == 0), stop=(kc == 7)))
                            m = MODS[st]
                            op("dve", [pm, BM], [m], lambda e: e.tensor_tensor(
                                out=m[:, nb * 512:(nb + 1) * 512], in0=pm[:], in1=BM[:, nb * 512:(nb + 1) * 512], op=ALU.add))
                    fw.barrier()
                    fw.release(wm.bufs)
                for st in range(2):
                    m = MODS[st]
                    op("dve", [m, GPRE], [m], lambda e: e.scalar_tensor_tensor(
                        out=m[:, D:2 * D], in0=m[:, D:2 * D], scalar=1.0, in1=GPRE[:], op0=ALU.add, op1=ALU.mult))
                    op("dve", [m, GPOST], [m], lambda e: e.tensor_tensor(
                        out=m[:, 2 * D:3 * D], in0=m[:, 2 * D:3 * D], in1=GPOST[:], op=ALU.mult))
                fw.barrier()

                if "P" in phases:
                    with ExitStack() as ps_:
                        WIN = fw.sb(ps_, "win", [128, 8, PIN], BF16)
                        wst = Ring([fw.sb(ps_, "wst%d" % i, [128, 1188], F32) for i in range(2)])
                        k = 0
                        for kc in range(8):
                            for q4 in range(4):
                                w = wst.next()
                                dma("sp", w, dr, w[:], w_in[l, kc * 128:(kc + 1) * 128, q4 * 1188:(q4 + 1) * 1188])
                                eng = "dve" if k % 2 == 0 else "act"
                                if eng == "dve":
                                    op("dve", [w], [WIN], lambda e: e.tensor_copy(out=WIN[:, kc, q4 * 1188:(q4 + 1) * 1188], in_=w[:]))
                                else:
                                    op("act", [w], [WIN], lambda e: e.copy(out=WIN[:, kc, q4 * 1188:(q4 + 1) * 1188], in_=w[:]))
                                k += 1
                        xr = Ring([fw.sb(ps_, "x%d" % i, [128, D], F32) for i in range(2)])
                        rp = Ring([fw.sb(ps_, "rope%d" % i, [128, 4, 96], F32) for i in range(2)])
                        sq = fw.sb(ps_, "sq", [128, D], F32)
                        ss = fw.sb(ps_, "ss", [128, 2], F32)
                        hb = fw.sb(ps_, "hb", [128, D], BF16)
                        hT = Ring([fw.sb(ps_, "hT%d" % i, [128, 8, 128], BF16) for i in range(2)])
                        phT = fw.ps(ps_, "phT", [128, 8, 128], BF16)
                        pj = Ring([fw.ps(ps_, "pj%d" % i, [128, 512], F32) for i in range(3)])
                        ptt = Ring([fw.ps(ps_, "ptt%d" % i, [128, 8, 128], BF16) for i in range(2)])
                        tbs = Ring([fw.sb(ps_, "tbs%d" % i, [128, NTBS], BF16) for i in range(2)])
                        tfs = Ring([fw.sb(ps_, "tfs%d" % i, [128, NTF], F32) for i in range(2)])
                        fms = Ring([fw.sb(ps_, "fms%d" % i, [128, NFM, 128], BF16) for i in range(2)])
                        t1 = fw.sb(ps_, "t1", [128, 512], F32)
                        t2 = fw.sb(ps_, "t2", [128, 512], F32)
                        t3 = fw.sb(ps_, "t3", [128, 384], F32)

                        def proj(ps, c0, wd, hTb):
                            for kc in range(8):
                                op("pe", [hTb, WIN], [ps], lambda e: e.matmul(
                                    ps[:, 0:wd], lhsT=hTb[:, kc, :], rhs=WIN[:, kc, c0:c0 + wd], start=(kc == 0), stop=(kc == 7)))

                        for ti in range(NT):
                            st = 1 if ti < NCTX else 0
                            m = MODS[st]
                            xt = xr.next()
                            dma("sp", xt, dr, xt[:], x_src(l, ti))
                            rt = rp.next()
                            dma("sp", rt, dr, rt[:], rope[ti].rearrange("p (a b) -> p a b", a=4))
                            op("act", [xt], [sq, ss], lambda e: e.activation(out=sq[:], in_=xt[:], func=AF.Square, accum_out=ss[:, 0:1]))
                            op("dve", [ss], [ss], lambda e: e.tensor_scalar(out=ss[:, 1:2], in0=ss[:, 0:1], scalar1=1.0 / D, scalar2=EPS,
                                                                            op0=ALU.mult, op1=ALU.add))
                            op("act", [ss], [ss], lambda e: e.activation(out=ss[:, 1:2], in_=ss[:, 1:2], func=AF.Sqrt))
                            op("dve", [ss], [ss], lambda e: e.reciprocal(out=ss[:, 1:2], in_=ss[:, 1:2]))
                            op("dve", [xt, ss, m], [sq], lambda e: e.scalar_tensor_tensor(
                                out=sq[:], in0=xt[:], scalar=ss[:, 1:2], in1=m[:, D:2 * D], op0=ALU.mult, op1=ALU.mult))
                            op("dve", [sq, m], [hb], lambda e: e.tensor_tensor(out=hb[:], in0=sq[:], in1=m[:, 0:D], op=ALU.add))
                            for kc in range(8):
                                op("pe", [hb, CB], [phT], lambda e: e.transpose(out=phT[:, kc, :], in_=hb[:, kc * 128:(kc + 1) * 128], identity=IDENT))
                            hTb = hT.next()
                            op("act", [phT], [hTb], lambda e: e.copy(out=hTb[:], in_=phT[:]))
                            tb = tbs.next()
                            tf = tfs.next()
                            fm = fms.next()
                            ps = pj.next(); proj(ps, 256, 512, hTb)
                            op("act", [ps], [t1], lambda e: e.activation(out=t1[:], in_=ps[:], func=AF.Sigmoid))
                            op("dve", [t1, OML], [t1], lambda e: e.tensor_tensor(out=t1[:], in0=t1[:], in1=OML[:], op=ALU.mult))
                            op("dve", [t1, LB], [t1], lambda e: e.tensor_tensor(out=t1[:], in0=t1[:], in1=LB[:], op=ALU.add))
                            op("act", [t1], [tf], lambda e: e.activation(out=tf[:, 0:512], in_=t1[:], func=AF.Ln))
                            op("dve", [t1], [tb], lambda e: e.tensor_scalar(out=tb[:, 0:512], in0=t1[:], scalar1=-1.0, scalar2=1.0,
                                                                           op0=ALU.mult, op1=ALU.add))
                            ps = pj.next(); proj(ps, 0, 256, hTb)
                            op("act", [ps], [tb], lambda e: e.activation(out=tb[:, 2944:3200], in_=ps[:, 0:256], func=AF.Silu))
                            ps = pj.next(); proj(ps, 768, 512, hTb)
                            op("dve", [ps], [tb], lambda e: e.tensor_copy(out=tb[:, 512:768], in_=ps[:, 0:256]))
                            op("act", [ps], [tb], lambda e: e.activation(out=tb[:, 768:1024], in_=ps[:, 256:512], func=AF.Silu))
                            for (c0, dst, ci) in ((1280, 3200, 0), (1664, 1024, 2)):
                                ps = pj.next(); proj(ps, c0, 384, hTb)
                                pv = ps.t[:, 0:384].rearrange("p (h a b j) -> p h a b j", h=4, a=2, b=2)
                                t2v = t2.t[:, 0:384].rearrange("p (h a b j) -> p h a b j", h=4, a=2, b=2)
                                cosb = rt.t[:, ci, :].unsqueeze(1).to_broadcast([128, 4, 96])
                                sinv = rt.t[:, ci + 1, :].rearrange("p (a b j) -> p a b j", a=2, b=2)
                                op("dve", [ps, rt], [t3], lambda e: e.tensor_tensor(
                                    out=t3[:].rearrange("p (h d) -> p h d", h=4), in0=ps[:, 0:384].rearrange("p (h d) -> p h d", h=4),
                                    in1=cosb, op=ALU.mult))
                                for b_ in range(2):
                                    op("dve", [ps, rt], [t2], lambda e: e.tensor_tensor(
                                        out=t2v[:, :, :, b_, :], in0=pv[:, :, :, 1 - b_, :],
                                        in1=sinv[:, :, b_, :].unsqueeze(1).to_broadcast([128, 4, 2, 24]), op=ALU.mult))
                                op("dve", [t3, t2], [tb], lambda e: e.tensor_tensor(out=tb[:, dst:dst + 384], in0=t3[:], in1=t2[:, 0:384], op=ALU.add))
                            ps = pj.next(); proj(ps, 2048, 384, hTb)
                            op("act", [ps], [tb], lambda e: e.copy(out=tb[:, 1408:1792], in_=ps[:, 0:384]))
                            ps = pj.next(); proj(ps, 2432, 384, hTb)
                            op("act", [ps], [t3], lambda e: e.activation(out=t3[:], in_=ps[:, 0:384], func=AF.Sigmoid))
                            ps = pj.next(); proj(ps, 2816, 384, hTb)
                            op("act", [ps], [t2], lambda e: e.activation(out=t2[:, 0:384], in_=ps[:, 0:384], func=AF.Silu))
                            op("dve", [t3, t2], [tb], lambda e: e.tensor_tensor(out=tb[:, 1792:2176], in0=t3[:], in1=t2[:, 0:384], op=ALU.mult))
                            ps = pj.next(); proj(ps, 3200, 16, hTb)
                            op("dve", [ps, GB], [tf], lambda e: e.tensor_tensor(out=tf[:, 512:528], in0=ps[:, 0:16], in1=GB[:], op=ALU.add))
                            op("act", [tf], [tf], lambda e: e.activation(out=tf[:, 520:528], in_=tf[:, 520:528], func=AF.Sigmoid))
                            op("act", [tf], [tf], lambda e: e.activation(out=tf[:, 520:528], in_=tf[:, 520:528], func=AF.Ln))
                            ps = pj.next(); proj(ps, 3216, 384, hTb)
                            op("act", [ps], [tb], lambda e: e.activation(out=tb[:, 3584:3968], in_=ps[:, 0:384], func=AF.Copy, scale=0.125))
                            ps = pj.next(); proj(ps, 3600, 384, hTb)
                            op("dve", [ps], [tb], lambda e: e.tensor_copy(out=tb[:, 3968:4352], in_=ps[:, 0:384]))
                            ps = pj.next(); proj(ps, 3984, 384, hTb)
                            op("act", [ps], [tb], lambda e: e.copy(out=tb[:, 2176:2560], in_=ps[:, 0:384]))
                            ps = pj.next(); proj(ps, 4368, 384, hTb)
                            op("act", [ps], [tb], lambda e: e.activation(out=tb[:, 2560:2944], in_=ps[:, 0:384], func=AF.Silu))
                            srcA = [2944, 3072, 0, 128, 256, 384]
                            srcC = [3584, 3712, 3840, 3968, 4096, 4224]
                            pt_ = ptt.next()
                            for j, c0 in enumerate(srcA):
                                op("pe", [tb, CB], [pt_], lambda e: e.transpose(out=pt_[:, j, :], in_=tb[:, c0:c0 + 128], identity=IDENT))
                            op("dve", [pt_], [fm], lambda e: e.tensor_copy(out=fm[:, 0:6, :], in_=pt_[:, 0:6, :]))
                            pt_ = ptt.next()
                            for j, c0 in enumerate(srcC):
                                op("pe", [tb, CB], [pt_], lambda e: e.transpose(out=pt_[:, j, :], in_=tb[:, c0:c0 + 128], identity=IDENT))
                            op("act", [pt_], [fm], lambda e: e.copy(out=fm[:, 6:12, :], in_=pt_[:, 0:6, :]))
                            pt_ = ptt.next()
                            for j in range(8):
                                c0 = (3200 if j < 4 else 1024) + (j % 4) * 96
                                op("pe", [tb, CB], [pt_], lambda e: e.transpose(out=pt_[0:96, j, :], in_=tb[:, c0:c0 + 96], identity=IDENT))
                            op("dve", [pt_], [fm], lambda e: e.tensor_copy(out=fm[0:96, 12:20, :], in_=pt_[0:96, 0:8, :]))
                            dma("pool", dr, tb, TB[ti * 128:(ti + 1) * 128, :], tb[:, 0:NTB], sem_buf=tb)
                            dma("pool", dr, tf, TF[ti * 128:(ti + 1) * 128, :], tf[:], sem_buf=tf)
                            dma("pool", dr, fm, FM[ti], fm[:], sem_buf=fm)
                        fw.barrier()
                        fw.release(wst.bufs + xr.bufs + rp.bufs + tbs.bufs + tfs.bufs + fms.bufs)

                if "A" in phases:
                    with ExitStack() as as_:
                        def ring(name, shape, dt, n=2, ps=False):
                            return Ring([(fw.ps if ps else fw.sb)(as_, "%s%d" % (name, i), shape, dt) for i in range(n)])
                        St = []
                        for d in range(2):
                            S = fw.sb(as_, "S%d" % d, [128, 2, 64], F32)
                            Sb = fw.sb(as_, "Sb%d" % d, [128, 2, 128], BF16)
                            op("dve", [], [S], lambda e: e.memset(S[:], 0.0))
                            op("dve", [], [Sb], lambda e: e.memset(Sb[:], 0.0))
                            R_ = dict(S=S, Sb=Sb,
                                      lf=ring("alf%d" % d, [128, 256], F32), qT=ring("aqT%d" % d, [128, 2, 128], BF16),
                                      kT=ring("akT%d" % d, [128, 2, 128], BF16), kt=ring("akt%d" % d, [128, 256], BF16),
                                      vt=ring("avt%d" % d, [128, 256], BF16), ost=ring("aos%d" % d, [128, 256], F32),
                                      bT=ring("abT%d" % d, [128, 2, 128], F32), E1=ring("aE1%d" % d, [128, 2, 128], F32),
                                      Es=ring("aEs%d" % d, [128, 2, 128], F32), tmp=ring("atm%d" % d, [128, 2, 128], F32),
                                      rr=ring("arr%d" % d, [128, 2, 8], F32), edk=ring("aed%d" % d, [128, 256], F32),
                                      kd=ring("akd%d" % d, [128, 256], BF16), Qf=ring("aQf%d" % d, [128, 2, 128], BF16),
                                      Qs=ring("aQs%d" % d, [128, 2, 2, 128], BF16), ATm=ring("aAT%d" % d, [128, 4, 128], BF16),
                                      Ek=ring("aEk%d" % d, [128, 2, 128], F32),
                                      Kv=[ring("aKv%d_%d" % (d, i), [128, 2, 128], BF16) for i in range(4)])
                            for i in range(4):
                                for b in R_["Kv"][i].bufs:
                                    op("dve", [], [b], lambda e: e.memset(b[:], 0.0))
                            for b in R_["rr"].bufs + R_["Qs"].bufs:
                                op("dve", [], [b], lambda e: e.memset(b[:], 0.0))
                            St.append(R_)
                        pbT = ring("apbT", [128, 2, 128], F32, 1, ps=True)
                        prs = ring("aprs", [128, 256], F32, 1, ps=True)
                        pAT = ring("apAT", [128, 4, 128], F32, 2, ps=True)
                        pO = ring("apO", [128, 256], F32, 2, ps=True)
                        pU = ring("apU", [128, 2, 128], F32, 1, ps=True)

                        def a_step(ti, d):
                            R_ = St[d]
                            S, Sb = R_["S"], R_["Sb"]
                            lf = R_["lf"].next(); qT = R_["qT"].next(); kT = R_["kT"].next(); kt = R_["kt"].next(); vt = R_["vt"].next()
                            r0 = ti * 128
                            dma("sp", lf, dr, lf[:], TF[r0:r0 + 128, d * 256:(d + 1) * 256])
                            dma("sp", qT, dr, qT[:], FM[ti, :, 0:2, :])
                            dma("sp", kT, dr, kT[:], FM[ti, :, 2 + 2 * d:4 + 2 * d, :])
                            dma("sp", kt, dr, kt[:], TB[r0:r0 + 128, d * 256:(d + 1) * 256])
                            dma("sp", vt, dr, vt[:], TB[r0:r0 + 128, 512:768])
                            pb = pbT.next()
                            for pt in range(2):
                                op("pe", [lf, CST], [pb], lambda e: e.matmul(pb[:, pt, :], lhsT=lf[:, pt * 128:(pt + 1) * 128], rhs=TRI[d], start=True, stop=True))
                            bT = R_["bT"].next()
                            op("act", [pb], [bT], lambda e: e.copy(out=bT[:], in_=pb[:]))
                            CUT = int(os.environ.get("A_CUT", 99))
                            if CUT < 1:
                                return
                            pr = prs.next()
                            op("pe", [lf, CST], [pr], lambda e: e.matmul(pr[:], lhsT=TRIS[d], rhs=lf[:], start=True, stop=True))
                            edk = R_["edk"].next()
                            op("act", [pr], [edk], lambda e: e.activation(out=edk[:], in_=pr[:], func=AF.Exp))
                            kd = R_["kd"].next()
                            op("dve", [edk, kt], [kd], lambda e: e.tensor_tensor(out=kd[:], in0=edk[:], in1=kt[:], op=ALU.mult))
                            E1 = R_["E1"].next()
                            op("act", [bT], [E1], lambda e: e.activation(out=E1[:], in_=bT[:], func=AF.Exp))
                            Qf = R_["Qf"].next()
                            op("dve", [E1, qT], [Qf], lambda e: e.tensor_tensor(out=Qf[:], in0=E1[:], in1=qT[:], op=ALU.mult))
                            if CUT < 2:
                                return
                            rr = R_["rr"].next()
                            if d == 0:
                                src = bT.t[:, :, 31:127:32]; dn = rr.t[:, :, 1:4]; dp = rr.t[:, :, 5:8]
                            else:
                                src = bT.t[:, :, 32:128:32]; dn = rr.t[:, :, 0:3]; dp = rr.t[:, :, 4:7]
                            op("dve", [bT], [rr], lambda e: e.tensor_scalar_mul(out=dn, in0=src, scalar1=-1.0))
                            op("dve", [bT], [rr], lambda e: e.tensor_copy(out=dp, in_=src))
                            tmp = R_["tmp"].next()
                            op("dve", [bT, rr], [tmp], lambda e: e.tensor_tensor(
                                out=tmp[:].rearrange("p a (i j) -> p a i j", i=4), in0=bT[:].rearrange("p a (i j) -> p a i j", i=4),
                                in1=rr.t[:, :, 0:4].unsqueeze(3).to_broadcast([128, 2, 4, 32]), op=ALU.add))
                            Es = R_["Es"].next()
                            op("act", [tmp], [Es], lambda e: e.activation(out=Es[:], in_=tmp[:], func=AF.Exp))
                            Qs = R_["Qs"].next()
                            for hl in range(2):
                                op("dve", [Es, qT], [Qs], lambda e: e.tensor_tensor(
                                    out=Qs[64 * hl:64 * hl + 64, hl, :, :], in0=Es[64 * hl:64 * hl + 64, :, :], in1=qT[64 * hl:64 * hl + 64, :, :], op=ALU.mult))
                            if CUT < 3:
                                return
                            Kvs = []
                            for i in range(4):
                                lo, hi = (0, 32 * (i + 1)) if d == 0 else (32 * i, 128)
                                Ek = R_["Ek"].next()
                                for pt in range(2):
                                    op("act", [bT, rr], [Ek], lambda e: e.activation(
                                        out=Ek[:, pt, lo:hi], in_=bT[:, pt, lo:hi], func=AF.Exp, bias=rr[:, pt, 4 + i:5 + i], scale=-1.0))
                                Kv = R_["Kv"][i].next()
                                op("dve", [Ek, kT], [Kv], lambda e: e.tensor_tensor(out=Kv[:, :, lo:hi], in0=Ek[:, :, lo:hi], in1=kT[:, :, lo:hi], op=ALU.mult))
                                Kvs.append(Kv)
                            if CUT < 4:
                                return
                            pa = pAT.next()
                            for h in range(4):
                                pt, bs = h // 2, 64 * (h % 2)
                                for i in range(4):
                                    Kv = Kvs[i]
                                    op("pe", [Kv, Qs], [pa], lambda e: e.matmul(
                                        pa[:, h, 32 * i:32 * i + 32], lhsT=Kv[:, pt, :], rhs=Qs[:, h % 2, pt, 32 * i:32 * i + 32],
                                        start=True, stop=True))
                            ATm = R_["ATm"].next()
                            op("dve", [pa, CB], [ATm], lambda e: e.tensor_tensor(
                                out=ATm[:], in0=pa[:], in1=MASK[d].unsqueeze(1).to_broadcast([128, 4, 128]), op=ALU.mult))
                            if CUT < 5:
                                return
                            po = pO.next()
                            for pt in range(2):
                                op("pe", [Qf, Sb], [po], lambda e: e.matmul(po[:, 128 * pt:128 * pt + 128], lhsT=Qf[:, pt, :], rhs=Sb[:, pt, :],
                                                                            start=True, stop=False, skip_group_check=True))
                                for hl in range(2):
                                    h = 2 * pt + hl
                                    op("pe", [ATm, vt], [po], lambda e: e.matmul(po[:, 64 * h:64 * h + 64], lhsT=ATm[:, h, :], rhs=vt[:, 64 * h:64 * h + 64],
                                                                                start=False, stop=(hl == 1), skip_group_check=True))
                            ost = R_["ost"].next()
                            op("act", [po], [ost], lambda e: e.copy(out=ost[:], in_=po[:]))
                            dma("pool", dr, ost, OA[d][r0:r0 + 128, :], ost[:], sem_buf=ost)
                            if CUT < 6:
                                return
                            pu = pU.next()
                            for pt in range(2):
                                op("pe", [kd, vt], [pu], lambda e: e.matmul(pu[:, pt, :], lhsT=kd[:, pt * 128:(pt + 1) * 128], rhs=vt[:, pt * 128:(pt + 1) * 128],
                                                                           start=True, stop=True))
                            col = 127 if d == 0 else 0
                            for pt in range(2):
                                for hl in range(2):
                                    bs = 64 * hl
                                    op("dve", [S, E1, pu], [S], lambda e: e.scalar_tensor_tensor(
                                        out=S[bs:bs + 64, pt, :], in0=S[bs:bs + 64, pt, :], scalar=E1[bs:bs + 64, pt, col:col + 1],
                                        in1=pu[bs:bs + 64, pt, bs:bs + 64], op0=ALU.mult, op1=ALU.add))
                            for hl in range(2):
                                bs = 64 * hl
                                op("act", [S], [Sb], lambda e: e.copy(out=Sb[bs:bs + 64, :, bs:bs + 64], in_=S[bs:bs + 64, :, :]))

                        for j in range(int(os.environ.get("A_STEPS", NT))):
                            a_step(j, 0)
                            a_step(BWD_ORDER[j], 1)
                        fw.barrier()
                        for R_ in St:
                            for k_ in ("lf", "qT", "kT", "kt", "vt", "ost"):
                                fw.release(R_[k_].bufs)

                if "B" in phases:
                    with ExitStack() as bs_:
                        def ring(name, shape, dt, n=2, ps=False):
                            return Ring([(fw.ps if ps else fw.sb)(bs_, "%s%d" % (name, i), shape, dt) for i in range(n)])
                        St = []
                        for d in range(2):
                            C = fw.sb(bs_, "C%d" % d, [128, 4, 97], F32)
                            Cb = fw.sb(bs_, "Cb%d" % d, [128, 4, 97], BF16)
                            op("dve", [], [C], lambda e: e.memset(C[:], 0.0))
                            op("dve", [], [Cb], lambda e: e.memset(Cb[:], 0.0))
                            R_ = dict(C=C, Cb=Cb, g=ring("bg%d" % d, [128, 16], F32), qT=ring("bqT%d" % d, [128, 4, 128], BF16),
                                      kT=ring("bkT%d" % d, [128, 4, 128], BF16), kt=ring("bkt%d" % d, [128, 384], BF16),
                                      vx=ring("bvx%d" % d, [128, 4, 97], BF16), X=ring("bX%d" % d, [128, 16], F32),
                                      E=ring("bE%d" % d, [128, 16], F32), kh=ring("bkh%d" % d, [128, 384], BF16),
                                      ST=ring("bST%d" % d, [128, 4, 128], BF16), u=ring("bu%d" % d, [128, 4, 97], F32),
                                      dd=ring("bdd%d" % d, [128, 8], F32), ost=ring("bos%d" % d, [128, 384], F32))
                            for b in R_["vx"].bufs:
                                op("dve", [], [b], lambda e: e.memset(b[:], 1.0))
                            St.append(R_)
                        pg = ring("bpg", [128, 16], F32, 2, ps=True)
                        psc = ring("bpsc", [128, 4, 128], F32, 2, ps=True)
                        pout = ring("bpout", [128, 4, 128], F32, 2, ps=True)
                        pdu = ring("bpdu", [128, 4, 128], F32, 1, ps=True)

                        def b_step(ti, d):
                            R_ = St[d]
                            C, Cb = R_["C"], R_["Cb"]
                            g = R_["g"].next(); qT = R_["qT"].next(); kT = R_["kT"].next(); kt = R_["kt"].next(); vx = R_["vx"].next()
                            r0 = ti * 128
                            dma("sp", g, dr, g[:], TF[r0:r0 + 128, 512:528])
                            dma("sp", qT, dr, qT[:], FM[ti, :, 12:16, :])
                            dma("sp", kT, dr, kT[:], FM[ti, :, 16:20, :])
                            dma("sp", kt, dr, kt[:], TB[r0:r0 + 128, 1024:1408])
                            dma("sp", vx, dr, vx[:, :, 0:96], TB[r0:r0 + 128, 1408:1792].rearrange("p (h d) -> p h d", h=4))
                            ig = g.t[:, 4 * d:4 * d + 4]
                            lfd = g.t[:, 8 + 4 * d:12 + 4 * d]
                            p_ = pg.next()
                            op("pe", [g, CST], [p_], lambda e: e.matmul(p_[:, 0:4], lhsT=TRI[d], rhs=lfd, start=True, stop=True))
                            op("pe", [g, CST], [p_], lambda e: e.matmul(p_[:, 4:8], lhsT=TRIS[d], rhs=lfd, start=True, stop=True))
                            op("pe", [g, CST], [p_], lambda e: e.matmul(p_[:, 8:12], lhsT=ONESF, rhs=lfd, start=True, stop=True))
                            X = R_["X"].next()
                            op("dve", [g, p_], [X], lambda e: e.tensor_tensor(out=X[:, 0:4], in0=ig, in1=p_[:, 0:4], op=ALU.subtract))
                            op("dve", [g, p_], [X], lambda e: e.tensor_tensor(out=X[:, 4:8], in0=ig, in1=p_[:, 4:8], op=ALU.add))
                            op("dve", [p_], [X], lambda e: e.tensor_copy(out=X[:, 8:12], in_=p_[:, 0:4]))
                            op("dve", [p_], [X], lambda e: e.tensor_copy(out=X[:, 12:16], in_=p_[:, 8:12]))
                            E = R_["E"].next()
                            op("act", [X], [E], lambda e: e.activation(out=E[:], in_=X[:], func=AF.Exp))
                            kh = R_["kh"].next()
                            op("dve", [kt, E], [kh], lambda e: e.tensor_tensor(
                                out=kh[:].rearrange("p (h d) -> p h d", h=4), in0=kt[:].rearrange("p (h d) -> p h d", h=4),
                                in1=E.t[:, 4:8].unsqueeze(2).to_broadcast([128, 4, 96]), op=ALU.mult))
                            sc = psc.next()
                            for h in range(4):
                                op("pe", [kT, qT], [sc], lambda e: e.matmul(sc[:, h, :], lhsT=kT[0:96, h, :], rhs=qT[0:96, h, :], start=True, stop=True))
                            ST = R_["ST"].next()
                            for h in range(4):
                                op("dve", [sc, E, CB], [ST], lambda e: e.scalar_tensor_tensor(
                                    out=ST[:, h, :], in0=sc[:, h, :], scalar=E[:, h:h + 1], in1=MASK[d], op0=ALU.mult, op1=ALU.mult))
                            po = pout.next()
                            for h in range(4):
                                op("pe", [ST, vx], [po], lambda e: e.matmul(po[:, h, 0:97], lhsT=ST[:, h, :], rhs=vx[:, h, :], start=True, stop=False))
                                op("pe", [qT, Cb], [po], lambda e: e.matmul(po[:, h, 0:97], lhsT=qT[0:96, h, :], rhs=Cb[0:96, h, :], start=False, stop=True))
                            u = R_["u"].next()
                            op("dve", [po, E], [u], lambda e: e.tensor_tensor(
                                out=u[:], in0=po[:, :, 0:97], in1=E.t[:, 8:12].unsqueeze(2).to_broadcast([128, 4, 97]), op=ALU.mult))
                            dd = R_["dd"].next()
                            op("dve", [u], [dd], lambda e: e.tensor_scalar_max(out=dd[:, 0:4], in0=u[:, :, 96], scalar1=1.0))
                            op("dve", [u, dd], [dd], lambda e: e.scalar_tensor_tensor(out=dd[:, 4:8], in0=u[:, :, 96], scalar=-1.0, in1=dd[:, 0:4],
                                                                                     op0=ALU.mult, op1=ALU.max))
                            op("dve", [dd], [dd], lambda e: e.reciprocal(out=dd[:, 0:4], in_=dd[:, 4:8]))
                            ost = R_["ost"].next()
                            op("dve", [u, dd], [ost], lambda e: e.tensor_tensor(
                                out=ost[:].rearrange("p (h d) -> p h d", h=4), in0=u[:, :, 0:96],
                                in1=dd.t[:, 0:4].unsqueeze(2).to_broadcast([128, 4, 96]), op=ALU.mult))
                            dma("pool", dr, ost, OB[d][r0:r0 + 128, :], ost[:], sem_buf=ost)
                            pd = pdu.next()
                            for h in range(4):
                                op("pe", [kh, vx], [pd], lambda e: e.matmul(pd[0:96, h, 0:97], lhsT=kh[:, 96 * h:96 * h + 96], rhs=vx[:, h, :], start=True, stop=True))
                            op("dve", [C, E], [C], lambda e: e.tensor_tensor(
                                out=C[0:96, :, :], in0=C[0:96, :, :], in1=E.t[0:96, 12:16].unsqueeze(2).to_broadcast([96, 4, 97]), op=ALU.mult))
                            op("dve", [C, pd], [C], lambda e: e.tensor_tensor(out=C[0:96, :, :], in0=C[0:96, :, :], in1=pd[0:96, :, 0:97], op=ALU.add))
                            op("act", [C], [Cb], lambda e: e.copy(out=Cb[0:96, :, :], in_=C[0:96, :, :]))

                        for j in range(NT):
                            b_step(j, 0)
                            b_step(BWD_ORDER[j], 1)
                        fw.barrier()
                        for R_ in St:
                            for k_ in ("g", "qT", "kT", "kt", "vx", "ost"):
                                fw.release(R_[k_].bufs)

                if "C" in phases:
                    with ExitStack() as cs_:
                        KT = fw.sb(cs_, "cKT", [128, NT, 3, 128], BF16)
                        VX = fw.sb(cs_, "cVX", [128, NT, 6, 65], BF16)
                        BIAS = fw.sb(cs_, "cBias", [128, 54, 128], BF16)
                        op("dve", [], [VX], lambda e: e.memset(VX[:], 1.0))
                        bst = Ring([fw.sb(cs_, "cbst%d" % i, [128, 6, 128], F32) for i in range(2)])
                        for g_ in range(9):
                            b = bst.next()
                            dma("sp", b, dr, b[:], rpbt[l, g_ * 6:(g_ + 1) * 6].rearrange("k s t -> s k t"))
                            op("dve", [b], [BIAS], lambda e: e.tensor_copy(out=BIAS[:, g_ * 6:(g_ + 1) * 6, :], in_=b[:]))
                        for ti in range(NT):
                            dma("sp", KT, dr, KT[:, ti, :, :], FM[ti, :, 9:12, :])
                        for ti in range(NT):
                            dma("sp", VX, dr, VX[:, ti, :, 0:64],
                                TB[ti * 128:(ti + 1) * 128, 2176:2560].rearrange("p (h d) -> p h d", h=6))
                        qr_ = Ring([fw.sb(cs_, "cq%d" % i, [128, 3, 128], BF16) for i in range(2)])
                        qzr = Ring([fw.sb(cs_, "cqz%d" % i, [128, 6, 128], BF16) for i in range(2)])
                        for b in qzr.bufs:
                            op("dve", [], [b], lambda e: e.memset(b[:], 0.0))
                        PTr = Ring([fw.sb(cs_, "cPT%d" % i, [128, 7, 128], BF16) for i in range(2)])
                        osr = Ring([fw.sb(cs_, "cos%d" % i, [128, 384], F32) for i in range(2)])
                        rrr = Ring([fw.sb(cs_, "crr%d" % i, [128, 6], F32) for i in range(2)])
                        psct = Ring([fw.ps(cs_, "cps%d" % i, [128, 8, 128], F32) for i in range(2)])
                        pcout = Ring([fw.ps(cs_, "cpo%d" % i, [128, 6, 65], F32) for i in range(2)])
                        tiles = list(range(NT)) if not last else list(range(NCTX, NT))
                        for ti in tiles:
                            if ti < NCTX:
                                keys = [(0, None), (1, None)]
                            else:
                                keys = [(NCTX + kt, kind) for kt, kind in _key_tiles(ti - NCTX)] + [(0, None), (1, None)]
                            nk = len(keys)
                            q = qr_.next()
                            dma("sp", q, dr, q[:], FM[ti, :, 6:9, :])
                            qz = qzr.next()
                            op("dve", [q], [qz], lambda e: e.tensor_copy(out=qz[0:64, 0:6:2, :], in_=q[0:64, :, :]))
                            op("dve", [q], [qz], lambda e: e.tensor_copy(out=qz[64:128, 1:6:2, :], in_=q[64:128, :, :]))
                            po = pcout.next()
                            for h in range(6):
                                pt, bs = h // 2, 64 * (h % 2)
                                sc = psct.next()
                                for j, (kt, kind) in enumerate(keys):
                                    op("pe", [KT, qz], [sc], lambda e: e.matmul(sc[:, j, :], lhsT=KT[:, kt, pt, :], rhs=qz[:, h, :],
                                                                               start=True, stop=(kind is None)))
                                    if kind is not None:
                                        op("pe", [CB, BIAS], [sc], lambda e: e.matmul(sc[:, j, :], lhsT=IDENT, rhs=BIAS[:, h * 9 + kind, :], start=False, stop=True))
                                PT = PTr.next()
                                op("act", [sc], [PT], lambda e: e.activation(out=PT[:, 0:nk, :], in_=sc[:, 0:nk, :], func=AF.Exp))
                                for j, (kt, kind) in enumerate(keys):
                                    op("pe", [PT, VX], [po], lambda e: e.matmul(po[:, h, :], lhsT=PT[:, j, :], rhs=VX[:, kt, h, :], start=(j == 0), stop=(j == nk - 1)))
                            rr = rrr.next()
                            op("dve", [po], [rr], lambda e: e.reciprocal(out=rr[:], in_=po[:, :, 64]))
                            ost = osr.next()
                            op("dve", [po, rr], [ost], lambda e: e.tensor_tensor(
                                out=ost[:].rearrange("p (h d) -> p h d", h=6), in0=po[:, :, 0:64],
                                in1=rr[:].unsqueeze(2).to_broadcast([128, 6, 64]), op=ALU.mult))
                            dma("pool", dr, ost, OC[ti * 128:(ti + 1) * 128, :], ost[:], sem_buf=ost)
                        fw.barrier()
                        fw.release([KT, VX] + bst.bufs + qr_.bufs + osr.bufs)

                if "O" in phases:
                    with ExitStack() as os_:
                        WO = fw.sb(os_, "wo", [128, 8, D], BF16)
                        wst = Ring([fw.sb(os_, "wost%d" % i, [128, D], F32) for i in range(2)])
                        for kc in range(8):
                            w = wst.next()
                            dma("sp", w, dr, w[:], w_out[l, kc * 128:(kc + 1) * 128, :])
                            op("dve", [w], [WO], lambda e: e.tensor_copy(out=WO[:, kc, :], in_=w[:]))

                        def ring(name, shape, dt, n=2, ps=False):
                            return Ring([(fw.ps if ps else fw.sb)(os_, "%s%d" % (name, i), shape, dt) for i in range(n)])
                        oa = [ring("ooa%d" % d, [128, 256], F32) for d in range(2)]
                        ob = [ring("oob%d" % d, [128, 384], F32) for d in range(2)]
                        oc = ring("ooc", [128, 384], F32)
                        gz = ring("ogz", [128, 3, 384], BF16)
                        xr = ring("ox", [128, D], F32)
                        xo = ring("oxo", [128, D], F32)
                        Y = ring("oY", [128, D], BF16)
                        YT = ring("oYT", [128, 8, 128], BF16)
                        sA = fw.sb(os_, "osA", [128, 256], F32)
                        sB = fw.sb(os_, "osB", [128, 384], F32)
                        tA = fw.sb(os_, "otA", [128, 384], F32)
                        st8 = fw.sb(os_, "ost8", [128, 16], F32)
                        big = fw.sb(os_, "obig", [128, D], F32)
                        pyt = ring("opyt", [128, 8, 128], BF16, 2, ps=True)
                        pu = ring("opu", [128, 2, 512], F32, 2, ps=True)
                        tiles = list(range(NT)) if not last else list(range(NCTX, NT))
                        for ti in tiles:
                            st = 1 if ti < NCTX else 0
                            m = MODS[st]
                            r0 = ti * 128
                            a0, a1 = oa[0].next(), oa[1].next()
                            b0, b1 = ob[0].next(), ob[1].next()
                            c_ = oc.next(); g_ = gz.next(); xt = xr.next()
                            dma("sp", a0, dr, a0[:], OA[0][r0:r0 + 128, :]); dma("sp", a1, dr, a1[:], OA[1][r0:r0 + 128, :])
                            dma("sp", b0, dr, b0[:], OB[0][r0:r0 + 128, :]); dma("sp", b1, dr, b1[:], OB[1][r0:r0 + 128, :])
                            dma("sp", c_, dr, c_[:], OC[r0:r0 + 128, :])
                            dma("sp", g_, dr, g_[:, 0, 0:256], TB[r0:r0 + 128, 768:1024])
                            dma("sp", g_, dr, g_[:, 1, :], TB[r0:r0 + 128, 1792:2176])
                            dma("sp", g_, dr, g_[:, 2, :], TB[r0:r0 + 128, 2560:2944])
                            dma("sp", xt, dr, xt[:], x_src(l, ti))
                            y = Y.next()
                            op("dve", [a0, a1], [sA], lambda e: e.tensor_tensor(out=sA[:], in0=a0[:], in1=a1[:], op=ALU.add))
                            op("dve", [sA], [tA], lambda e: e.tensor_tensor(out=tA[:, 0:256], in0=sA[:], in1=sA[:], op=ALU.mult))
                            op("dve", [tA], [st8], lambda e: e.tensor_reduce(out=st8[:, 0:4], in_=tA[:, 0:256].rearrange("p (h d) -> p h d", h=4),
                                                                            axis=AX.X, op=ALU.add))
                            op("dve", [st8], [st8], lambda e: e.tensor_scalar(out=st8[:, 0:4], in0=st8[:, 0:4], scalar1=1.0 / 64, scalar2=64.0 * EPS,
                                                                              op0=ALU.mult, op1=ALU.add))
                            op("act", [st8], [st8], lambda e: e.activation(out=st8[:, 0:4], in_=st8[:, 0:4], func=AF.Sqrt))
                            op("dve", [st8], [st8], lambda e: e.reciprocal(out=st8[:, 0:4], in_=st8[:, 0:4]))
                            op("dve", [sA, st8], [sA], lambda e: e.tensor_tensor(
                                out=sA[:].rearrange("p (h d) -> p h d", h=4), in0=sA[:].rearrange("p (h d) -> p h d", h=4),
                                in1=st8.t[:, 0:4].unsqueeze(2).to_broadcast([128, 4, 64]), op=ALU.mult))
                            op("dve", [sA, GNA], [sA], lambda e: e.tensor_tensor(out=sA[:], in0=sA[:], in1=GNA[:], op=ALU.mult))
                            op("dve", [sA, g_], [y], lambda e: e.tensor_tensor(out=y[:, 0:256], in0=sA[:], in1=g_[:, 0, 0:256], op=ALU.mult))
                            op("dve", [b0, b1], [sB], lambda e: e.tensor_tensor(out=sB[:], in0=b0[:], in1=b1[:], op=ALU.add))
                            op("dve", [sB], [tA], lambda e: e.tensor_tensor(out=tA[:], in0=sB[:], in1=sB[:], op=ALU.mult))
                            op("dve", [tA], [st8], lambda e: e.tensor_reduce(out=st8[:, 4:8], in_=tA[:].rearrange("p (h d) -> p h d", h=4),
                                                                            axis=AX.X, op=ALU.add))
                            op("dve", [st8], [st8], lambda e: e.tensor_scalar(out=st8[:, 4:8], in0=st8[:, 4:8], scalar1=1.0 / 96, scalar2=EPS,
                                                                              op0=ALU.mult, op1=ALU.add))
                            op("act", [st8], [st8], lambda e: e.activation(out=st8[:, 4:8], in_=st8[:, 4:8], func=AF.Sqrt))
                            op("dve", [st8], [st8], lambda e: e.reciprocal(out=st8[:, 4:8], in_=st8[:, 4:8]))
                            op("dve", [sB, st8], [sB], lambda e: e.tensor_tensor(
                                out=sB[:].rearrange("p (h d) -> p h d", h=4), in0=sB[:].rearrange("p (h d) -> p h d", h=4),
                                in1=st8.t[:, 4:8].unsqueeze(2).to_broadcast([128, 4, 96]), op=ALU.mult))
                            op("dve", [sB, GNB], [sB], lambda e: e.tensor_tensor(out=sB[:], in0=sB[:], in1=GNB[:], op=ALU.mult))
                            op("dve", [sB, g_], [y], lambda e: e.tensor_tensor(out=y[:, 256:640], in0=sB[:], in1=g_[:, 1, :], op=ALU.mult))
                            op("dve", [c_, g_], [y], lambda e: e.tensor_tensor(out=y[:, 640:1024], in0=c_[:], in1=g_[:, 2, :], op=ALU.mult))
                            py = pyt.next()
                            for kc in range(8):
                                op("pe", [y, CB], [py], lambda e: e.transpose(out=py[:, kc, :], in_=y[:, kc * 128:(kc + 1) * 128], identity=IDENT))
                            yT = YT.next()
                            op("act", [py], [yT], lambda e: e.copy(out=yT[:], in_=py[:]))
                            u = pu.next()
                            for nb in range(2):
                                for kc in range(8):
                                    op("pe", [yT, WO], [u], lambda e: e.matmul(u[:, nb, :], lhsT=yT[:, kc, :], rhs=WO[:, kc, nb * 512:(nb + 1) * 512],
                                                                              start=(kc == 0), stop=(kc == 7)))
                            for nb in range(2):
                                op("act", [u], [big, st8], lambda e: e.activation(out=big[:, nb * 512:(nb + 1) * 512], in_=u[:, nb, :], func=AF.Square,
                                                                                accum_out=st8[:, 8 + nb:9 + nb]))
                            op("dve", [st8], [st8], lambda e: e.tensor_tensor(out=st8[:, 10:11], in0=st8[:, 8:9], in1=st8[:, 9:10], op=ALU.add))
                            op("dve", [st8], [st8], lambda e: e.tensor_scalar(out=st8[:, 10:11], in0=st8[:, 10:11], scalar1=1.0 / D, scalar2=EPS,
                                                                              op0=ALU.mult, op1=ALU.add))
                            op("act", [st8], [st8], lambda e: e.activation(out=st8[:, 10:11], in_=st8[:, 10:11], func=AF.Sqrt))
                            op("dve", [st8], [st8], lambda e: e.reciprocal(out=st8[:, 10:11], in_=st8[:, 10:11]))
                            op("dve", [u, st8, m], [big], lambda e: e.scalar_tensor_tensor(
                                out=big[:].rearrange("p (a b) -> p a b", a=2), in0=u[:], scalar=st8[:, 10:11],
                                in1=m[:, 2 * D:3 * D].rearrange("p (a b) -> p a b", a=2), op0=ALU.mult, op1=ALU.mult))
                            xo_ = xo.next()
                            op("dve", [big, xt], [xo_], lambda e: e.tensor_tensor(out=xo_[:], in0=big[:], in1=xt[:], op=ALU.add))
                            if last:
                                dma("pool", dr, xo_, y_out[(ti - NCTX) * 128:(ti - NCTX + 1) * 128, :], xo_[:], sem_buf=xo_)
                            else:
                                dma("pool", dr, xo_, XC[r0:r0 + 128, :], xo_[:], sem_buf=xo_)
                        fw.barrier()
                        fw.release(wst.bufs + oa[0].bufs + oa[1].bufs + ob[0].bufs + ob[1].bufs + oc.bufs + gz.bufs + xr.bufs + xo.bufs)
                fw.barrier()
                fw.release([BM, GPRE, GPOST, LB, OML, GNA, GNB, GB])
        fw.barrier()
    return nc


def make_in_maps(inputs, cores):
    x = np.asarray(inputs["x"], np.float32)
    c = np.asarray(inputs["c"], np.float32)
    ctx = np.asarray(inputs["ctx"], np.float32)
    c_ctx = np.asarray(inputs["c_ctx"], np.float32)
    shared = {
        "w_mod": np.ascontiguousarray(inputs["w_mod"], np.float32),
        "b_mod": np.ascontiguousarray(inputs["b_mod"], np.float32),
        "g_pre": np.ascontiguousarray(inputs["g_pre"], np.float32),
        "g_post": np.ascontiguousarray(inputs["g_post"], np.float32),
        "w_in": np.ascontiguousarray(inputs["w_in"], np.float32),
        "w_out": np.ascontiguousarray(inputs["w_out"], np.float32),
        "hgrn_lb": np.ascontiguousarray(np.asarray(inputs["hgrn_lb"], np.float32).reshape(DEPTH, 512)),
        "hgrn_gn": np.ascontiguousarray(inputs["hgrn_gn"], np.float32),
        "gate_b": np.ascontiguousarray(np.asarray(inputs["mlstm_gate_b"], np.float32).reshape(DEPTH, 16)),
        "mlstm_gn": np.ascontiguousarray(inputs["mlstm_gn"], np.float32),
        "rpbt": _rpb_tables(np.asarray(inputs["na_rpb"], np.float32)),
        "rope": _rope_tables().reshape(NT, 128, 384),
        "consts": _consts(),
    }
    maps = []
    for b in cores:
        cT = np.concatenate([c[b].reshape(8, 128).T, c_ctx.reshape(8, 128).T], axis=1)
        m = dict(shared)
        m["x"] = np.ascontiguousarray(x[b])
        m["ctx"] = np.ascontiguousarray(ctx[b])
        m["cT"] = np.ascontiguousarray(cT, np.float32)
        maps.append(m)
    return maps


def kernel(**inputs):
    nc = build()
    maps = make_in_maps(inputs, list(range(8)))
    res = run_bass_kernel_spmd(nc, maps, core_ids=list(range(8)))
    out = np.stack([np.asarray(r["y"], np.float32) for r in res.results], axis=0)
    return out
```

```python
import os
import numpy as np
from contextlib import ExitStack
import concourse.bass as bass
import concourse.mybir as mybir
from concourse.bass_utils import run_bass_kernel_spmd

F32 = mybir.dt.float32
BF16 = mybir.dt.bfloat16
AF = mybir.ActivationFunctionType
ALU = mybir.AluOpType
AX = mybir.AxisListType

D = 1024
NT = 34
NCTX = 2
TOK = NT * 128
PIN = 4752
DEPTH = 2
EPS = 1e-6
NTB = 2944
NTBS = 4352
NTF = 528
NFM = 20
NEG = -30000.0
BWD_ORDER = [1, 0] + list(range(NT - 1, 1, -1))

SEM_CHUNK = 24000


class Buf:
    def __init__(self, t=None, name=""):
        self.t = t
        self.name = name
        self.w = None
        self.r = {}
        self.dsem = None

    def __getitem__(self, k):
        return self.t[k]


class DramTrack:
    def __init__(self):
        self.d = {}

    def get(self, name):
        if name not in self.d:
            self.d[name] = Buf(None, name)
        return self.d[name]


class DmaSem:
    def __init__(self, sem):
        self.sem = sem
        self.n = 0


class FW:
    def __init__(self, nc, es):
        self.nc = nc
        self.es = es
        self.E = {"pe": nc.tensor, "act": nc.scalar, "dve": nc.vector, "pool": nc.gpsimd, "sp": nc.sync}
        self.cnt = {e: 0 for e in self.E}
        self.sems = {e: [] for e in self.E}
        self.seen = {e: {} for e in self.E}
        self.dsems = []
        self.free_dsems = []
        self.bar_sem = es.enter_context(nc.semaphore("barsem"))
        self.bar_n = 0
        self.uid = 0

    def sb(self, es, name, shape, dt):
        self.uid += 1
        t = es.enter_context(self.nc.sbuf_tensor("%s_%d" % (name, self.uid), list(shape), dt))
        return Buf(t, name)

    def ps(self, es, name, shape, dt):
        self.uid += 1
        t = es.enter_context(self.nc.psum_tensor("%s_%d" % (name, self.uid), list(shape), dt))
        return Buf(t, name)

    def get_dsem(self):
        if self.free_dsems:
            return self.free_dsems.pop()
        s = self.es.enter_context(self.nc.semaphore("dsem%d" % len(self.dsems)))
        d = DmaSem(s)
        self.dsems.append(d)
        return d

    def release(self, bufs):
        for b in bufs:
            if b.dsem is not None:
                self.free_dsems.append(b.dsem)
                b.dsem = None

    def _esem(self, e, n):
        k = (n - 1) // SEM_CHUNK
        while len(self.sems[e]) <= k:
            self.sems[e].append(self.es.enter_context(self.nc.semaphore("es_%s_%d" % (e, len(self.sems[e])))))
        return self.sems[e][k], (n - 1) % SEM_CHUNK + 1

    def _wait_tok(self, e, tok):
        key, n = tok
        if key == e and e == "pe":
            return
        if self.seen[e].get(key, 0) >= n:
            return
        self.seen[e][key] = n
        eng = self.E[e]
        if isinstance(key, str):
            sem, val = self._esem(key, n)
            eng.wait_ge(sem, val)
        else:
            eng.wait_ge(key.sem, n)

    def _deps(self, e, reads, writes):
        for b in reads:
            if b.w is not None:
                self._wait_tok(e, b.w)
        for b in writes:
            if b.w is not None:
                self._wait_tok(e, b.w)
            for key, n in list(b.r.items()):
                self._wait_tok(e, (key, n))

    def _mark(self, tok, reads, writes):
        key, n = tok
        for b in writes:
            b.w = tok
            b.r = {}
        for b in reads:
            if b in writes:
                continue
            if b.r.get(key, 0) < n:
                b.r[key] = n

    def op(self, e, reads, writes, fn):
        self._deps(e, reads, writes)
        ins = fn(self.E[e])
        self.cnt[e] += 1
        n = self.cnt[e]
        sem, _ = self._esem(e, n)
        ins.then_inc(sem, 1)
        self._mark((e, n), reads, writes)
        return ins

    def dma(self, q, out_buf, in_buf, out_ap, in_ap, sem_buf=None, **kw):
        if isinstance(out_buf, DramTrack):
            out_buf = out_buf.get(out_ap.name)
        if isinstance(in_buf, DramTrack):
            in_buf = in_buf.get(in_ap.name)
        sb_ = sem_buf if sem_buf is not None else out_buf
        if sb_.dsem is None:
            sb_.dsem = self.get_dsem()
        ds = sb_.dsem
        self._deps(q, [in_buf], [out_buf])
        ins = self.E[q].dma_start(out=out_ap, in_=in_ap, **kw)
        ds.n += 16
        ins.then_inc(ds.sem, 16)
        self._mark((ds, ds.n), [in_buf], [out_buf])
        return ins

    def barrier(self):
        sp = self.E["sp"]
        for e in self.E:
            if e != "sp" and self.cnt[e] > 0:
                self._wait_tok("sp", (e, self.cnt[e]))
        for d in self.dsems:
            if d.n > 0:
                self._wait_tok("sp", (d, d.n))
        self.bar_n += 1
        sp.sem_inc(self.bar_sem, 1)
        for e in self.E:
            if e != "sp":
                self.E[e].wait_ge(self.bar_sem, self.bar_n)
        for e in self.E:
            for e2 in self.E:
                if e2 != e:
                    self.seen[e][e2] = self.cnt[e2]
            for d in self.dsems:
                self.seen[e][d] = d.n


class Ring:
    def __init__(self, bufs):
        self.bufs = bufs
        self.i = 0

    def next(self):
        b = self.bufs[self.i % len(self.bufs)]
        self.i += 1
        return b


def _consts():
    s = np.arange(128)[:, None]
    t = np.arange(128)[None, :]
    c = np.zeros((128, 6, 128), np.float32)
    c[:, 0] = (s == t)
    c[:, 1] = (s <= t)
    c[:, 2] = (s >= t)
    c[:, 3] = (s > t)
    c[:, 4] = (s < t)
    c[:, 5] = 1.0
    return c.reshape(128, 768)


def _rope_tables():
    tab = np.zeros((NT, 128, 4, 96), np.float32)
    qs = np.float32(96 ** -0.5)
    tab[:NCTX, :, 0, :] = qs
    tab[:NCTX, :, 2, :] = 1.0
    inv = (np.float32(10000.0) ** (-np.arange(0, 48, 2, dtype=np.float32) / np.float32(48))).astype(np.float32)
    T = np.arange(4096)
    pos = [(T // 64).astype(np.float32), (T % 64).astype(np.float32)]
    cos = np.zeros((4096, 2, 2, 24), np.float32)
    sin = np.zeros((4096, 2, 2, 24), np.float32)
    for p in range(2):
        ang = (pos[p][:, None] * inv[None, :]).astype(np.float32)
        cs, sn = np.cos(ang).astype(np.float32), np.sin(ang).astype(np.float32)
        cos[:, p, 0], cos[:, p, 1] = cs, cs
        sin[:, p, 0], sin[:, p, 1] = -sn, sn
    cos = cos.reshape(32, 128, 96)
    sin = sin.reshape(32, 128, 96)
    tab[NCTX:, :, 0] = cos * qs
    tab[NCTX:, :, 1] = sin * qs
    tab[NCTX:, :, 2] = cos
    tab[NCTX:, :, 3] = sin
    return tab


def _rpb_tables(na_rpb):
    L, H = na_rpb.shape[0], na_rpb.shape[1]
    s = np.arange(128)[:, None]
    t = np.arange(128)[None, :]
    kc, qc = s % 64, t % 64
    qr = t // 64
    cs = np.clip(qc - 8, 0, 48)
    band = (kc >= cs) & (kc < cs + 16)
    dc = np.clip(kc - qc + 15, 0, 30)
    out = np.full((L, H, 9, 128, 128), NEG, np.float32)
    for k in range(9):
        o = [-3, -2, -1, 0, 1, 2, 3, -2, 2][k]
        kr = 2 * o + s // 64
        diff = kr - qr
        ok = band & (diff >= -7) & (diff <= 7)
        if k >= 7:
            ok = ok & (diff >= -4) & (diff <= 3)
        dr = np.clip(diff + 7, 0, 14)
        for l in range(L):
            for h in range(H):
                g = na_rpb[l, h][dr, dc]
                out[l, h, k] = np.where(ok, g, np.float32(NEG))
    return out.reshape(L, H * 9, 128, 128)


def _key_tiles(lt):
    if lt <= 1:
        return [(kt, kt - lt + 3) for kt in range(0, 4)]
    if lt >= 30:
        return [(kt, kt - lt + 3) for kt in range(28, 32)]
    res = []
    for o in range(-2, 3):
        k = o + 3
        if o == -2:
            k = 7
        if o == 2:
            k = 8
        res.append((lt + o, k))
    return res


def build(debug=(), nlayers=DEPTH, phases="PABCO"):
    nc = bass.Bass("TRN2", target_bir_lowering=False)

    def din(name, shape, dt=F32):
        return nc.dram_tensor(name, list(shape), dt, kind="ExternalInput").ap()

    def dscr(name, shape, dt=F32):
        kind = "ExternalOutput" if name in debug else "Internal"
        return nc.dram_tensor(name, list(shape), dt, kind=kind).ap()

    x_in = din("x", [4096, D])
    ctx_in = din("ctx", [256, D])
    cT_in = din("cT", [128, 16])
    w_mod = din("w_mod", [DEPTH, D, 3 * D])
    b_mod = din("b_mod", [DEPTH, 3 * D])
    g_pre = din("g_pre", [DEPTH, D])
    g_post = din("g_post", [DEPTH, D])
    w_in = din("w_in", [DEPTH, D, PIN])
    w_out = din("w_out", [DEPTH, D, D])
    hgrn_lb = din("hgrn_lb", [DEPTH, 512])
    hgrn_gn = din("hgrn_gn", [DEPTH, 256])
    gate_b = din("gate_b", [DEPTH, 16])
    mlstm_gn = din("mlstm_gn", [DEPTH, 384])
    rpbt = din("rpbt", [DEPTH, 54, 128, 128])
    rope = din("rope", [NT, 128, 4 * 96])
    consts = din("consts", [128, 768])
    y_out = nc.dram_tensor("y", [4096, D], F32, kind="ExternalOutput").ap()

    XC = dscr("XC", [TOK, D])
    TB = dscr("TB", [TOK, NTB], BF16)
    TF = dscr("TF", [TOK, NTF])
    FM = dscr("FM", [NT, 128, NFM, 128], BF16)
    OA = [dscr("OA%d" % d, [TOK, 256]) for d in range(2)]
    OB = [dscr("OB%d" % d, [TOK, 384]) for d in range(2)]
    OC = dscr("OC", [TOK, 384])
    dr = DramTrack()

    with ExitStack() as es:
        fw = FW(nc, es)
        op, dma = fw.op, fw.dma

        CST = fw.sb(es, "cst", [128, 768], F32)
        dma("sp", CST, dr, CST[:], consts[:, :])
        IDENTF = CST.t[:, 0:128]
        TRI = [CST.t[:, 128:256], CST.t[:, 256:384]]
        TRIS = [CST.t[:, 384:512], CST.t[:, 512:640]]
        ONESF = CST.t[:, 640:768]
        CB = fw.sb(es, "cstb", [128, 384], BF16)
        op("dve", [CST], [CB], lambda e: e.tensor_copy(out=CB[:], in_=CST[:, 0:384]))
        IDENT = CB.t[:, 0:128]
        MASK = [CB.t[:, 128:256], CB.t[:, 256:384]]
        CT = fw.sb(es, "cT", [128, 16], F32)
        dma("sp", CT, dr, CT[:], cT_in[:, :])
        SC = fw.sb(es, "sc", [128, 16], F32)
        op("act", [CT], [SC], lambda e: e.activation(out=SC[:], in_=CT[:], func=AF.Exp, scale=-1.0))
        op("act", [SC], [SC], lambda e: e.activation(out=SC[:], in_=SC[:], func=AF.Ln, bias=1.0))
        op("act", [SC], [SC], lambda e: e.activation(out=SC[:], in_=SC[:], func=AF.Exp, scale=-1.0))
        op("dve", [SC, CT], [SC], lambda e: e.tensor_tensor(out=SC[:], in0=SC[:], in1=CT[:], op=ALU.mult))
        fw.barrier()

        def x_src(l, ti):
            if l == 0:
                if ti < NCTX:
                    return ctx_in[ti * 128:(ti + 1) * 128, :]
                return x_in[(ti - NCTX) * 128:(ti - NCTX + 1) * 128, :]
            return XC[ti * 128:(ti + 1) * 128, :]

        for l in range(nlayers):
            last = (l == DEPTH - 1)
            with ExitStack() as les:
                MODS = [fw.sb(les, "mod%d" % i, [128, 3 * D], F32) for i in range(2)]
                LB = fw.sb(les, "lb", [128, 512], F32)
                OML = fw.sb(les, "oml", [128, 512], F32)
                GNA = fw.sb(les, "gna", [128, 256], F32)
                GNB = fw.sb(les, "gnb", [128, 384], F32)
                GB = fw.sb(les, "gb", [128, 16], F32)
                dma("sp", GNA, dr, GNA[:], hgrn_gn[l:l + 1, :].partition_broadcast(128))
                dma("sp", GNB, dr, GNB[:], mlstm_gn[l:l + 1, :].partition_broadcast(128))
                dma("sp", GB, dr, GB[:], gate_b[l:l + 1, :].partition_broadcast(128))
                if l == 0:
                    op("dve", [], [LB], lambda e: e.memset(LB[:], 0.0))
                else:
                    dma("sp", LB, dr, LB[:], hgrn_lb[1:2, :].partition_broadcast(128))
                    dma("sp", OML, dr, OML[:], hgrn_lb[0:1, :].partition_broadcast(128))
                    op("dve", [LB, OML], [LB], lambda e: e.tensor_sub(out=LB[:], in0=LB[:], in1=OML[:]))
                    op("act", [LB], [LB], lambda e: e.activation(out=LB[:], in_=LB[:], func=AF.Exp, scale=-1.0))
                    op("act", [LB], [LB], lambda e: e.activation(out=LB[:], in_=LB[:], func=AF.Ln, bias=1.0))
                    op("act", [LB], [LB], lambda e: e.activation(out=LB[:], in_=LB[:], func=AF.Exp, scale=-1.0))
                op("dve", [LB], [OML], lambda e: e.tensor_scalar(out=OML[:], in0=LB[:], scalar1=-1.0, scalar2=1.0,
                                                                  op0=ALU.mult, op1=ALU.add))
                with ExitStack() as ms:
                    BM = fw.sb(ms, "bm", [128, 3 * D], F32)
                    GPRE = fw.sb(ms, "gpre", [128, D], F32)
                    GPOST = fw.sb(ms, "gpost", [128, D], F32)
                    dma("sp", BM, dr, BM[:], b_mod[l:l + 1, :].partition_broadcast(128))
                    dma("sp", GPRE, dr, GPRE[:], g_pre[l:l + 1, :].partition_broadcast(128))
                    dma("sp", GPOST, dr, GPOST[:], g_post[l:l + 1, :].partition_broadcast(128))
                    wm = Ring([fw.sb(ms, "wm%d" % i, [128, 8, 512], F32) for i in range(2)])
                    pmod = Ring([fw.ps(ms, "pmod%d" % i, [128, 512], F32) for i in range(4)])
                    for nb in range(6):
                        w = wm.next()
                        dma("sp", w, dr, w[:], w_mod[l, :, nb * 512:(nb + 1) * 512].rearrange("(kc p) n -> p kc n", p=128))
                        for st in range(2):
                            pm = pmod.next()
                            for kc in range(8):
                                op("pe", [SC, w], [pm], lambda e: e.matmul(
                                    pm[:], lhsT=SC[:, st * 8 + kc:st * 8 + kc + 1].to_broadcast([128, 128]),
                                    rhs=w[:, kc, :], start=(kc == 0), stop=(kc == 7)))
                            m = MODS[st]
                            op("dve", [pm, BM], [m], lambda e: e.tensor_tensor(
                                out=m[:, nb * 512:(nb + 1) * 512], in0=pm[:], in1=BM[:, nb * 512:(nb + 1) * 512], op=ALU.add))
                    for st in range(2):
                        m = MODS[st]
                        op("dve", [m, GPRE], [m], lambda e: e.scalar_tensor_tensor(
                            out=m[:, D:2 * D], in0=m[:, D:2 * D], scalar=1.0, in1=GPRE[:], op0=ALU.add, op1=ALU.mult))
                        op("dve", [m, GPOST], [m], lambda e: e.tensor_tensor(
                            out=m[:, 2 * D:3 * D], in0=m[:, 2 * D:3 * D], in1=GPOST[:], op=ALU.mult))
                    fw.barrier()
                    fw.release(wm.bufs + [BM, GPRE, GPOST])

                if "P" in phases:
                    with ExitStack() as ps_:
                        WIN = fw.sb(ps_, "win", [128, 8, PIN], BF16)
                        wsc = ExitStack()
                        wst = Ring([fw.sb(wsc, "wst%d" % i, [128, 1188], F32) for i in range(2)])
                        k = 0
                        for kc in range(8):
                            for q4 in range(4):
                                w = wst.next()
                                dma("sp", w, dr, w[:], w_in[l, kc * 128:(kc + 1) * 128, q4 * 1188:(q4 + 1) * 1188])
                                eng = "dve" if k % 2 == 0 else "act"
                                if eng == "dve":
                                    op("dve", [w], [WIN], lambda e: e.tensor_copy(out=WIN[:, kc, q4 * 1188:(q4 + 1) * 1188], in_=w[:]))
                                else:
                                    op("act", [w], [WIN], lambda e: e.copy(out=WIN[:, kc, q4 * 1188:(q4 + 1) * 1188], in_=w[:]))
                                k += 1
                        fw.barrier()
                        fw.release(wst.bufs)
                        wsc.close()
                        xr = Ring([fw.sb(ps_, "x%d" % i, [128, D], F32) for i in range(2)])
                        rp = Ring([fw.sb(ps_, "rope%d" % i, [128, 4, 96], F32) for i in range(2)])
                        sqr = Ring([fw.sb(ps_, "sq%d" % i, [128, D], F32) for i in range(2)])
                        ssr = Ring([fw.sb(ps_, "ss%d" % i, [128, 4], F32) for i in range(3)])
                        hbr = Ring([fw.sb(ps_, "hb%d" % i, [128, D], BF16) for i in range(2)])
                        hT = Ring([fw.sb(ps_, "hT%d" % i, [128, 8, 128], BF16) for i in range(2)])
                        phT = fw.ps(ps_, "phT", [128, 8, 128], BF16)
                        pj = Ring([fw.ps(ps_, "pj%d" % i, [128, 512], F32) for i in range(5)])
                        ptt = Ring([fw.ps(ps_, "ptt%d" % i, [128, 8, 128], BF16) for i in range(2)])
                        tbs = Ring([fw.sb(ps_, "tbs%d" % i, [128, NTBS], BF16) for i in range(2)])
                        tfs = Ring([fw.sb(ps_, "tfs%d" % i, [128, NTF], F32) for i in range(2)])
                        fms = Ring([fw.sb(ps_, "fms%d" % i, [128, NFM, 128], BF16) for i in range(2)])
                        tmpr = Ring([fw.sb(ps_, "ptmp%d" % i, [128, 512], F32) for i in range(8)])

                        def proj(ps, c0, wd, hTb):
                            for kc in range(8):
                                op("pe", [hTb, WIN], [ps], lambda e: e.matmul(
                                    ps[:, 0:wd], lhsT=hTb[:, kc, :], rhs=WIN[:, kc, c0:c0 + wd], start=(kc == 0), stop=(kc == 7)))

                        def sigm(src_buf, src_ap, wd):
                            a = tmpr.next(); b = tmpr.next()
                            op("act", [src_buf], [a], lambda e: e.activation(out=a[:, 0:wd], in_=src_ap, func=AF.Exp, scale=-1.0))
                            op("act", [a], [b], lambda e: e.activation(out=b[:, 0:wd], in_=a[:, 0:wd], func=AF.Ln, bias=1.0))
                            op("act", [b], [a], lambda e: e.activation(out=a[:, 0:wd], in_=b[:, 0:wd], func=AF.Exp, scale=-1.0))
                            return a, b

                        PST = {}

                        def prologue(ti):
                            st = 1 if ti < NCTX else 0
                            m = MODS[st]
                            xt = xr.next()
                            dma("sp", xt, dr, xt[:], x_src(l, ti))
                            rt = rp.next()
                            dma("sp", rt, dr, rt[:], rope[ti].rearrange("p (a b) -> p a b", a=4))
                            sq = sqr.next(); ss = ssr.next(); hb = hbr.next()
                            op("act", [xt], [sq, ss], lambda e: e.activation(out=sq[:], in_=xt[:], func=AF.Square, accum_out=ss[:, 0:1]))
                            op("dve", [ss], [ss], lambda e: e.tensor_scalar(out=ss[:, 1:2], in0=ss[:, 0:1], scalar1=1.0 / D, scalar2=EPS,
                                                                            op0=ALU.mult, op1=ALU.add))
                            op("act", [ss], [ss], lambda e: e.activation(out=ss[:, 2:3], in_=ss[:, 1:2], func=AF.Ln))
                            op("act", [ss], [ss], lambda e: e.activation(out=ss[:, 3:4], in_=ss[:, 2:3], func=AF.Exp, scale=-0.5))
                            op("dve", [xt, ss, m], [sq], lambda e: e.scalar_tensor_tensor(
                                out=sq[:], in0=xt[:], scalar=ss[:, 3:4], in1=m[:, D:2 * D], op0=ALU.mult, op1=ALU.mult))
                            op("dve", [sq, m], [hb], lambda e: e.tensor_tensor(out=hb[:], in0=sq[:], in1=m[:, 0:D], op=ALU.add))
                            for kc in range(8):
                                op("pe", [hb, CB], [phT], lambda e: e.transpose(out=phT[:, kc, :], in_=hb[:, kc * 128:(kc + 1) * 128], identity=IDENT))
                            hTb = hT.next()
                            op("act", [phT], [hTb], lambda e: e.copy(out=hTb[:], in_=phT[:]))
                            PST[ti] = dict(hTb=hTb, rt=rt)

                        def groups_a(ti):
                            hTb, rt = PST[ti]["hTb"], PST[ti]["rt"]
                            tb = tbs.next()
                            tf = tfs.next()
                            PST[ti].update(tb=tb, tf=tf)
                            ps = pj.next(); proj(ps, 256, 512, hTb)
                            sg, sp_ = sigm(ps, ps[:], 512)
                            if l == 0:
                                op("dve", [sp_], [tf], lambda e: e.tensor_scalar_mul(out=tf[:, 0:512], in0=sp_[:], scalar1=-1.0))
                            else:
                                op("dve", [sg, OML], [sg], lambda e: e.tensor_tensor(out=sg[:], in0=sg[:], in1=OML[:], op=ALU.mult))
                                op("dve", [sg, LB], [sg], lambda e: e.tensor_tensor(out=sg[:], in0=sg[:], in1=LB[:], op=ALU.add))
                                op("act", [sg], [tf], lambda e: e.activation(out=tf[:, 0:512], in_=sg[:], func=AF.Ln))
                            op("dve", [sg], [tb], lambda e: e.tensor_scalar(out=tb[:, 0:512], in0=sg[:], scalar1=-1.0, scalar2=1.0,
                                                                           op0=ALU.mult, op1=ALU.add))
                            ps = pj.next(); proj(ps, 0, 256, hTb)
                            sg, _ = sigm(ps, ps[:, 0:256], 256)
                            op("dve", [ps, sg], [tb], lambda e: e.tensor_tensor(out=tb[:, 2944:3200], in0=ps[:, 0:256], in1=sg[:, 0:256], op=ALU.mult))
                            ps = pj.next(); proj(ps, 768, 512, hTb)
                            op("dve", [ps], [tb], lambda e: e.tensor_copy(out=tb[:, 512:768], in_=ps[:, 0:256]))
                            sg, _ = sigm(ps, ps[:, 256:512], 256)
                            op("dve", [ps, sg], [tb], lambda e: e.tensor_tensor(out=tb[:, 768:1024], in0=ps[:, 256:512], in1=sg[:, 0:256], op=ALU.mult))
                            for (c0, dst, ci) in ((1280, 3200, 0), (1664, 1024, 2)):
                                ps = pj.next(); proj(ps, c0, 384, hTb)
                                t2 = tmpr.next(); t3 = tmpr.next()
                                pv = ps.t[:, 0:384].rearrange("p (h a b j) -> p h a b j", h=4, a=2, b=2)
                                t2v = t2.t[:, 0:384].rearrange("p (h a b j) -> p h a b j", h=4, a=2, b=2)
                                cosb = rt.t[:, ci, :].unsqueeze(1).to_broadcast([128, 4, 96])
                                sinv = rt.t[:, ci + 1, :].rearrange("p (a b j) -> p a b j", a=2, b=2)
                                op("dve", [ps, rt], [t3], lambda e: e.tensor_tensor(
                                    out=t3[:, 0:384].rearrange("p (h d) -> p h d", h=4), in0=ps[:, 0:384].rearrange("p (h d) -> p h d", h=4),
                                    in1=cosb, op=ALU.mult))
                                for b_ in range(2):
                                    op("dve", [ps, rt], [t2], lambda e: e.tensor_tensor(
                                        out=t2v[:, :, :, b_, :], in0=pv[:, :, :, 1 - b_, :],
                                        in1=sinv[:, :, b_, :].unsqueeze(1).to_broadcast([128, 4, 2, 24]), op=ALU.mult))
                                op("pool", [t3, t2], [tb], lambda e: e.tensor_tensor(out=tb[:, dst:dst + 384], in0=t3[:, 0:384], in1=t2[:, 0:384], op=ALU.add))

                        def groups_b(ti):
                            hTb, rt, tb, tf = PST[ti]["hTb"], PST[ti]["rt"], PST[ti]["tb"], PST[ti]["tf"]
                            ps = pj.next(); proj(ps, 2048, 384, hTb)
                            op("act", [ps], [tb], lambda e: e.copy(out=tb[:, 1408:1792], in_=ps[:, 0:384]))
                            ps = pj.next(); proj(ps, 2432, 384, hTb)
                            sgo, _ = sigm(ps, ps[:, 0:384], 384)
                            ps = pj.next(); proj(ps, 2816, 384, hTb)
                            sgz, _ = sigm(ps, ps[:, 0:384], 384)
                            op("dve", [ps, sgz], [sgz], lambda e: e.tensor_tensor(out=sgz[:, 0:384], in0=ps[:, 0:384], in1=sgz[:, 0:384], op=ALU.mult))
                            op("pool", [sgo, sgz], [tb], lambda e: e.tensor_tensor(out=tb[:, 1792:2176], in0=sgo[:, 0:384], in1=sgz[:, 0:384], op=ALU.mult))
                            ps = pj.next(); proj(ps, 3200, 16, hTb)
                            op("dve", [ps, GB], [tf], lambda e: e.tensor_tensor(out=tf[:, 512:528], in0=ps[:, 0:16], in1=GB[:], op=ALU.add))
                            _, spg = sigm(tf, tf[:, 520:528], 8)
                            op("dve", [spg], [tf], lambda e: e.tensor_scalar_mul(out=tf[:, 520:528], in0=spg[:, 0:8], scalar1=-1.0))
                            ps = pj.next(); proj(ps, 3216, 384, hTb)
                            op("act", [ps], [tb], lambda e: e.activation(out=tb[:, 3584:3968], in_=ps[:, 0:384], func=AF.Copy, scale=0.125))
                            ps = pj.next(); proj(ps, 3600, 384, hTb)
                            op("dve", [ps], [tb], lambda e: e.tensor_copy(out=tb[:, 3968:4352], in_=ps[:, 0:384]))
                            ps = pj.next(); proj(ps, 3984, 384, hTb)
                            op("act", [ps], [tb], lambda e: e.copy(out=tb[:, 2176:2560], in_=ps[:, 0:384]))
                            ps = pj.next(); proj(ps, 4368, 384, hTb)
                            sg, _ = sigm(ps, ps[:, 0:384], 384)
                            op("dve", [ps, sg], [tb], lambda e: e.tensor_tensor(out=tb[:, 2560:2944], in0=ps[:, 0:384], in1=sg[:, 0:384], op=ALU.mult))

                        def tail(ti):
                            tb, tf = PST[ti]["tb"], PST[ti]["tf"]
                            fm = fms.next()
                            srcA = [2944, 3072, 0, 128, 256, 384]
                            srcC = [3584, 3712, 3840, 3968, 4096, 4224]
                            pt_ = ptt.next()
                            for j, c0 in enumerate(srcA):
                                op("pe", [tb, CB], [pt_], lambda e: e.transpose(out=pt_[:, j, :], in_=tb[:, c0:c0 + 128], identity=IDENT))
                            op("dve", [pt_], [fm], lambda e: e.tensor_copy(out=fm[:, 0:6, :], in_=pt_[:, 0:6, :]))
                            pt_ = ptt.next()
                            for j, c0 in enumerate(srcC):
                                op("pe", [tb, CB], [pt_], lambda e: e.transpose(out=pt_[:, j, :], in_=tb[:, c0:c0 + 128], identity=IDENT))
                            op("act", [pt_], [fm], lambda e: e.copy(out=fm[:, 6:12, :], in_=pt_[:, 0:6, :]))
                            pt_ = ptt.next()
                            for j in range(8):
                                c0 = (3200 if j < 4 else 1024) + (j % 4) * 96
                                op("pe", [tb, CB], [pt_], lambda e: e.transpose(out=pt_[0:96, j, :], in_=tb[:, c0:c0 + 96], identity=IDENT))
                            op("dve", [pt_], [fm], lambda e: e.tensor_copy(out=fm[0:96, 12:20, :], in_=pt_[0:96, 0:8, :]))
                            dma("pool", dr, tb, TB[ti * 128:(ti + 1) * 128, :], tb[:, 0:NTB], sem_buf=tb)
                            dma("pool", dr, tf, TF[ti * 128:(ti + 1) * 128, :], tf[:], sem_buf=tf)
                            dma("pool", dr, fm, FM[ti], fm[:], sem_buf=fm)
                            del PST[ti]

                        prologue(0)
                        for ti in range(NT):
                            if ti + 1 < NT:
                                prologue(ti + 1)
                            groups_a(ti)
                            if ti > 0:
                                tail(ti - 1)
                            groups_b(ti)
                        tail(NT - 1)
                        fw.barrier()
                        fw.release(xr.bufs + rp.bufs + tbs.bufs + tfs.bufs + fms.bufs)

                if "A" in phases:
                    with ExitStack() as as_:
                        def ring(name, shape, dt, n=2, ps=False):
                            return Ring([(fw.ps if ps else fw.sb)(as_, "%s%d" % (name, i), shape, dt) for i in range(n)])
                        St = []
                        for d in range(2):
                            S = fw.sb(as_, "S%d" % d, [128, 2, 64], F32)
                            Sb = fw.sb(as_, "Sb%d" % d, [128, 2, 128], BF16)
                            op("dve", [], [S], lambda e: e.memset(S[:], 0.0))
                            op("dve", [], [Sb], lambda e: e.memset(Sb[:], 0.0))
                            R_ = dict(S=S, Sb=Sb,
                                      lf=ring("alf%d" % d, [128, 256], F32), qT=ring("aqT%d" % d, [128, 2, 128], BF16),
                                      kT=ring("akT%d" % d, [128, 2, 128], BF16), kt=ring("akt%d" % d, [128, 256], BF16),
                                      vt=ring("avt%d" % d, [128, 256], BF16), ost=ring("aos%d" % d, [128, 256], F32),
                                      bT=ring("abT%d" % d, [128, 2, 128], F32), E1=ring("aE1%d" % d, [128, 2, 128], F32),
                                      Es=ring("aEs%d" % d, [128, 2, 128], F32), tmp=ring("atm%d" % d, [128, 2, 128], F32),
                                      rr=ring("arr%d" % d, [128, 2, 8], F32), edk=ring("aed%d" % d, [128, 256], F32),
                                      kd=ring("akd%d" % d, [128, 256], BF16), Qf=ring("aQf%d" % d, [128, 2, 128], BF16),
                                      Qs=ring("aQs%d" % d, [128, 2, 2, 128], BF16), ATm=ring("aAT%d" % d, [128, 4, 128], BF16),
                                      Ek=ring("aEk%d" % d, [128, 2, 128], F32),
                                      Kv=[ring("aKv%d_%d" % (d, i), [128, 2, 128], BF16) for i in range(4)])
                            for i in range(4):
                                for b in R_["Kv"][i].bufs:
                                    op("dve", [], [b], lambda e: e.memset(b[:], 0.0))
                            for b in R_["rr"].bufs + R_["Qs"].bufs:
                                op("dve", [], [b], lambda e: e.memset(b[:], 0.0))
                            St.append(R_)
                        pbT = ring("apbT", [128, 2, 128], F32, 1, ps=True)
                        prs = ring("aprs", [128, 256], F32, 1, ps=True)
                        pAT = ring("apAT", [128, 4, 128], F32, 2, ps=True)
                        pO = ring("apO", [128, 256], F32, 2, ps=True)
                        pU = ring("apU", [128, 2, 128], F32, 1, ps=True)

                        def a_front(ti, d):
                            R_ = St[d]
                            lf = R_["lf"].next(); qT = R_["qT"].next(); kT = R_["kT"].next(); kt = R_["kt"].next(); vt = R_["vt"].next()
                            r0 = ti * 128
                            dma("sp", lf, dr, lf[:], TF[r0:r0 + 128, d * 256:(d + 1) * 256])
                            dma("sp", qT, dr, qT[:], FM[ti, :, 0:2, :])
                            dma("sp", kT, dr, kT[:], FM[ti, :, 2 + 2 * d:4 + 2 * d, :])
                            dma("sp", kt, dr, kt[:], TB[r0:r0 + 128, d * 256:(d + 1) * 256])
                            dma("sp", vt, dr, vt[:], TB[r0:r0 + 128, 512:768])
                            pb = pbT.next()
                            for pt in range(2):
                                op("pe", [lf, CST], [pb], lambda e: e.matmul(pb[:, pt, :], lhsT=lf[:, pt * 128:(pt + 1) * 128], rhs=TRI[d], start=True, stop=True))
                            bT = R_["bT"].next()
                            op("act", [pb], [bT], lambda e: e.copy(out=bT[:], in_=pb[:]))
                            pr = prs.next()
                            op("pe", [lf, CST], [pr], lambda e: e.matmul(pr[:], lhsT=TRIS[d], rhs=lf[:], start=True, stop=True))
                            edk = R_["edk"].next()
                            op("act", [pr], [edk], lambda e: e.activation(out=edk[:], in_=pr[:], func=AF.Exp))
                            kd = R_["kd"].next()
                            op("pool", [edk, kt], [kd], lambda e: e.tensor_tensor(out=kd[:], in0=edk[:], in1=kt[:], op=ALU.mult))
                            E1 = R_["E1"].next()
                            op("act", [bT], [E1], lambda e: e.activation(out=E1[:], in_=bT[:], func=AF.Exp))
                            Qf = R_["Qf"].next()
                            op("pool", [E1, qT], [Qf], lambda e: e.tensor_tensor(out=Qf[:], in0=E1[:], in1=qT[:], op=ALU.mult))
                            rr = R_["rr"].next()
                            if d == 0:
                                src = bT.t[:, :, 31:127:32]; dn = rr.t[:, :, 1:4]; dp = rr.t[:, :, 5:8]
                            else:
                                src = bT.t[:, :, 32:128:32]; dn = rr.t[:, :, 0:3]; dp = rr.t[:, :, 4:7]
                            op("dve", [bT], [rr], lambda e: e.tensor_scalar_mul(out=dn, in0=src, scalar1=-1.0))
                            op("dve", [bT], [rr], lambda e: e.tensor_copy(out=dp, in_=src))
                            tmp = R_["tmp"].next()
                            op("dve", [bT, rr], [tmp], lambda e: e.tensor_tensor(
                                out=tmp[:].rearrange("p a (i j) -> p a i j", i=4), in0=bT[:].rearrange("p a (i j) -> p a i j", i=4),
                                in1=rr.t[:, :, 0:4].unsqueeze(3).to_broadcast([128, 2, 4, 32]), op=ALU.add))
                            Es = R_["Es"].next()
                            op("act", [tmp], [Es], lambda e: e.activation(out=Es[:], in_=tmp[:], func=AF.Exp))
                            Qs = R_["Qs"].next()
                            for hl in range(2):
                                op("dve", [Es, qT], [Qs], lambda e: e.tensor_tensor(
                                    out=Qs[64 * hl:64 * hl + 64, hl, :, :], in0=Es[64 * hl:64 * hl + 64, :, :], in1=qT[64 * hl:64 * hl + 64, :, :], op=ALU.mult))
                            Kvs = []
                            for i in range(4):
                                lo, hi = (0, 32 * (i + 1)) if d == 0 else (32 * i, 128)
                                Ek = R_["Ek"].next()
                                for pt in range(2):
                                    op("act", [bT, rr], [Ek], lambda e: e.activation(
                                        out=Ek[:, pt, lo:hi], in_=bT[:, pt, lo:hi], func=AF.Exp, bias=rr[:, pt, 4 + i:5 + i], scale=-1.0))
                                Kv = R_["Kv"][i].next()
                                op("dve", [Ek, kT], [Kv], lambda e: e.tensor_tensor(out=Kv[:, :, lo:hi], in0=Ek[:, :, lo:hi], in1=kT[:, :, lo:hi], op=ALU.mult))
                                Kvs.append(Kv)
                            pa = pAT.next()
                            for h in range(4):
                                pt = h // 2
                                for i in range(4):
                                    Kv = Kvs[i]
                                    op("pe", [Kv, Qs], [pa], lambda e: e.matmul(
                                        pa[:, h, 32 * i:32 * i + 32], lhsT=Kv[:, pt, :], rhs=Qs[:, h % 2, pt, 32 * i:32 * i + 32],
                                        start=True, stop=True))
                            ATm = R_["ATm"].next()
                            op("dve", [pa, CB], [ATm], lambda e: e.tensor_tensor(
                                out=ATm[:], in0=pa[:], in1=MASK[d].unsqueeze(1).to_broadcast([128, 4, 128]), op=ALU.mult))
                            return dict(ti=ti, d=d, vt=vt, kd=kd, E1=E1, Qf=Qf, ATm=ATm)

                        def a_back(c):
                            ti, d, vt, kd, E1, Qf, ATm = c["ti"], c["d"], c["vt"], c["kd"], c["E1"], c["Qf"], c["ATm"]
                            R_ = St[d]
                            S, Sb = R_["S"], R_["Sb"]
                            r0 = ti * 128
                            po = pO.next()
                            for pt in range(2):
                                op("pe", [Qf, Sb], [po], lambda e: e.matmul(po[:, 128 * pt:128 * pt + 128], lhsT=Qf[:, pt, :], rhs=Sb[:, pt, :],
                                                                            start=True, stop=False, skip_group_check=True))
                                for hl in range(2):
                                    h = 2 * pt + hl
                                    op("pe", [ATm, vt], [po], lambda e: e.matmul(po[:, 64 * h:64 * h + 64], lhsT=ATm[:, h, :], rhs=vt[:, 64 * h:64 * h + 64],
                                                                                start=False, stop=(hl == 1), skip_group_check=True))
                            pu = pU.next()
                            for pt in range(2):
                                op("pe", [kd, vt], [pu], lambda e: e.matmul(pu[:, pt, :], lhsT=kd[:, pt * 128:(pt + 1) * 128], rhs=vt[:, pt * 128:(pt + 1) * 128],
                                                                           start=True, stop=True))
                            col = 127 if d == 0 else 0
                            for pt in range(2):
                                for hl in range(2):
                                    bs = 64 * hl
                                    op("dve", [S, E1, pu], [S], lambda e: e.scalar_tensor_tensor(
                                        out=S[bs:bs + 64, pt, :], in0=S[bs:bs + 64, pt, :], scalar=E1[bs:bs + 64, pt, col:col + 1],
                                        in1=pu[bs:bs + 64, pt, bs:bs + 64], op0=ALU.mult, op1=ALU.add))
                            for hl in range(2):
                                bs = 64 * hl
                                op("act", [S], [Sb], lambda e: e.copy(out=Sb[bs:bs + 64, :, bs:bs + 64], in_=S[bs:bs + 64, :, :]))
                            ost = R_["ost"].next()
                            op("act", [po], [ost], lambda e: e.copy(out=ost[:], in_=po[:]))
                            dma("pool", dr, ost, OA[d][r0:r0 + 128, :], ost[:], sem_buf=ost)

                        seq = []
                        for j in range(NT):
                            seq += [(j, 0), (BWD_ORDER[j], 1)]
                        nxt = a_front(*seq[0])
                        for k in range(len(seq)):
                            cur = nxt
                            if k + 1 < len(seq):
                                nxt = a_front(*seq[k + 1])
                            a_back(cur)
                        fw.barrier()
                        for R_ in St:
                            for k_ in ("lf", "qT", "kT", "kt", "vt", "ost"):
                                fw.release(R_[k_].bufs)

                if "B" in phases:
                    with ExitStack() as bs_:
                        def ring(name, shape, dt, n=2, ps=False):
                            return Ring([(fw.ps if ps else fw.sb)(bs_, "%s%d" % (name, i), shape, dt) for i in range(n)])
                        St = []
                        for d in range(2):
                            C = fw.sb(bs_, "C%d" % d, [128, 4, 97], F32)
                            Cb = fw.sb(bs_, "Cb%d" % d, [128, 4, 97], BF16)
                            op("dve", [], [C], lambda e: e.memset(C[:], 0.0))
                            op("dve", [], [Cb], lambda e: e.memset(Cb[:], 0.0))
                            R_ = dict(C=C, Cb=Cb, g=ring("bg%d" % d, [128, 16], F32), qT=ring("bqT%d" % d, [128, 4, 128], BF16),
                                      kT=ring("bkT%d" % d, [128, 4, 128], BF16), kt=ring("bkt%d" % d, [128, 384], BF16),
                                      vx=ring("bvx%d" % d, [128, 4, 97], BF16), X=ring("bX%d" % d, [128, 16], F32),
                                      E=ring("bE%d" % d, [128, 16], F32), kh=ring("bkh%d" % d, [128, 384], BF16),
                                      ST=ring("bST%d" % d, [128, 4, 128], BF16), u=ring("bu%d" % d, [128, 4, 97], F32),
                                      dd=ring("bdd%d" % d, [128, 8], F32), ost=ring("bos%d" % d, [128, 384], F32))
                            for b in R_["vx"].bufs:
                                op("dve", [], [b], lambda e: e.memset(b[:], 1.0))
                            St.append(R_)
                        pg = ring("bpg", [128, 16], F32, 2, ps=True)
                        psc = ring("bpsc", [128, 4, 128], F32, 2, ps=True)
                        pout = ring("bpout", [128, 4, 128], F32, 2, ps=True)
                        pdu = ring("bpdu", [128, 4, 128], F32, 1, ps=True)

                        def b_front(ti, d):
                            R_ = St[d]
                            g = R_["g"].next(); qT = R_["qT"].next(); kT = R_["kT"].next(); kt = R_["kt"].next(); vx = R_["vx"].next()
                            r0 = ti * 128
                            dma("sp", g, dr, g[:], TF[r0:r0 + 128, 512:528])
                            dma("sp", qT, dr, qT[:], FM[ti, :, 12:16, :])
                            dma("sp", kT, dr, kT[:], FM[ti, :, 16:20, :])
                            dma("sp", kt, dr, kt[:], TB[r0:r0 + 128, 1024:1408])
                            dma("sp", vx, dr, vx[:, :, 0:96], TB[r0:r0 + 128, 1408:1792].rearrange("p (h d) -> p h d", h=4))
                            ig = g.t[:, 4 * d:4 * d + 4]
                            lfd = g.t[:, 8 + 4 * d:12 + 4 * d]
                            p_ = pg.next()
                            op("pe", [g, CST], [p_], lambda e: e.matmul(p_[:, 0:4], lhsT=TRI[d], rhs=lfd, start=True, stop=True))
                            op("pe", [g, CST], [p_], lambda e: e.matmul(p_[:, 4:8], lhsT=TRIS[d], rhs=lfd, start=True, stop=True))
                            op("pe", [g, CST], [p_], lambda e: e.matmul(p_[:, 8:12], lhsT=ONESF, rhs=lfd, start=True, stop=True))
                            X = R_["X"].next()
                            op("dve", [g, p_], [X], lambda e: e.tensor_tensor(out=X[:, 0:4], in0=ig, in1=p_[:, 0:4], op=ALU.subtract))
                            op("dve", [g, p_], [X], lambda e: e.tensor_tensor(out=X[:, 4:8], in0=ig, in1=p_[:, 4:8], op=ALU.add))
                            op("dve", [p_], [X], lambda e: e.tensor_copy(out=X[:, 8:12], in_=p_[:, 0:4]))
                            op("dve", [p_], [X], lambda e: e.tensor_copy(out=X[:, 12:16], in_=p_[:, 8:12]))
                            E = R_["E"].next()
                            op("act", [X], [E], lambda e: e.activation(out=E[:], in_=X[:], func=AF.Exp))
                            kh = R_["kh"].next()
                            op("pool", [kt, E], [kh], lambda e: e.tensor_tensor(
                                out=kh[:].rearrange("p (h d) -> p h d", h=4), in0=kt[:].rearrange("p (h d) -> p h d", h=4),
                                in1=E.t[:, 4:8].unsqueeze(2).to_broadcast([128, 4, 96]), op=ALU.mult))
                            sc = psc.next()
                            for h in range(4):
                                op("pe", [kT, qT], [sc], lambda e: e.matmul(sc[:, h, :], lhsT=kT[0:96, h, :], rhs=qT[0:96, h, :], start=True, stop=True))
                            ST = R_["ST"].next()
                            for h in range(4):
                                op("dve", [sc, E, CB], [ST], lambda e: e.scalar_tensor_tensor(
                                    out=ST[:, h, :], in0=sc[:, h, :], scalar=E[:, h:h + 1], in1=MASK[d], op0=ALU.mult, op1=ALU.mult))
                            return dict(ti=ti, d=d, qT=qT, vx=vx, E=E, kh=kh, ST=ST)

                        def b_back(c):
                            ti, d, qT, vx, E, kh, ST = c["ti"], c["d"], c["qT"], c["vx"], c["E"], c["kh"], c["ST"]
                            R_ = St[d]
                            C, Cb = R_["C"], R_["Cb"]
                            r0 = ti * 128
                            po = pout.next()
                            for h in range(4):
                                op("pe", [ST, vx], [po], lambda e: e.matmul(po[:, h, 0:97], lhsT=ST[:, h, :], rhs=vx[:, h, :], start=True, stop=False))
                                op("pe", [qT, Cb], [po], lambda e: e.matmul(po[:, h, 0:97], lhsT=qT[0:96, h, :], rhs=Cb[0:96, h, :], start=False, stop=True))
                            pd = pdu.next()
                            for h in range(4):
                                op("pe", [kh, vx], [pd], lambda e: e.matmul(pd[0:96, h, 0:97], lhsT=kh[:, 96 * h:96 * h + 96], rhs=vx[:, h, :], start=True, stop=True))
                            op("dve", [C, E], [C], lambda e: e.tensor_tensor(
                                out=C[0:96, :, :], in0=C[0:96, :, :], in1=E.t[0:96, 12:16].unsqueeze(2).to_broadcast([96, 4, 97]), op=ALU.mult))
                            op("dve", [C, pd], [C], lambda e: e.tensor_tensor(out=C[0:96, :, :], in0=C[0:96, :, :], in1=pd[0:96, :, 0:97], op=ALU.add))
                            op("act", [C], [Cb], lambda e: e.copy(out=Cb[0:96, :, :], in_=C[0:96, :, :]))
                            u = R_["u"].next()
                            op("dve", [po, E], [u], lambda e: e.tensor_tensor(
                                out=u[:], in0=po[:, :, 0:97], in1=E.t[:, 8:12].unsqueeze(2).to_broadcast([128, 4, 97]), op=ALU.mult))
                            dd = R_["dd"].next()
                            op("dve", [u], [dd], lambda e: e.tensor_scalar_max(out=dd[:, 0:4], in0=u[:, :, 96], scalar1=1.0))
                            op("dve", [u, dd], [dd], lambda e: e.scalar_tensor_tensor(out=dd[:, 4:8], in0=u[:, :, 96], scalar=-1.0, in1=dd[:, 0:4],
                                                                                     op0=ALU.mult, op1=ALU.max))
                            op("dve", [dd], [dd], lambda e: e.reciprocal(out=dd[:, 0:4], in_=dd[:, 4:8]))
                            ost = R_["ost"].next()
                            op("pool", [u, dd], [ost], lambda e: e.tensor_tensor(
                                out=ost[:].rearrange("p (h d) -> p h d", h=4), in0=u[:, :, 0:96],
                                in1=dd.t[:, 0:4].unsqueeze(2).to_broadcast([128, 4, 96]), op=ALU.mult))
                            dma("pool", dr, ost, OB[d][r0:r0 + 128, :], ost[:], sem_buf=ost)

                        seq = []
                        for j in range(NT):
                            seq += [(j, 0), (BWD_ORDER[j], 1)]
                        nxt = b_front(*seq[0])
                        for k in range(len(seq)):
                            cur = nxt
                            if k + 1 < len(seq):
                                nxt = b_front(*seq[k + 1])
                            b_back(cur)
                        fw.barrier()
                        for R_ in St:
                            for k_ in ("g", "qT", "kT", "kt", "vx", "ost"):
                                fw.release(R_[k_].bufs)

                if "C" in phases:
                    with ExitStack() as cs_:
                        KT = fw.sb(cs_, "cKT", [128, NT, 3, 128], BF16)
                        VX = fw.sb(cs_, "cVX", [128, NT, 6, 65], BF16)
                        BIAS = fw.sb(cs_, "cBias", [128, 54, 128], BF16)
                        op("dve", [], [VX], lambda e: e.memset(VX[:], 1.0))
                        bst = Ring([fw.sb(cs_, "cbst%d" % i, [128, 6, 128], F32) for i in range(2)])
                        for g_ in range(9):
                            b = bst.next()
                            dma("sp", b, dr, b[:], rpbt[l, g_ * 6:(g_ + 1) * 6].rearrange("k s t -> s k t"))
                            op("dve", [b], [BIAS], lambda e: e.tensor_copy(out=BIAS[:, g_ * 6:(g_ + 1) * 6, :], in_=b[:]))
                        for ti in range(NT):
                            dma("sp", KT, dr, KT[:, ti, :, :], FM[ti, :, 9:12, :])
                        for ti in range(NT):
                            dma("sp", VX, dr, VX[:, ti, :, 0:64],
                                TB[ti * 128:(ti + 1) * 128, 2176:2560].rearrange("p (h d) -> p h d", h=6))
                        qr_ = Ring([fw.sb(cs_, "cq%d" % i, [128, 3, 128], BF16) for i in range(2)])
                        qzr = Ring([fw.sb(cs_, "cqz%d" % i, [128, 6, 128], BF16) for i in range(2)])
                        for b in qzr.bufs:
                            op("dve", [], [b], lambda e: e.memset(b[:], 0.0))
                        PTr = Ring([fw.sb(cs_, "cPT%d" % i, [128, 7, 128], BF16) for i in range(3)])
                        osr = Ring([fw.sb(cs_, "cos%d" % i, [128, 384], F32) for i in range(2)])
                        rrr = Ring([fw.sb(cs_, "crr%d" % i, [128, 6], F32) for i in range(2)])
                        psct = Ring([fw.ps(cs_, "cps%d" % i, [128, 8, 128], F32) for i in range(2)])
                        pcout = Ring([fw.ps(cs_, "cpo%d" % i, [128, 6, 65], F32) for i in range(2)])
                        tiles = list(range(NT)) if not last else list(range(NCTX, NT))
                        TS = {}

                        def c_keys(ti):
                            if ti < NCTX:
                                return [(0, None), (1, None)]
                            return [(NCTX + kt, kind) for kt, kind in _key_tiles(ti - NCTX)] + [(0, None), (1, None)]

                        def c_scores(ti, h):
                            if h == 0:
                                q = qr_.next()
                                dma("sp", q, dr, q[:], FM[ti, :, 6:9, :])
                                qz = qzr.next()
                                op("pool", [q], [qz], lambda e: e.tensor_copy(out=qz[0:64, 0:6:2, :], in_=q[0:64, :, :]))
                                op("pool", [q], [qz], lambda e: e.tensor_copy(out=qz[64:128, 1:6:2, :], in_=q[64:128, :, :]))
                                TS[ti] = dict(qz=qz, po=pcout.next())
                            qz = TS[ti]["qz"]
                            keys = c_keys(ti)
                            nk = len(keys)
                            pt = h // 2
                            sc = psct.next()
                            for j, (kt, kind) in enumerate(keys):
                                op("pe", [KT, qz], [sc], lambda e: e.matmul(sc[:, j, :], lhsT=KT[:, kt, pt, :], rhs=qz[:, h, :],
                                                                           start=True, stop=(kind is None)))
                                if kind is not None:
                                    op("pe", [CB, BIAS], [sc], lambda e: e.matmul(sc[:, j, :], lhsT=IDENT, rhs=BIAS[:, h * 9 + kind, :], start=False, stop=True))
                            PT = PTr.next()
                            op("act", [sc], [PT], lambda e: e.activation(out=PT[:, 0:nk, :], in_=sc[:, 0:nk, :], func=AF.Exp))
                            return (ti, h, PT)

                        def c_pv(c):
                            ti, h, PT = c
                            keys = c_keys(ti)
                            nk = len(keys)
                            po = TS[ti]["po"]
                            for j, (kt, kind) in enumerate(keys):
                                op("pe", [PT, VX], [po], lambda e: e.matmul(po[:, h, :], lhsT=PT[:, j, :], rhs=VX[:, kt, h, :], start=(j == 0), stop=(j == nk - 1)))
                            if h == 5:
                                rr = rrr.next()
                                op("dve", [po], [rr], lambda e: e.reciprocal(out=rr[:], in_=po[:, :, 64]))
                                ost = osr.next()
                                op("dve", [po, rr], [ost], lambda e: e.tensor_tensor(
                                    out=ost[:].rearrange("p (h d) -> p h d", h=6), in0=po[:, :, 0:64],
                                    in1=rr[:].unsqueeze(2).to_broadcast([128, 6, 64]), op=ALU.mult))
                                dma("pool", dr, ost, OC[ti * 128:(ti + 1) * 128, :], ost[:], sem_buf=ost)
                                del TS[ti]

                        seq = [(ti, h) for ti in tiles for h in range(6)]
                        nxt = c_scores(*seq[0])
                        for k in range(len(seq)):
                            cur = nxt
                            if k + 1 < len(seq):
                                nxt = c_scores(*seq[k + 1])
                            c_pv(cur)
                        fw.barrier()
                        fw.release([KT, VX] + bst.bufs + qr_.bufs + osr.bufs)

                if "O" in phases:
                    with ExitStack() as os_:
                        WO = fw.sb(os_, "wo", [128, 8, D], BF16)
                        wst = Ring([fw.sb(os_, "wost%d" % i, [128, D], F32) for i in range(2)])
                        for kc in range(8):
                            w = wst.next()
                            dma("sp", w, dr, w[:], w_out[l, kc * 128:(kc + 1) * 128, :])
                            op("dve", [w], [WO], lambda e: e.tensor_copy(out=WO[:, kc, :], in_=w[:]))

                        def ring(name, shape, dt, n=2, ps=False):
                            return Ring([(fw.ps if ps else fw.sb)(os_, "%s%d" % (name, i), shape, dt) for i in range(n)])
                        oa = [ring("ooa%d" % d, [128, 256], F32) for d in range(2)]
                        ob = [ring("oob%d" % d, [128, 384], F32) for d in range(2)]
                        oc = ring("ooc", [128, 384], F32)
                        gz = ring("ogz", [128, 3, 384], BF16)
                        xr = ring("ox", [128, D], F32)
                        xo = ring("oxo", [128, D], F32)
                        Y = ring("oY", [128, D], BF16)
                        YT = ring("oYT", [128, 8, 128], BF16)
                        sAr = ring("osA", [128, 256], F32)
                        sBr = ring("osB", [128, 384], F32)
                        tAr = ring("otA", [128, 256], F32)
                        tBr = ring("otB", [128, 384], F32)
                        st8r = ring("ost8", [128, 16], F32)
                        stAr = ring("ostA", [128, 16], F32)
                        stBr = ring("ostB", [128, 16], F32)
                        bigr = ring("obig", [128, D], F32)
                        pyt = ring("opyt", [128, 8, 128], BF16, 2, ps=True)
                        pu = ring("opu", [128, 2, 512], F32, 2, ps=True)
                        tiles = list(range(NT)) if not last else list(range(NCTX, NT))
                        for ti in tiles:
                            st = 1 if ti < NCTX else 0
                            m = MODS[st]
                            r0 = ti * 128
                            a0, a1 = oa[0].next(), oa[1].next()
                            b0, b1 = ob[0].next(), ob[1].next()
                            c_ = oc.next(); g_ = gz.next(); xt = xr.next()
                            dma("sp", a0, dr, a0[:], OA[0][r0:r0 + 128, :]); dma("sp", a1, dr, a1[:], OA[1][r0:r0 + 128, :])
                            dma("sp", b0, dr, b0[:], OB[0][r0:r0 + 128, :]); dma("sp", b1, dr, b1[:], OB[1][r0:r0 + 128, :])
                            dma("sp", c_, dr, c_[:], OC[r0:r0 + 128, :])
                            dma("sp", g_, dr, g_[:, 0, 0:256], TB[r0:r0 + 128, 768:1024])
                            dma("sp", g_, dr, g_[:, 1, :], TB[r0:r0 + 128, 1792:2176])
                            dma("sp", g_, dr, g_[:, 2, :], TB[r0:r0 + 128, 2560:2944])
                            dma("sp", xt, dr, xt[:], x_src(l, ti))
                            y = Y.next()
                            sA = sAr.next(); sB = sBr.next(); tA = tAr.next(); tB = tBr.next(); st8 = st8r.next(); big = bigr.next(); stA = stAr.next(); stB = stBr.next()
                            op("pool", [a0, a1], [sA], lambda e: e.tensor_tensor(out=sA[:], in0=a0[:], in1=a1[:], op=ALU.add))
                            op("pool", [sA], [tA], lambda e: e.tensor_tensor(out=tA[:, 0:256], in0=sA[:], in1=sA[:], op=ALU.mult))
                            op("dve", [tA], [stA], lambda e: e.tensor_reduce(out=stA[:, 0:4], in_=tA[:, 0:256].rearrange("p (h d) -> p h d", h=4),
                                                                            axis=AX.X, op=ALU.add))
                            op("pool", [stA], [stA], lambda e: e.tensor_scalar(out=stA[:, 0:4], in0=stA[:, 0:4], scalar1=1.0 / 64, scalar2=64.0 * EPS,
                                                                              op0=ALU.mult, op1=ALU.add))
                            op("act", [stA], [stA], lambda e: e.activation(out=stA[:, 0:4], in_=stA[:, 0:4], func=AF.Ln))
                            op("act", [stA], [stA], lambda e: e.activation(out=stA[:, 0:4], in_=stA[:, 0:4], func=AF.Exp, scale=-0.5))
                            op("pool", [sA, stA], [sA], lambda e: e.tensor_tensor(
                                out=sA[:].rearrange("p (h d) -> p h d", h=4), in0=sA[:].rearrange("p (h d) -> p h d", h=4),
                                in1=stA.t[:, 0:4].unsqueeze(2).to_broadcast([128, 4, 64]), op=ALU.mult))
                            op("pool", [sA, GNA], [sA], lambda e: e.tensor_tensor(out=sA[:], in0=sA[:], in1=GNA[:], op=ALU.mult))
                            op("pool", [sA, g_], [y], lambda e: e.tensor_tensor(out=y[:, 0:256], in0=sA[:], in1=g_[:, 0, 0:256], op=ALU.mult))
                            op("dve", [b0, b1], [sB], lambda e: e.tensor_tensor(out=sB[:], in0=b0[:], in1=b1[:], op=ALU.add))
                            op("dve", [sB], [tB], lambda e: e.tensor_tensor(out=tB[:], in0=sB[:], in1=sB[:], op=ALU.mult))
                            op("dve", [tB], [stB], lambda e: e.tensor_reduce(out=stB[:, 4:8], in_=tB[:].rearrange("p (h d) -> p h d", h=4),
                                                                            axis=AX.X, op=ALU.add))
                            op("dve", [stB], [stB], lambda e: e.tensor_scalar(out=stB[:, 4:8], in0=stB[:, 4:8], scalar1=1.0 / 96, scalar2=EPS,
                                                                              op0=ALU.mult, op1=ALU.add))
                            op("act", [stB], [stB], lambda e: e.activation(out=stB[:, 4:8], in_=stB[:, 4:8], func=AF.Ln))
                            op("act", [stB], [stB], lambda e: e.activation(out=stB[:, 4:8], in_=stB[:, 4:8], func=AF.Exp, scale=-0.5))
                            op("dve", [sB, stB], [sB], lambda e: e.tensor_tensor(
                                out=sB[:].rearrange("p (h d) -> p h d", h=4), in0=sB[:].rearrange("p (h d) -> p h d", h=4),
                                in1=stB.t[:, 4:8].unsqueeze(2).to_broadcast([128, 4, 96]), op=ALU.mult))
                            op("dve", [sB, GNB], [sB], lambda e: e.tensor_tensor(out=sB[:], in0=sB[:], in1=GNB[:], op=ALU.mult))
                            op("dve", [sB, g_], [y], lambda e: e.tensor_tensor(out=y[:, 256:640], in0=sB[:], in1=g_[:, 1, :], op=ALU.mult))
                            op("dve", [c_, g_], [y], lambda e: e.tensor_tensor(out=y[:, 640:1024], in0=c_[:], in1=g_[:, 2, :], op=ALU.mult))
                            py = pyt.next()
                            for kc in range(8):
                                op("pe", [y, CB], [py], lambda e: e.transpose(out=py[:, kc, :], in_=y[:, kc * 128:(kc + 1) * 128], identity=IDENT))
                            yT = YT.next()
                            op("act", [py], [yT], lambda e: e.copy(out=yT[:], in_=py[:]))
                            u = pu.next()
                            for nb in range(2):
                                for kc in range(8):
                                    op("pe", [yT, WO], [u], lambda e: e.matmul(u[:, nb, :], lhsT=yT[:, kc, :], rhs=WO[:, kc, nb * 512:(nb + 1) * 512],
                                                                              start=(kc == 0), stop=(kc == 7)))
                            for nb in range(2):
                                op("act", [u], [big, st8], lambda e: e.activation(out=big[:, nb * 512:(nb + 1) * 512], in_=u[:, nb, :], func=AF.Square,
                                                                                accum_out=st8[:, 8 + nb:9 + nb]))
                            op("dve", [st8], [st8], lambda e: e.tensor_tensor(out=st8[:, 10:11], in0=st8[:, 8:9], in1=st8[:, 9:10], op=ALU.add))
                            op("dve", [st8], [st8], lambda e: e.tensor_scalar(out=st8[:, 10:11], in0=st8[:, 10:11], scalar1=1.0 / D, scalar2=EPS,
                                                                              op0=ALU.mult, op1=ALU.add))
                            op("act", [st8], [st8], lambda e: e.activation(out=st8[:, 10:11], in_=st8[:, 10:11], func=AF.Ln))
                            op("act", [st8], [st8], lambda e: e.activation(out=st8[:, 10:11], in_=st8[:, 10:11], func=AF.Exp, scale=-0.5))
                            op("dve", [u, st8, m], [big], lambda e: e.scalar_tensor_tensor(
                                out=big[:].rearrange("p (a b) -> p a b", a=2), in0=u[:], scalar=st8[:, 10:11],
                                in1=m[:, 2 * D:3 * D].rearrange("p (a b) -> p a b", a=2), op0=ALU.mult, op1=ALU.mult))
                            xo_ = xo.next()
                            op("dve", [big, xt], [xo_], lambda e: e.tensor_tensor(out=xo_[:], in0=big[:], in1=xt[:], op=ALU.add))
                            if last:
                                dma("pool", dr, xo_, y_out[(ti - NCTX) * 128:(ti - NCTX + 1) * 128, :], xo_[:], sem_buf=xo_)
                            else:
                                dma("pool", dr, xo_, XC[r0:r0 + 128, :], xo_[:], sem_buf=xo_)
                        fw.barrier()
                        fw.release(wst.bufs + oa[0].bufs + oa[1].bufs + ob[0].bufs + ob[1].bufs + oc.bufs + gz.bufs + xr.bufs + xo.bufs)
                fw.barrier()
                fw.release([LB, OML, GNA, GNB, GB])
        fw.barrier()
    return nc


def make_in_maps(inputs, cores):
    x = np.asarray(inputs["x"], np.float32)
    c = np.asarray(inputs["c"], np.float32)
    ctx = np.asarray(inputs["ctx"], np.float32)
    c_ctx = np.asarray(inputs["c_ctx"], np.float32)
    shared = {
        "w_mod": np.ascontiguousarray(inputs["w_mod"], np.float32),
        "b_mod": np.ascontiguousarray(inputs["b_mod"], np.float32),
        "g_pre": np.ascontiguousarray(inputs["g_pre"], np.float32),
        "g_post": np.ascontiguousarray(inputs["g_post"], np.float32),
        "w_in": np.ascontiguousarray(inputs["w_in"], np.float32),
        "w_out": np.ascontiguousarray(inputs["w_out"], np.float32),
        "hgrn_lb": np.ascontiguousarray(np.asarray(inputs["hgrn_lb"], np.float32).reshape(DEPTH, 512)),
        "hgrn_gn": np.ascontiguousarray(inputs["hgrn_gn"], np.float32),
        "gate_b": np.ascontiguousarray(np.asarray(inputs["mlstm_gate_b"], np.float32).reshape(DEPTH, 16)),
        "mlstm_gn": np.ascontiguousarray(inputs["mlstm_gn"], np.float32),
        "rpbt": _rpb_tables(np.asarray(inputs["na_rpb"], np.float32)),
        "rope": _rope_tables().reshape(NT, 128, 384),
        "consts": _consts(),
    }
    maps = []
    for b in cores:
        cT = np.concatenate([c[b].reshape(8, 128).T, c_ctx.reshape(8, 128).T], axis=1)
        m = dict(shared)
        m["x"] = np.ascontiguousarray(x[b])
        m["ctx"] = np.ascontiguousarray(ctx[b])
        m["cT"] = np.ascontiguousarray(cT, np.float32)
        maps.append(m)
    return maps


def kernel(**inputs):
    nc = build()
    maps = make_in_maps(inputs, list(range(8)))
    res = run_bass_kernel_spmd(nc, maps, core_ids=list(range(8)))
    out = np.stack([np.asarray(r["y"], np.float32) for r in res.results], axis=0)
    return out
```

```python
import os
import numpy as np
from contextlib import ExitStack
import concourse.bass as bass
import concourse.mybir as mybir
from concourse.bass_utils import run_bass_kernel_spmd

F32 = mybir.dt.float32
BF16 = mybir.dt.bfloat16
AF = mybir.ActivationFunctionType
ALU = mybir.AluOpType
AX = mybir.AxisListType

D = 1024
NT = 34
NCTX = 2
TOK = NT * 128
PIN = 4752
DEPTH = 2
EPS = 1e-6
NTB = 2944
NTBS = 4352
NTF = 528
NFM = 20
NEG = -30000.0
BWD_ORDER = [1, 0] + list(range(NT - 1, 1, -1))

SEM_CHUNK = 24000


class Buf:
    def __init__(self, t=None, name=""):
        self.t = t
        self.name = name
        self.w = None
        self.r = {}
        self.dsem = None

    def __getitem__(self, k):
        return self.t[k]


class DramTrack:
    def __init__(self):
        self.d = {}

    def get(self, name):
        if name not in self.d:
            self.d[name] = Buf(None, name)
        return self.d[name]


class DmaSem:
    def __init__(self, sem):
        self.sem = sem
        self.n = 0


class FW:
    def __init__(self, nc, es):
        self.nc = nc
        self.es = es
        self.E = {"pe": nc.tensor, "act": nc.scalar, "dve": nc.vector, "pool": nc.gpsimd, "sp": nc.sync}
        self.cnt = {e: 0 for e in self.E}
        self.sems = {e: [] for e in self.E}
        self.seen = {e: {} for e in self.E}
        self.dsems = []
        self.free_dsems = []
        self.bar_sem = es.enter_context(nc.semaphore("barsem"))
        self.bar_n = 0
        self.uid = 0

    def sb(self, es, name, shape, dt):
        self.uid += 1
        t = es.enter_context(self.nc.sbuf_tensor("%s_%d" % (name, self.uid), list(shape), dt))
        return Buf(t, name)

    def ps(self, es, name, shape, dt):
        self.uid += 1
        t = es.enter_context(self.nc.psum_tensor("%s_%d" % (name, self.uid), list(shape), dt))
        return Buf(t, name)

    def get_dsem(self):
        if self.free_dsems:
            return self.free_dsems.pop()
        s = self.es.enter_context(self.nc.semaphore("dsem%d" % len(self.dsems)))
        d = DmaSem(s)
        self.dsems.append(d)
        return d

    def release(self, bufs):
        for b in bufs:
            if b.dsem is not None:
                self.free_dsems.append(b.dsem)
                b.dsem = None

    def _esem(self, e, n):
        k = (n - 1) // SEM_CHUNK
        while len(self.sems[e]) <= k:
            self.sems[e].append(self.es.enter_context(self.nc.semaphore("es_%s_%d" % (e, len(self.sems[e])))))
        return self.sems[e][k], (n - 1) % SEM_CHUNK + 1

    def _wait_tok(self, e, tok):
        key, n = tok
        if key == e and e == "pe":
            return
        if self.seen[e].get(key, 0) >= n:
            return
        self.seen[e][key] = n
        eng = self.E[e]
        if isinstance(key, str):
            sem, val = self._esem(key, n)
            eng.wait_ge(sem, val)
        else:
            eng.wait_ge(key.sem, n)

    def _deps(self, e, reads, writes):
        for b in reads:
            if b.w is not None:
                self._wait_tok(e, b.w)
        for b in writes:
            if b.w is not None:
                self._wait_tok(e, b.w)
            for key, n in list(b.r.items()):
                self._wait_tok(e, (key, n))

    def _mark(self, tok, reads, writes):
        key, n = tok
        for b in writes:
            b.w = tok
            b.r = {}
        for b in reads:
            if b in writes:
                continue
            if b.r.get(key, 0) < n:
                b.r[key] = n

    def op(self, e, reads, writes, fn):
        self._deps(e, reads, writes)
        ins = fn(self.E[e])
        self.cnt[e] += 1
        n = self.cnt[e]
        sem, _ = self._esem(e, n)
        ins.then_inc(sem, 1)
        self._mark((e, n), reads, writes)
        return ins

    def dma(self, q, out_buf, in_buf, out_ap, in_ap, sem_buf=None, **kw):
        if isinstance(out_buf, DramTrack):
            out_buf = out_buf.get(out_ap.name)
        if isinstance(in_buf, DramTrack):
            in_buf = in_buf.get(in_ap.name)
        sb_ = sem_buf if sem_buf is not None else out_buf
        if sb_.dsem is None:
            sb_.dsem = self.get_dsem()
        ds = sb_.dsem
        self._deps(q, [in_buf], [out_buf])
        ins = self.E[q].dma_start(out=out_ap, in_=in_ap, **kw)
        ds.n += 16
        ins.then_inc(ds.sem, 16)
        self._mark((ds, ds.n), [in_buf], [out_buf])
        return ins

    def barrier(self):
        sp = self.E["sp"]
        for e in self.E:
            if e != "sp" and self.cnt[e] > 0:
                self._wait_tok("sp", (e, self.cnt[e]))
        for d in self.dsems:
            if d.n > 0:
                self._wait_tok("sp", (d, d.n))
        self.bar_n += 1
        sp.sem_inc(self.bar_sem, 1)
        for e in self.E:
            if e != "sp":
                self.E[e].wait_ge(self.bar_sem, self.bar_n)
        for e in self.E:
            for e2 in self.E:
                if e2 != e:
                    self.seen[e][e2] = self.cnt[e2]
            for d in self.dsems:
                self.seen[e][d] = d.n


def run_skewed(stages, items, cap=None, name=""):
    n = len(stages)
    if cap is None:
        cap = int(os.environ.get("SKEW_CAP_" + name, os.environ.get("SKEW_CAP", n - 1)))
    cap = min(cap, n - 1)
    for t in range(len(items) + cap):
        for off in range(cap, -1, -1):
            k = t - off
            if 0 <= k < len(items):
                for si in range(n):
                    if min(si, cap) == off:
                        stages[si](items[k])


class Ring:
    def __init__(self, bufs):
        self.bufs = bufs
        self.i = 0

    def next(self):
        b = self.bufs[self.i % len(self.bufs)]
        self.i += 1
        return b


def _consts():
    s = np.arange(128)[:, None]
    t = np.arange(128)[None, :]
    c = np.zeros((128, 6, 128), np.float32)
    c[:, 0] = (s == t)
    c[:, 1] = (s <= t)
    c[:, 2] = (s >= t)
    c[:, 3] = (s > t)
    c[:, 4] = (s < t)
    c[:, 5] = 1.0
    return c.reshape(128, 768)


def _rope_tables():
    tab = np.zeros((NT, 128, 4, 96), np.float32)
    qs = np.float32(96 ** -0.5)
    tab[:NCTX, :, 0, :] = qs
    tab[:NCTX, :, 2, :] = 1.0
    inv = (np.float32(10000.0) ** (-np.arange(0, 48, 2, dtype=np.float32) / np.float32(48))).astype(np.float32)
    T = np.arange(4096)
    pos = [(T // 64).astype(np.float32), (T % 64).astype(np.float32)]
    cos = np.zeros((4096, 2, 2, 24), np.float32)
    sin = np.zeros((4096, 2, 2, 24), np.float32)
    for p in range(2):
        ang = (pos[p][:, None] * inv[None, :]).astype(np.float32)
        cs, sn = np.cos(ang).astype(np.float32), np.sin(ang).astype(np.float32)
        cos[:, p, 0], cos[:, p, 1] = cs, cs
        sin[:, p, 0], sin[:, p, 1] = -sn, sn
    cos = cos.reshape(32, 128, 96)
    sin = sin.reshape(32, 128, 96)
    tab[NCTX:, :, 0] = cos * qs
    tab[NCTX:, :, 1] = sin * qs
    tab[NCTX:, :, 2] = cos
    tab[NCTX:, :, 3] = sin
    return tab


def _rpb_tables(na_rpb):
    L, H = na_rpb.shape[0], na_rpb.shape[1]
    s = np.arange(128)[:, None]
    t = np.arange(128)[None, :]
    kc, qc = s % 64, t % 64
    qr = t // 64
    cs = np.clip(qc - 8, 0, 48)
    band = (kc >= cs) & (kc < cs + 16)
    dc = np.clip(kc - qc + 15, 0, 30)
    out = np.full((L, H, 9, 128, 128), NEG, np.float32)
    for k in range(9):
        o = [-3, -2, -1, 0, 1, 2, 3, -2, 2][k]
        kr = 2 * o + s // 64
        diff = kr - qr
        ok = band & (diff >= -7) & (diff <= 7)
        if k >= 7:
            ok = ok & (diff >= -4) & (diff <= 3)
        dr = np.clip(diff + 7, 0, 14)
        for l in range(L):
            for h in range(H):
                g = na_rpb[l, h][dr, dc]
                out[l, h, k] = np.where(ok, g, np.float32(NEG))
    return out.reshape(L, H * 9, 128, 128)


def _key_tiles(lt):
    if lt <= 1:
        return [(kt, kt - lt + 3) for kt in range(0, 4)]
    if lt >= 30:
        return [(kt, kt - lt + 3) for kt in range(28, 32)]
    res = []
    for o in range(-2, 3):
        k = o + 3
        if o == -2:
            k = 7
        if o == 2:
            k = 8
        res.append((lt + o, k))
    return res


def build(debug=(), nlayers=DEPTH, phases="PABCO"):
    nc = bass.Bass("TRN2", target_bir_lowering=False)

    def din(name, shape, dt=F32):
        return nc.dram_tensor(name, list(shape), dt, kind="ExternalInput").ap()

    def dscr(name, shape, dt=F32):
        kind = "ExternalOutput" if name in debug else "Internal"
        return nc.dram_tensor(name, list(shape), dt, kind=kind).ap()

    x_in = din("x", [4096, D])
    ctx_in = din("ctx", [256, D])
    cT_in = din("cT", [128, 16])
    w_mod = din("w_mod", [DEPTH, D, 3 * D])
    b_mod = din("b_mod", [DEPTH, 3 * D])
    g_pre = din("g_pre", [DEPTH, D])
    g_post = din("g_post", [DEPTH, D])
    w_in = din("w_in", [DEPTH, D, PIN])
    w_out = din("w_out", [DEPTH, D, D])
    hgrn_lb = din("hgrn_lb", [DEPTH, 512])
    hgrn_gn = din("hgrn_gn", [DEPTH, 256])
    gate_b = din("gate_b", [DEPTH, 16])
    mlstm_gn = din("mlstm_gn", [DEPTH, 384])
    rpbt = din("rpbt", [DEPTH, 54, 128, 128])
    rope = din("rope", [NT, 128, 4 * 96])
    consts = din("consts", [128, 768])
    y_out = nc.dram_tensor("y", [4096, D], F32, kind="ExternalOutput").ap()

    XC = dscr("XC", [TOK, D])
    TB = dscr("TB", [TOK, NTB], BF16)
    TF = dscr("TF", [TOK, NTF])
    FM = dscr("FM", [NT, 128, NFM, 128], BF16)
    OA = [dscr("OA%d" % d, [TOK, 256]) for d in range(2)]
    OB = [dscr("OB%d" % d, [TOK, 384]) for d in range(2)]
    OC = dscr("OC", [TOK, 384])
    dr = DramTrack()

    with ExitStack() as es:
        fw = FW(nc, es)
        op, dma = fw.op, fw.dma

        CST = fw.sb(es, "cst", [128, 768], F32)
        dma("sp", CST, dr, CST[:], consts[:, :])
        IDENTF = CST.t[:, 0:128]
        TRI = [CST.t[:, 128:256], CST.t[:, 256:384]]
        TRIS = [CST.t[:, 384:512], CST.t[:, 512:640]]
        ONESF = CST.t[:, 640:768]
        CB = fw.sb(es, "cstb", [128, 384], BF16)
        op("dve", [CST], [CB], lambda e: e.tensor_copy(out=CB[:], in_=CST[:, 0:384]))
        IDENT = CB.t[:, 0:128]
        MASK = [CB.t[:, 128:256], CB.t[:, 256:384]]
        CT = fw.sb(es, "cT", [128, 16], F32)
        dma("sp", CT, dr, CT[:], cT_in[:, :])
        SC = fw.sb(es, "sc", [128, 16], F32)
        op("act", [CT], [SC], lambda e: e.activation(out=SC[:], in_=CT[:], func=AF.Exp, scale=-1.0))
        op("act", [SC], [SC], lambda e: e.activation(out=SC[:], in_=SC[:], func=AF.Ln, bias=1.0))
        op("act", [SC], [SC], lambda e: e.activation(out=SC[:], in_=SC[:], func=AF.Exp, scale=-1.0))
        op("dve", [SC, CT], [SC], lambda e: e.tensor_tensor(out=SC[:], in0=SC[:], in1=CT[:], op=ALU.mult))
        fw.barrier()

        def x_src(l, ti):
            if l == 0:
                if ti < NCTX:
                    return ctx_in[ti * 128:(ti + 1) * 128, :]
                return x_in[(ti - NCTX) * 128:(ti - NCTX + 1) * 128, :]
            return XC[ti * 128:(ti + 1) * 128, :]

        for l in range(nlayers):
            last = (l == DEPTH - 1)
            with ExitStack() as les:
                MODS = [fw.sb(les, "mod%d" % i, [128, 3 * D], F32) for i in range(2)]
                LB = fw.sb(les, "lb", [128, 512], F32)
                OML = fw.sb(les, "oml", [128, 512], F32)
                GNA = fw.sb(les, "gna", [128, 256], F32)
                GNB = fw.sb(les, "gnb", [128, 384], F32)
                GB = fw.sb(les, "gb", [128, 16], F32)
                dma("sp", GNA, dr, GNA[:], hgrn_gn[l:l + 1, :].partition_broadcast(128))
                dma("sp", GNB, dr, GNB[:], mlstm_gn[l:l + 1, :].partition_broadcast(128))
                dma("sp", GB, dr, GB[:], gate_b[l:l + 1, :].partition_broadcast(128))
                if l == 0:
                    op("dve", [], [LB], lambda e: e.memset(LB[:], 0.0))
                else:
                    dma("sp", LB, dr, LB[:], hgrn_lb[1:2, :].partition_broadcast(128))
                    dma("sp", OML, dr, OML[:], hgrn_lb[0:1, :].partition_broadcast(128))
                    op("dve", [LB, OML], [LB], lambda e: e.tensor_sub(out=LB[:], in0=LB[:], in1=OML[:]))
                    op("act", [LB], [LB], lambda e: e.activation(out=LB[:], in_=LB[:], func=AF.Exp, scale=-1.0))
                    op("act", [LB], [LB], lambda e: e.activation(out=LB[:], in_=LB[:], func=AF.Ln, bias=1.0))
                    op("act", [LB], [LB], lambda e: e.activation(out=LB[:], in_=LB[:], func=AF.Exp, scale=-1.0))
                op("dve", [LB], [OML], lambda e: e.tensor_scalar(out=OML[:], in0=LB[:], scalar1=-1.0, scalar2=1.0,
                                                                  op0=ALU.mult, op1=ALU.add))
                with ExitStack() as ms:
                    BM = fw.sb(ms, "bm", [128, 3 * D], F32)
                    GPRE = fw.sb(ms, "gpre", [128, D], F32)
                    GPOST = fw.sb(ms, "gpost", [128, D], F32)
                    dma("sp", BM, dr, BM[:], b_mod[l:l + 1, :].partition_broadcast(128))
                    dma("sp", GPRE, dr, GPRE[:], g_pre[l:l + 1, :].partition_broadcast(128))
                    dma("sp", GPOST, dr, GPOST[:], g_post[l:l + 1, :].partition_broadcast(128))
                    wm = Ring([fw.sb(ms, "wm%d" % i, [128, 8, 512], F32) for i in range(2)])
                    pmod = Ring([fw.ps(ms, "pmod%d" % i, [128, 512], F32) for i in range(4)])
                    for nb in range(6):
                        w = wm.next()
                        dma("sp", w, dr, w[:], w_mod[l, :, nb * 512:(nb + 1) * 512].rearrange("(kc p) n -> p kc n", p=128))
                        for st in range(2):
                            pm = pmod.next()
                            for kc in range(8):
                                op("pe", [SC, w], [pm], lambda e: e.matmul(
                                    pm[:], lhsT=SC[:, st * 8 + kc:st * 8 + kc + 1].to_broadcast([128, 128]),
                                    rhs=w[:, kc, :], start=(kc == 0), stop=(kc == 7)))
                            m = MODS[st]
                            op("dve", [pm, BM], [m], lambda e: e.tensor_tensor(
                                out=m[:, nb * 512:(nb + 1) * 512], in0=pm[:], in1=BM[:, nb * 512:(nb + 1) * 512], op=ALU.add))
                    for st in range(2):
                        m = MODS[st]
                        op("dve", [m, GPRE], [m], lambda e: e.scalar_tensor_tensor(
                            out=m[:, D:2 * D], in0=m[:, D:2 * D], scalar=1.0, in1=GPRE[:], op0=ALU.add, op1=ALU.mult))
                        op("dve", [m, GPOST], [m], lambda e: e.tensor_tensor(
                            out=m[:, 2 * D:3 * D], in0=m[:, 2 * D:3 * D], in1=GPOST[:], op=ALU.mult))
                    fw.barrier()
                    fw.release(wm.bufs + [BM, GPRE, GPOST])

                if "P" in phases:
                    with ExitStack() as ps_:
                        WIN = fw.sb(ps_, "win", [128, 8, PIN], BF16)
                        wsc = ExitStack()
                        wst = Ring([fw.sb(wsc, "wst%d" % i, [128, 1188], F32) for i in range(2)])
                        k = 0
                        for kc in range(8):
                            for q4 in range(4):
                                w = wst.next()
                                dma("sp", w, dr, w[:], w_in[l, kc * 128:(kc + 1) * 128, q4 * 1188:(q4 + 1) * 1188])
                                eng = "dve" if k % 2 == 0 else "act"
                                if eng == "dve":
                                    op("dve", [w], [WIN], lambda e: e.tensor_copy(out=WIN[:, kc, q4 * 1188:(q4 + 1) * 1188], in_=w[:]))
                                else:
                                    op("act", [w], [WIN], lambda e: e.copy(out=WIN[:, kc, q4 * 1188:(q4 + 1) * 1188], in_=w[:]))
                                k += 1
                        fw.barrier()
                        fw.release(wst.bufs)
                        wsc.close()
                        xr = Ring([fw.sb(ps_, "x%d" % i, [128, D], F32) for i in range(2)])
                        rp = Ring([fw.sb(ps_, "rope%d" % i, [128, 4, 96], F32) for i in range(2)])
                        sqr = Ring([fw.sb(ps_, "sq%d" % i, [128, D], F32) for i in range(2)])
                        ssr = Ring([fw.sb(ps_, "ss%d" % i, [128, 4], F32) for i in range(3)])
                        hbr = Ring([fw.sb(ps_, "hb%d" % i, [128, D], BF16) for i in range(2)])
                        hT = Ring([fw.sb(ps_, "hT%d" % i, [128, 8, 128], BF16) for i in range(2)])
                        phT = fw.ps(ps_, "phT", [128, 8, 128], BF16)
                        pj = Ring([fw.ps(ps_, "pj%d" % i, [128, 512], F32) for i in range(5)])
                        ptt = Ring([fw.ps(ps_, "ptt%d" % i, [128, 8, 128], BF16) for i in range(2)])
                        tbs = Ring([fw.sb(ps_, "tbs%d" % i, [128, NTBS], BF16) for i in range(2)])
                        tfs = Ring([fw.sb(ps_, "tfs%d" % i, [128, NTF], F32) for i in range(2)])
                        fms = Ring([fw.sb(ps_, "fms%d" % i, [128, NFM, 128], BF16) for i in range(2)])
                        tmpr = Ring([fw.sb(ps_, "ptmp%d" % i, [128, 512], F32) for i in range(8)])

                        def proj(ps, c0, wd, hTb):
                            for kc in range(8):
                                op("pe", [hTb, WIN], [ps], lambda e: e.matmul(
                                    ps[:, 0:wd], lhsT=hTb[:, kc, :], rhs=WIN[:, kc, c0:c0 + wd], start=(kc == 0), stop=(kc == 7)))

                        def sigm(src_buf, src_ap, wd):
                            a = tmpr.next(); b = tmpr.next()
                            op("act", [src_buf], [a], lambda e: e.activation(out=a[:, 0:wd], in_=src_ap, func=AF.Exp, scale=-1.0))
                            op("act", [a], [b], lambda e: e.activation(out=b[:, 0:wd], in_=a[:, 0:wd], func=AF.Ln, bias=1.0))
                            op("act", [b], [a], lambda e: e.activation(out=a[:, 0:wd], in_=b[:, 0:wd], func=AF.Exp, scale=-1.0))
                            return a, b

                        PST = {}

                        def prologue(ti):
                            st = 1 if ti < NCTX else 0
                            m = MODS[st]
                            xt = xr.next()
                            dma("sp", xt, dr, xt[:], x_src(l, ti))
                            rt = rp.next()
                            dma("sp", rt, dr, rt[:], rope[ti].rearrange("p (a b) -> p a b", a=4))
                            sq = sqr.next(); ss = ssr.next(); hb = hbr.next()
                            op("act", [xt], [sq, ss], lambda e: e.activation(out=sq[:], in_=xt[:], func=AF.Square, accum_out=ss[:, 0:1]))
                            op("dve", [ss], [ss], lambda e: e.tensor_scalar(out=ss[:, 1:2], in0=ss[:, 0:1], scalar1=1.0 / D, scalar2=EPS,
                                                                            op0=ALU.mult, op1=ALU.add))
                            op("act", [ss], [ss], lambda e: e.activation(out=ss[:, 2:3], in_=ss[:, 1:2], func=AF.Ln))
                            op("act", [ss], [ss], lambda e: e.activation(out=ss[:, 3:4], in_=ss[:, 2:3], func=AF.Exp, scale=-0.5))
                            op("dve", [xt, ss, m], [sq], lambda e: e.scalar_tensor_tensor(
                                out=sq[:], in0=xt[:], scalar=ss[:, 3:4], in1=m[:, D:2 * D], op0=ALU.mult, op1=ALU.mult))
                            op("dve", [sq, m], [hb], lambda e: e.tensor_tensor(out=hb[:], in0=sq[:], in1=m[:, 0:D], op=ALU.add))
                            for kc in range(8):
                                op("pe", [hb, CB], [phT], lambda e: e.transpose(out=phT[:, kc, :], in_=hb[:, kc * 128:(kc + 1) * 128], identity=IDENT))
                            hTb = hT.next()
                            op("act", [phT], [hTb], lambda e: e.copy(out=hTb[:], in_=phT[:]))
                            PST[ti] = dict(hTb=hTb, rt=rt)

                        def groups_a(ti):
                            hTb, rt = PST[ti]["hTb"], PST[ti]["rt"]
                            tb = tbs.next()
                            tf = tfs.next()
                            PST[ti].update(tb=tb, tf=tf)
                            ps = pj.next(); proj(ps, 256, 512, hTb)
                            sg, sp_ = sigm(ps, ps[:], 512)
                            if l == 0:
                                op("dve", [sp_], [tf], lambda e: e.tensor_scalar_mul(out=tf[:, 0:512], in0=sp_[:], scalar1=-1.0))
                            else:
                                op("dve", [sg, OML], [sg], lambda e: e.tensor_tensor(out=sg[:], in0=sg[:], in1=OML[:], op=ALU.mult))
                                op("dve", [sg, LB], [sg], lambda e: e.tensor_tensor(out=sg[:], in0=sg[:], in1=LB[:], op=ALU.add))
                                op("act", [sg], [tf], lambda e: e.activation(out=tf[:, 0:512], in_=sg[:], func=AF.Ln))
                            op("dve", [sg], [tb], lambda e: e.tensor_scalar(out=tb[:, 0:512], in0=sg[:], scalar1=-1.0, scalar2=1.0,
                                                                           op0=ALU.mult, op1=ALU.add))
                            ps = pj.next(); proj(ps, 0, 256, hTb)
                            sg, _ = sigm(ps, ps[:, 0:256], 256)
                            op("dve", [ps, sg], [tb], lambda e: e.tensor_tensor(out=tb[:, 2944:3200], in0=ps[:, 0:256], in1=sg[:, 0:256], op=ALU.mult))
                            ps = pj.next(); proj(ps, 768, 512, hTb)
                            op("dve", [ps], [tb], lambda e: e.tensor_copy(out=tb[:, 512:768], in_=ps[:, 0:256]))
                            sg, _ = sigm(ps, ps[:, 256:512], 256)
                            op("dve", [ps, sg], [tb], lambda e: e.tensor_tensor(out=tb[:, 768:1024], in0=ps[:, 256:512], in1=sg[:, 0:256], op=ALU.mult))
                            for (c0, dst, ci) in ((1280, 3200, 0), (1664, 1024, 2)):
                                ps = pj.next(); proj(ps, c0, 384, hTb)
                                t2 = tmpr.next(); t3 = tmpr.next()
                                pv = ps.t[:, 0:384].rearrange("p (h a b j) -> p h a b j", h=4, a=2, b=2)
                                t2v = t2.t[:, 0:384].rearrange("p (h a b j) -> p h a b j", h=4, a=2, b=2)
                                cosb = rt.t[:, ci, :].unsqueeze(1).to_broadcast([128, 4, 96])
                                sinv = rt.t[:, ci + 1, :].rearrange("p (a b j) -> p a b j", a=2, b=2)
                                op("dve", [ps, rt], [t3], lambda e: e.tensor_tensor(
                                    out=t3[:, 0:384].rearrange("p (h d) -> p h d", h=4), in0=ps[:, 0:384].rearrange("p (h d) -> p h d", h=4),
                                    in1=cosb, op=ALU.mult))
                                for b_ in range(2):
                                    op("dve", [ps, rt], [t2], lambda e: e.tensor_tensor(
                                        out=t2v[:, :, :, b_, :], in0=pv[:, :, :, 1 - b_, :],
                                        in1=sinv[:, :, b_, :].unsqueeze(1).to_broadcast([128, 4, 2, 24]), op=ALU.mult))
                                op("pool", [t3, t2], [tb], lambda e: e.tensor_tensor(out=tb[:, dst:dst + 384], in0=t3[:, 0:384], in1=t2[:, 0:384], op=ALU.add))

                        def groups_b(ti):
                            hTb, rt, tb, tf = PST[ti]["hTb"], PST[ti]["rt"], PST[ti]["tb"], PST[ti]["tf"]
                            ps = pj.next(); proj(ps, 2048, 384, hTb)
                            op("act", [ps], [tb], lambda e: e.copy(out=tb[:, 1408:1792], in_=ps[:, 0:384]))
                            ps = pj.next(); proj(ps, 2432, 384, hTb)
                            sgo, _ = sigm(ps, ps[:, 0:384], 384)
                            ps = pj.next(); proj(ps, 2816, 384, hTb)
                            sgz, _ = sigm(ps, ps[:, 0:384], 384)
                            op("dve", [ps, sgz], [sgz], lambda e: e.tensor_tensor(out=sgz[:, 0:384], in0=ps[:, 0:384], in1=sgz[:, 0:384], op=ALU.mult))
                            op("pool", [sgo, sgz], [tb], lambda e: e.tensor_tensor(out=tb[:, 1792:2176], in0=sgo[:, 0:384], in1=sgz[:, 0:384], op=ALU.mult))
                            ps = pj.next(); proj(ps, 3200, 16, hTb)
                            op("dve", [ps, GB], [tf], lambda e: e.tensor_tensor(out=tf[:, 512:528], in0=ps[:, 0:16], in1=GB[:], op=ALU.add))
                            _, spg = sigm(tf, tf[:, 520:528], 8)
                            op("dve", [spg], [tf], lambda e: e.tensor_scalar_mul(out=tf[:, 520:528], in0=spg[:, 0:8], scalar1=-1.0))
                            ps = pj.next(); proj(ps, 3216, 384, hTb)
                            op("act", [ps], [tb], lambda e: e.activation(out=tb[:, 3584:3968], in_=ps[:, 0:384], func=AF.Copy, scale=0.125))
                            ps = pj.next(); proj(ps, 3600, 384, hTb)
                            op("dve", [ps], [tb], lambda e: e.tensor_copy(out=tb[:, 3968:4352], in_=ps[:, 0:384]))
                            ps = pj.next(); proj(ps, 3984, 384, hTb)
                            op("act", [ps], [tb], lambda e: e.copy(out=tb[:, 2176:2560], in_=ps[:, 0:384]))
                            ps = pj.next(); proj(ps, 4368, 384, hTb)
                            sg, _ = sigm(ps, ps[:, 0:384], 384)
                            op("dve", [ps, sg], [tb], lambda e: e.tensor_tensor(out=tb[:, 2560:2944], in0=ps[:, 0:384], in1=sg[:, 0:384], op=ALU.mult))

                        def tail(ti):
                            tb, tf = PST[ti]["tb"], PST[ti]["tf"]
                            fm = fms.next()
                            srcA = [2944, 3072, 0, 128, 256, 384]
                            srcC = [3584, 3712, 3840, 3968, 4096, 4224]
                            pt_ = ptt.next()
                            for j, c0 in enumerate(srcA):
                                op("pe", [tb, CB], [pt_], lambda e: e.transpose(out=pt_[:, j, :], in_=tb[:, c0:c0 + 128], identity=IDENT))
                            op("dve", [pt_], [fm], lambda e: e.tensor_copy(out=fm[:, 0:6, :], in_=pt_[:, 0:6, :]))
                            pt_ = ptt.next()
                            for j, c0 in enumerate(srcC):
                                op("pe", [tb, CB], [pt_], lambda e: e.transpose(out=pt_[:, j, :], in_=tb[:, c0:c0 + 128], identity=IDENT))
                            op("act", [pt_], [fm], lambda e: e.copy(out=fm[:, 6:12, :], in_=pt_[:, 0:6, :]))
                            pt_ = ptt.next()
                            for j in range(8):
                                c0 = (3200 if j < 4 else 1024) + (j % 4) * 96
                                op("pe", [tb, CB], [pt_], lambda e: e.transpose(out=pt_[0:96, j, :], in_=tb[:, c0:c0 + 96], identity=IDENT))
                            op("dve", [pt_], [fm], lambda e: e.tensor_copy(out=fm[0:96, 12:20, :], in_=pt_[0:96, 0:8, :]))
                            del PST[ti]

                            def stores():
                                dma("pool", dr, tb, TB[ti * 128:(ti + 1) * 128, :], tb[:, 0:NTB], sem_buf=tb)
                                dma("pool", dr, tf, TF[ti * 128:(ti + 1) * 128, :], tf[:], sem_buf=tf)
                                dma("pool", dr, fm, FM[ti], fm[:], sem_buf=fm)
                            return stores

                        prologue(0)
                        for ti in range(NT):
                            if ti + 1 < NT:
                                prologue(ti + 1)
                            groups_a(ti)
                            pst = tail(ti - 1) if ti > 0 else None
                            groups_b(ti)
                            if pst is not None:
                                pst()
                        tail(NT - 1)()
                        fw.barrier()
                        fw.release(xr.bufs + rp.bufs + tbs.bufs + tfs.bufs + fms.bufs)

                if "A" in phases:
                    with ExitStack() as as_:
                        LA = int(os.environ.get("LA_A", 2))

                        def ring(name, shape, dt, n=None, ps=False):
                            if n is None:
                                n = LA + 1
                            return Ring([(fw.ps if ps else fw.sb)(as_, "%s%d" % (name, i), shape, dt) for i in range(n)])
                        St = []
                        for d in range(2):
                            S = fw.sb(as_, "S%d" % d, [128, 2, 64], F32)
                            Sb = fw.sb(as_, "Sb%d" % d, [128, 2, 128], BF16)
                            op("dve", [], [S], lambda e: e.memset(S[:], 0.0))
                            op("dve", [], [Sb], lambda e: e.memset(Sb[:], 0.0))
                            R_ = dict(S=S, Sb=Sb,
                                      lf=ring("alf%d" % d, [128, 256], F32), qT=ring("aqT%d" % d, [128, 2, 128], BF16),
                                      kT=ring("akT%d" % d, [128, 2, 128], BF16), kt=ring("akt%d" % d, [128, 256], BF16),
                                      vt=ring("avt%d" % d, [128, 256], BF16), ost=ring("aos%d" % d, [128, 256], F32, 3),
                                      bT=ring("abT%d" % d, [128, 2, 128], F32), E1=ring("aE1%d" % d, [128, 2, 128], F32),
                                      Es=ring("aEs%d" % d, [128, 2, 128], F32), tmp=ring("atm%d" % d, [128, 2, 128], F32),
                                      rr=ring("arr%d" % d, [128, 2, 8], F32), edk=ring("aed%d" % d, [128, 256], F32),
                                      kd=ring("akd%d" % d, [128, 256], BF16), Qf=ring("aQf%d" % d, [128, 2, 128], BF16),
                                      Qs=ring("aQs%d" % d, [128, 2, 2, 128], BF16), ATm=ring("aAT%d" % d, [128, 4, 128], BF16),
                                      Ek=[ring("aEk%d_%d" % (d, i), [128, 2, 128], F32) for i in range(4)],
                                      Kv=[ring("aKv%d_%d" % (d, i), [128, 2, 128], BF16) for i in range(4)])
                            for i in range(4):
                                for b in R_["Kv"][i].bufs:
                                    op("dve", [], [b], lambda e: e.memset(b[:], 0.0))
                            for b in R_["rr"].bufs + R_["Qs"].bufs:
                                op("dve", [], [b], lambda e: e.memset(b[:], 0.0))
                            St.append(R_)
                        pbT = ring("apbT", [128, 2, 128], F32, 1, ps=True)
                        prs = ring("aprs", [128, 256], F32, 1, ps=True)
                        pAT = ring("apAT", [128, 4, 128], F32, 2, ps=True)
                        pO = ring("apO", [128, 256], F32, 2, ps=True)
                        pU = ring("apU", [128, 2, 128], F32, 1, ps=True)

                        def a_s0(c):
                            ti, d = c["ti"], c["d"]
                            R_ = St[d]
                            lf = R_["lf"].next(); qT = R_["qT"].next(); kT = R_["kT"].next(); kt = R_["kt"].next(); vt = R_["vt"].next()
                            r0 = ti * 128
                            dma("sp", lf, dr, lf[:], TF[r0:r0 + 128, d * 256:(d + 1) * 256])
                            dma("sp", qT, dr, qT[:], FM[ti, :, 0:2, :])
                            dma("sp", kT, dr, kT[:], FM[ti, :, 2 + 2 * d:4 + 2 * d, :])
                            dma("sp", kt, dr, kt[:], TB[r0:r0 + 128, d * 256:(d + 1) * 256])
                            dma("sp", vt, dr, vt[:], TB[r0:r0 + 128, 512:768])
                            pb = pbT.next()
                            for pt in range(2):
                                op("pe", [lf, CST], [pb], lambda e: e.matmul(pb[:, pt, :], lhsT=lf[:, pt * 128:(pt + 1) * 128], rhs=TRI[d], start=True, stop=True))
                            bT = R_["bT"].next()
                            op("act", [pb], [bT], lambda e: e.copy(out=bT[:], in_=pb[:]))
                            pr = prs.next()
                            op("pe", [lf, CST], [pr], lambda e: e.matmul(pr[:], lhsT=TRIS[d], rhs=lf[:], start=True, stop=True))
                            edk = R_["edk"].next()
                            op("act", [pr], [edk], lambda e: e.activation(out=edk[:], in_=pr[:], func=AF.Exp))
                            c.update(qT=qT, kT=kT, kt=kt, vt=vt, bT=bT, edk=edk)

                        def a_s1(c):
                            d, bT, edk, kt = c["d"], c["bT"], c["edk"], c["kt"]
                            R_ = St[d]
                            kd = R_["kd"].next()
                            op("pool", [edk, kt], [kd], lambda e: e.tensor_tensor(out=kd[:], in0=edk[:], in1=kt[:], op=ALU.mult))
                            E1 = R_["E1"].next()
                            op("act", [bT], [E1], lambda e: e.activation(out=E1[:], in_=bT[:], func=AF.Exp))
                            rr = R_["rr"].next()
                            if d == 0:
                                src = bT.t[:, :, 31:127:32]; dn = rr.t[:, :, 1:4]; dp = rr.t[:, :, 5:8]
                            else:
                                src = bT.t[:, :, 32:128:32]; dn = rr.t[:, :, 0:3]; dp = rr.t[:, :, 4:7]
                            op("dve", [bT], [rr], lambda e: e.tensor_scalar_mul(out=dn, in0=src, scalar1=-1.0))
                            op("dve", [bT], [rr], lambda e: e.tensor_copy(out=dp, in_=src))
                            tmp = R_["tmp"].next()
                            op("dve", [bT, rr], [tmp], lambda e: e.tensor_tensor(
                                out=tmp[:].rearrange("p a (i j) -> p a i j", i=4), in0=bT[:].rearrange("p a (i j) -> p a i j", i=4),
                                in1=rr.t[:, :, 0:4].unsqueeze(3).to_broadcast([128, 2, 4, 32]), op=ALU.add))
                            c.update(kd=kd, E1=E1, rr=rr, tmp=tmp)

                        def a_s2(c):
                            d, bT, E1, rr, tmp, qT = c["d"], c["bT"], c["E1"], c["rr"], c["tmp"], c["qT"]
                            R_ = St[d]
                            Qf = R_["Qf"].next()
                            op("pool", [E1, qT], [Qf], lambda e: e.tensor_tensor(out=Qf[:], in0=E1[:], in1=qT[:], op=ALU.mult))
                            Es = R_["Es"].next()
                            op("act", [tmp], [Es], lambda e: e.activation(out=Es[:], in_=tmp[:], func=AF.Exp))
                            Eks = []
                            for i in range(4):
                                lo, hi = (0, 32 * (i + 1)) if d == 0 else (32 * i, 128)
                                Ek = R_["Ek"][i].next()
                                for pt in range(2):
                                    op("act", [bT, rr], [Ek], lambda e: e.activation(
                                        out=Ek[:, pt, lo:hi], in_=bT[:, pt, lo:hi], func=AF.Exp, bias=rr[:, pt, 4 + i:5 + i], scale=-1.0))
                                Eks.append(Ek)
                            c.update(Qf=Qf, Es=Es, Eks=Eks)

                        def a_s3(c):
                            d, Es, Eks, qT, kT = c["d"], c["Es"], c["Eks"], c["qT"], c["kT"]
                            R_ = St[d]
                            Qs = R_["Qs"].next()
                            for hl in range(2):
                                op("dve", [Es, qT], [Qs], lambda e: e.tensor_tensor(
                                    out=Qs[64 * hl:64 * hl + 64, hl, :, :], in0=Es[64 * hl:64 * hl + 64, :, :], in1=qT[64 * hl:64 * hl + 64, :, :], op=ALU.mult))
                            Kvs = []
                            for i in range(4):
                                lo, hi = (0, 32 * (i + 1)) if d == 0 else (32 * i, 128)
                                Ek = Eks[i]
                                Kv = R_["Kv"][i].next()
                                op("pool" if i % 2 == 1 else "dve", [Ek, kT], [Kv],
                                   lambda e: e.tensor_tensor(out=Kv[:, :, lo:hi], in0=Ek[:, :, lo:hi], in1=kT[:, :, lo:hi], op=ALU.mult))
                                Kvs.append(Kv)
                            c.update(Qs=Qs, Kvs=Kvs)

                        def a_s4(c):
                            d, Kvs, Qs = c["d"], c["Kvs"], c["Qs"]
                            R_ = St[d]
                            pa = pAT.next()
                            for h in range(4):
                                pt = h // 2
                                for i in range(4):
                                    Kv = Kvs[i]
                                    op("pe", [Kv, Qs], [pa], lambda e: e.matmul(
                                        pa[:, h, 32 * i:32 * i + 32], lhsT=Kv[:, pt, :], rhs=Qs[:, h % 2, pt, 32 * i:32 * i + 32],
                                        start=True, stop=True))
                            ATm = R_["ATm"].next()
                            op("dve", [pa, CB], [ATm], lambda e: e.tensor_tensor(
                                out=ATm[:], in0=pa[:], in1=MASK[d].unsqueeze(1).to_broadcast([128, 4, 128]), op=ALU.mult))
                            c["ATm"] = ATm

                        def a_s5(c):
                            ti, d, vt, kd, E1, Qf, ATm = c["ti"], c["d"], c["vt"], c["kd"], c["E1"], c["Qf"], c["ATm"]
                            R_ = St[d]
                            S, Sb = R_["S"], R_["Sb"]
                            r0 = ti * 128
                            po = pO.next()
                            for pt in range(2):
                                op("pe", [Qf, Sb], [po], lambda e: e.matmul(po[:, 128 * pt:128 * pt + 128], lhsT=Qf[:, pt, :], rhs=Sb[:, pt, :],
                                                                            start=True, stop=False, skip_group_check=True))
                                for hl in range(2):
                                    h = 2 * pt + hl
                                    op("pe", [ATm, vt], [po], lambda e: e.matmul(po[:, 64 * h:64 * h + 64], lhsT=ATm[:, h, :], rhs=vt[:, 64 * h:64 * h + 64],
                                                                                start=False, stop=(hl == 1), skip_group_check=True))
                            pu = pU.next()
                            for pt in range(2):
                                op("pe", [kd, vt], [pu], lambda e: e.matmul(pu[:, pt, :], lhsT=kd[:, pt * 128:(pt + 1) * 128], rhs=vt[:, pt * 128:(pt + 1) * 128],
                                                                           start=True, stop=True))
                            col = 127 if d == 0 else 0
                            for pt in range(2):
                                for hl in range(2):
                                    bs = 64 * hl
                                    op("dve", [S, E1, pu], [S], lambda e: e.scalar_tensor_tensor(
                                        out=S[bs:bs + 64, pt, :], in0=S[bs:bs + 64, pt, :], scalar=E1[bs:bs + 64, pt, col:col + 1],
                                        in1=pu[bs:bs + 64, pt, bs:bs + 64], op0=ALU.mult, op1=ALU.add))
                            for hl in range(2):
                                bs = 64 * hl
                                op("act", [S], [Sb], lambda e: e.copy(out=Sb[bs:bs + 64, :, bs:bs + 64], in_=S[bs:bs + 64, :, :]))
                            ost = R_["ost"].next()
                            op("act", [po], [ost], lambda e: e.copy(out=ost[:], in_=po[:]))
                            c["store"] = lambda: dma("pool", dr, ost, OA[d][r0:r0 + 128, :], ost[:], sem_buf=ost)

                        def a_s6(c):
                            c["store"]()

                        seq = []
                        for j in range(NT):
                            seq += [dict(ti=j, d=0), dict(ti=BWD_ORDER[j], d=1)]
                        run_skewed([a_s0, a_s1, a_s2, a_s3, a_s4, a_s5, a_s6], seq, name="A")
                        fw.barrier()
                        for R_ in St:
                            for k_ in ("lf", "qT", "kT", "kt", "vt", "ost"):
                                fw.release(R_[k_].bufs)

                if "B" in phases:
                    with ExitStack() as bs_:
                        def ring(name, shape, dt, n=3, ps=False):
                            return Ring([(fw.ps if ps else fw.sb)(bs_, "%s%d" % (name, i), shape, dt) for i in range(n)])
                        St = []
                        for d in range(2):
                            C = fw.sb(bs_, "C%d" % d, [128, 4, 97], F32)
                            Cb = fw.sb(bs_, "Cb%d" % d, [128, 4, 97], BF16)
                            op("dve", [], [C], lambda e: e.memset(C[:], 0.0))
                            op("dve", [], [Cb], lambda e: e.memset(Cb[:], 0.0))
                            R_ = dict(C=C, Cb=Cb, g=ring("bg%d" % d, [128, 16], F32), qT=ring("bqT%d" % d, [128, 4, 128], BF16),
                                      kT=ring("bkT%d" % d, [128, 4, 128], BF16), kt=ring("bkt%d" % d, [128, 384], BF16),
                                      vx=ring("bvx%d" % d, [128, 4, 97], BF16), X=ring("bX%d" % d, [128, 16], F32),
                                      E=ring("bE%d" % d, [128, 16], F32), kh=ring("bkh%d" % d, [128, 384], BF16),
                                      ST=ring("bST%d" % d, [128, 4, 128], BF16), u=ring("bu%d" % d, [128, 4, 97], F32),
                                      dd=ring("bdd%d" % d, [128, 8], F32), ost=ring("bos%d" % d, [128, 384], F32, 3))
                            for b in R_["vx"].bufs:
                                op("dve", [], [b], lambda e: e.memset(b[:], 1.0))
                            St.append(R_)
                        pg = ring("bpg", [128, 16], F32, 2, ps=True)
                        psc = ring("bpsc", [128, 4, 128], F32, 2, ps=True)
                        pout = ring("bpout", [128, 4, 128], F32, 2, ps=True)
                        pdu = ring("bpdu", [128, 4, 128], F32, 1, ps=True)

                        def b_s0(c):
                            ti, d = c["ti"], c["d"]
                            R_ = St[d]
                            g = R_["g"].next(); qT = R_["qT"].next(); kT = R_["kT"].next(); kt = R_["kt"].next(); vx = R_["vx"].next()
                            r0 = ti * 128
                            dma("sp", g, dr, g[:], TF[r0:r0 + 128, 512:528])
                            dma("sp", qT, dr, qT[:], FM[ti, :, 12:16, :])
                            dma("sp", kT, dr, kT[:], FM[ti, :, 16:20, :])
                            dma("sp", kt, dr, kt[:], TB[r0:r0 + 128, 1024:1408])
                            dma("sp", vx, dr, vx[:, :, 0:96], TB[r0:r0 + 128, 1408:1792].rearrange("p (h d) -> p h d", h=4))
                            ig = g.t[:, 4 * d:4 * d + 4]
                            lfd = g.t[:, 8 + 4 * d:12 + 4 * d]
                            p_ = pg.next()
                            op("pe", [g, CST], [p_], lambda e: e.matmul(p_[:, 0:4], lhsT=TRI[d], rhs=lfd, start=True, stop=True))
                            op("pe", [g, CST], [p_], lambda e: e.matmul(p_[:, 4:8], lhsT=TRIS[d], rhs=lfd, start=True, stop=True))
                            op("pe", [g, CST], [p_], lambda e: e.matmul(p_[:, 8:12], lhsT=ONESF, rhs=lfd, start=True, stop=True))
                            sc = psc.next()
                            for h in range(4):
                                op("pe", [kT, qT], [sc], lambda e: e.matmul(sc[:, h, :], lhsT=kT[0:96, h, :], rhs=qT[0:96, h, :], start=True, stop=True))
                            X = R_["X"].next()
                            op("dve", [g, p_], [X], lambda e: e.tensor_tensor(out=X[:, 0:4], in0=ig, in1=p_[:, 0:4], op=ALU.subtract))
                            op("dve", [g, p_], [X], lambda e: e.tensor_tensor(out=X[:, 4:8], in0=ig, in1=p_[:, 4:8], op=ALU.add))
                            op("dve", [p_], [X], lambda e: e.tensor_copy(out=X[:, 8:12], in_=p_[:, 0:4]))
                            op("dve", [p_], [X], lambda e: e.tensor_copy(out=X[:, 12:16], in_=p_[:, 8:12]))
                            c.update(qT=qT, kt=kt, vx=vx, X=X, sc=sc)

                        def b_s1(c):
                            d, kt, X = c["d"], c["kt"], c["X"]
                            R_ = St[d]
                            E = R_["E"].next()
                            op("act", [X], [E], lambda e: e.activation(out=E[:], in_=X[:], func=AF.Exp))
                            kh = R_["kh"].next()
                            op("pool", [kt, E], [kh], lambda e: e.tensor_tensor(
                                out=kh[:].rearrange("p (h d) -> p h d", h=4), in0=kt[:].rearrange("p (h d) -> p h d", h=4),
                                in1=E.t[:, 4:8].unsqueeze(2).to_broadcast([128, 4, 96]), op=ALU.mult))
                            c.update(E=E, kh=kh)

                        def b_s2(c):
                            d, sc, E = c["d"], c["sc"], c["E"]
                            R_ = St[d]
                            ST = R_["ST"].next()
                            for h in range(4):
                                op("dve", [sc, E, CB], [ST], lambda e: e.scalar_tensor_tensor(
                                    out=ST[:, h, :], in0=sc[:, h, :], scalar=E[:, h:h + 1], in1=MASK[d], op0=ALU.mult, op1=ALU.mult))
                            c.update(ST=ST)

                        def b_s3(c):
                            d, qT, vx, E, kh, ST = c["d"], c["qT"], c["vx"], c["E"], c["kh"], c["ST"]
                            R_ = St[d]
                            C, Cb = R_["C"], R_["Cb"]
                            po = pout.next()
                            for h in range(4):
                                op("pe", [ST, vx], [po], lambda e: e.matmul(po[:, h, 0:97], lhsT=ST[:, h, :], rhs=vx[:, h, :], start=True, stop=False))
                                op("pe", [qT, Cb], [po], lambda e: e.matmul(po[:, h, 0:97], lhsT=qT[0:96, h, :], rhs=Cb[0:96, h, :], start=False, stop=True))
                            pd = pdu.next()
                            for h in range(4):
                                op("pe", [kh, vx], [pd], lambda e: e.matmul(pd[0:96, h, 0:97], lhsT=kh[:, 96 * h:96 * h + 96], rhs=vx[:, h, :], start=True, stop=True))
                            op("dve", [C, E], [C], lambda e: e.tensor_tensor(
                                out=C[0:96, :, :], in0=C[0:96, :, :], in1=E.t[0:96, 12:16].unsqueeze(2).to_broadcast([96, 4, 97]), op=ALU.mult))
                            op("dve", [C, pd], [C], lambda e: e.tensor_tensor(out=C[0:96, :, :], in0=C[0:96, :, :], in1=pd[0:96, :, 0:97], op=ALU.add))
                            op("act", [C], [Cb], lambda e: e.copy(out=Cb[0:96, :, :], in_=C[0:96, :, :]))
                            c.update(po=po)

                        def b_s4(c):
                            ti, d, po, E = c["ti"], c["d"], c["po"], c["E"]
                            R_ = St[d]
                            r0 = ti * 128
                            u = R_["u"].next()
                            op("dve", [po, E], [u], lambda e: e.tensor_tensor(
                                out=u[:], in0=po[:, :, 0:97], in1=E.t[:, 8:12].unsqueeze(2).to_broadcast([128, 4, 97]), op=ALU.mult))
                            dd = R_["dd"].next()
                            op("dve", [u], [dd], lambda e: e.tensor_scalar_max(out=dd[:, 0:4], in0=u[:, :, 96], scalar1=1.0))
                            op("dve", [u, dd], [dd], lambda e: e.scalar_tensor_tensor(out=dd[:, 4:8], in0=u[:, :, 96], scalar=-1.0, in1=dd[:, 0:4],
                                                                                     op0=ALU.mult, op1=ALU.max))
                            op("dve", [dd], [dd], lambda e: e.reciprocal(out=dd[:, 0:4], in_=dd[:, 4:8]))
                            ost = R_["ost"].next()
                            op("pool", [u, dd], [ost], lambda e: e.tensor_tensor(
                                out=ost[:].rearrange("p (h d) -> p h d", h=4), in0=u[:, :, 0:96],
                                in1=dd.t[:, 0:4].unsqueeze(2).to_broadcast([128, 4, 96]), op=ALU.mult))
                            c["store"] = lambda: dma("pool", dr, ost, OB[d][r0:r0 + 128, :], ost[:], sem_buf=ost)

                        def b_s5(c):
                            c["store"]()

                        seq = []
                        for j in range(NT):
                            seq += [dict(ti=j, d=0), dict(ti=BWD_ORDER[j], d=1)]
                        run_skewed([b_s0, b_s1, b_s2, b_s3, b_s4, b_s5], seq, name="B")
                        fw.barrier()
                        for R_ in St:
                            for k_ in ("g", "qT", "kT", "kt", "vx", "ost"):
                                fw.release(R_[k_].bufs)

                if "C" in phases:
                    with ExitStack() as cs_:
                        KT = fw.sb(cs_, "cKT", [128, NT, 3, 128], BF16)
                        VX = fw.sb(cs_, "cVX", [128, NT, 6, 65], BF16)
                        BIAS = fw.sb(cs_, "cBias", [128, 54, 128], BF16)
                        op("dve", [], [VX], lambda e: e.memset(VX[:], 1.0))
                        bst = Ring([fw.sb(cs_, "cbst%d" % i, [128, 6, 128], F32) for i in range(2)])
                        for g_ in range(9):
                            b = bst.next()
                            dma("sp", b, dr, b[:], rpbt[l, g_ * 6:(g_ + 1) * 6].rearrange("k s t -> s k t"))
                            op("dve", [b], [BIAS], lambda e: e.tensor_copy(out=BIAS[:, g_ * 6:(g_ + 1) * 6, :], in_=b[:]))
                        for ti in range(NT):
                            dma("sp", KT, dr, KT[:, ti, :, :], FM[ti, :, 9:12, :])
                        for ti in range(NT):
                            dma("sp", VX, dr, VX[:, ti, :, 0:64],
                                TB[ti * 128:(ti + 1) * 128, 2176:2560].rearrange("p (h d) -> p h d", h=6))
                        qr_ = Ring([fw.sb(cs_, "cq%d" % i, [128, 3, 128], BF16) for i in range(2)])
                        qzr = Ring([fw.sb(cs_, "cqz%d" % i, [128, 6, 128], BF16) for i in range(3)])
                        for b in qzr.bufs:
                            op("dve", [], [b], lambda e: e.memset(b[:], 0.0))
                        PTr = Ring([fw.sb(cs_, "cPT%d" % i, [128, 7, 128], BF16) for i in range(3)])
                        osr = Ring([fw.sb(cs_, "cos%d" % i, [128, 384], F32) for i in range(3)])
                        rrr = Ring([fw.sb(cs_, "crr%d" % i, [128, 6], F32) for i in range(2)])
                        psct = Ring([fw.ps(cs_, "cps%d" % i, [128, 8, 128], F32) for i in range(3)])
                        pcout = Ring([fw.ps(cs_, "cpo%d" % i, [128, 6, 65], F32) for i in range(2)])
                        tiles = list(range(NT)) if not last else list(range(NCTX, NT))
                        TS = {}

                        def c_keys(ti):
                            if ti < NCTX:
                                return [(0, None), (1, None)]
                            return [(NCTX + kt, kind) for kt, kind in _key_tiles(ti - NCTX)] + [(0, None), (1, None)]

                        def c_s0(c):
                            ti, h = c["ti"], c["h"]
                            if h == 0:
                                q = qr_.next()
                                dma("sp", q, dr, q[:], FM[ti, :, 6:9, :])
                                qz = qzr.next()
                                op("pool", [q], [qz], lambda e: e.tensor_copy(out=qz[0:64, 0:6:2, :], in_=q[0:64, :, :]))
                                op("pool", [q], [qz], lambda e: e.tensor_copy(out=qz[64:128, 1:6:2, :], in_=q[64:128, :, :]))
                                TS[ti] = dict(qz=qz, po=pcout.next())
                            qz = TS[ti]["qz"]
                            c["po"] = TS[ti]["po"]
                            keys = c_keys(ti)
                            pt = h // 2
                            sc = psct.next()
                            for j, (kt, kind) in enumerate(keys):
                                op("pe", [KT, qz], [sc], lambda e: e.matmul(sc[:, j, :], lhsT=KT[:, kt, pt, :], rhs=qz[:, h, :],
                                                                           start=True, stop=(kind is None)))
                                if kind is not None:
                                    op("pe", [CB, BIAS], [sc], lambda e: e.matmul(sc[:, j, :], lhsT=IDENT, rhs=BIAS[:, h * 9 + kind, :], start=False, stop=True))
                            c["sc"] = sc

                        def c_s1(c):
                            nk = len(c_keys(c["ti"]))
                            sc = c["sc"]
                            PT = PTr.next()
                            op("act", [sc], [PT], lambda e: e.activation(out=PT[:, 0:nk, :], in_=sc[:, 0:nk, :], func=AF.Exp))
                            c["PT"] = PT

                        def c_s2(c):
                            ti, h, PT, po = c["ti"], c["h"], c["PT"], c["po"]
                            keys = c_keys(ti)
                            nk = len(keys)
                            for j, (kt, kind) in enumerate(keys):
                                op("pe", [PT, VX], [po], lambda e: e.matmul(po[:, h, :], lhsT=PT[:, j, :], rhs=VX[:, kt, h, :], start=(j == 0), stop=(j == nk - 1)))
                            if h == 5:
                                rr = rrr.next()
                                op("dve", [po], [rr], lambda e: e.reciprocal(out=rr[:], in_=po[:, :, 64]))
                                ost = osr.next()
                                op("dve", [po, rr], [ost], lambda e: e.tensor_tensor(
                                    out=ost[:].rearrange("p (h d) -> p h d", h=6), in0=po[:, :, 0:64],
                                    in1=rr[:].unsqueeze(2).to_broadcast([128, 6, 64]), op=ALU.mult))
                                c["store"] = lambda: dma("pool", dr, ost, OC[ti * 128:(ti + 1) * 128, :], ost[:], sem_buf=ost)

                        def c_s3(c):
                            if "store" in c:
                                c["store"]()

                        run_skewed([c_s0, c_s1, c_s2, c_s3], [dict(ti=ti, h=h) for ti in tiles for h in range(6)], name="C")
                        fw.barrier()
                        fw.release([KT, VX] + bst.bufs + qr_.bufs + osr.bufs)

                if "O" in phases:
                    with ExitStack() as os_:
                        WO = fw.sb(os_, "wo", [128, 8, D], BF16)
                        wst = Ring([fw.sb(os_, "wost%d" % i, [128, D], F32) for i in range(2)])
                        for kc in range(8):
                            w = wst.next()
                            dma("sp", w, dr, w[:], w_out[l, kc * 128:(kc + 1) * 128, :])
                            op("dve", [w], [WO], lambda e: e.tensor_copy(out=WO[:, kc, :], in_=w[:]))

                        def ring(name, shape, dt, n=2, ps=False):
                            return Ring([(fw.ps if ps else fw.sb)(os_, "%s%d" % (name, i), shape, dt) for i in range(n)])
                        oa = [ring("ooa%d" % d, [128, 256], F32) for d in range(2)]
                        ob = [ring("oob%d" % d, [128, 384], F32) for d in range(2)]
                        oc = ring("ooc", [128, 384], F32, 3)
                        gz = ring("ogz", [128, 3, 384], BF16, 3)
                        xr = ring("ox", [128, D], F32, 6)
                        xo = ring("oxo", [128, D], F32, 3)
                        Y = ring("oY", [128, D], BF16, 3)
                        YT = ring("oYT", [128, 8, 128], BF16, 3)
                        sAr = ring("osA", [128, 256], F32, 3)
                        sBr = ring("osB", [128, 384], F32, 3)
                        tAr = ring("otA", [128, 256], BF16, 2)
                        tBr = ring("otB", [128, 384], F32, 2)
                        st8r = ring("ost8", [128, 16], F32, 3)
                        stAr = ring("ostA", [128, 16], F32, 3)
                        stBr = ring("ostB", [128, 16], F32, 3)
                        bigr = ring("obig", [128, D], F32, 2)
                        junk = ring("ojunk", [128, 512], BF16, 2)
                        pyt = ring("opyt", [128, 8, 128], BF16, 2, ps=True)
                        pu = ring("opu", [128, 2, 512], F32, 3, ps=True)
                        tiles = list(range(NT)) if not last else list(range(NCTX, NT))

                        def o_s0(c):
                            ti = c["ti"]
                            r0 = ti * 128
                            a0, a1 = oa[0].next(), oa[1].next()
                            b0, b1 = ob[0].next(), ob[1].next()
                            c_ = oc.next(); g_ = gz.next(); xt = xr.next()
                            dma("sp", a0, dr, a0[:], OA[0][r0:r0 + 128, :]); dma("sp", a1, dr, a1[:], OA[1][r0:r0 + 128, :])
                            dma("sp", b0, dr, b0[:], OB[0][r0:r0 + 128, :]); dma("sp", b1, dr, b1[:], OB[1][r0:r0 + 128, :])
                            dma("sp", c_, dr, c_[:], OC[r0:r0 + 128, :])
                            dma("sp", g_, dr, g_[:, 0, 0:256], TB[r0:r0 + 128, 768:1024])
                            dma("sp", g_, dr, g_[:, 1, :], TB[r0:r0 + 128, 1792:2176])
                            dma("sp", g_, dr, g_[:, 2, :], TB[r0:r0 + 128, 2560:2944])
                            dma("sp", xt, dr, xt[:], x_src(l, ti))
                            sA = sAr.next(); sB = sBr.next(); tA = tAr.next(); tB = tBr.next(); stA = stAr.next(); stB = stBr.next()
                            c.update(c_=c_, g_=g_, xt=xt, sA=sA, sB=sB, stA=stA, stB=stB)
                            op("pool", [a0, a1], [sA], lambda e: e.tensor_tensor(out=sA[:], in0=a0[:], in1=a1[:], op=ALU.add))
                            for h in range(4):
                                op("act", [sA], [tA, stA], lambda e: e.activation(out=tA[:, 64 * h:64 * h + 64], in_=sA[:, 64 * h:64 * h + 64],
                                                                                   func=AF.Square, accum_out=stA[:, h:h + 1]))
                            op("pool", [stA], [stA], lambda e: e.tensor_scalar(out=stA[:, 0:4], in0=stA[:, 0:4], scalar1=1.0 / 64, scalar2=64.0 * EPS,
                                                                              op0=ALU.mult, op1=ALU.add))
                            op("dve", [b0, b1], [sB], lambda e: e.tensor_tensor(out=sB[:], in0=b0[:], in1=b1[:], op=ALU.add))
                            op("dve", [sB], [tB], lambda e: e.tensor_tensor(out=tB[:], in0=sB[:], in1=sB[:], op=ALU.mult))
                            op("dve", [tB], [stB], lambda e: e.tensor_reduce(out=stB[:, 4:8], in_=tB[:].rearrange("p (h d) -> p h d", h=4),
                                                                            axis=AX.X, op=ALU.add))
                            op("dve", [stB], [stB], lambda e: e.tensor_scalar(out=stB[:, 4:8], in0=stB[:, 4:8], scalar1=1.0 / 96, scalar2=EPS,
                                                                              op0=ALU.mult, op1=ALU.add))

                        def o_s1(c):
                            c_, g_, sA, sB, stA, stB = c["c_"], c["g_"], c["sA"], c["sB"], c["stA"], c["stB"]
                            y = Y.next()
                            c["y"] = y
                            op("act", [stA], [stA], lambda e: e.activation(out=stA[:, 0:4], in_=stA[:, 0:4], func=AF.Ln))
                            op("act", [stA], [stA], lambda e: e.activation(out=stA[:, 0:4], in_=stA[:, 0:4], func=AF.Exp, scale=-0.5))
                            op("act", [stB], [stB], lambda e: e.activation(out=stB[:, 4:8], in_=stB[:, 4:8], func=AF.Ln))
                            op("act", [stB], [stB], lambda e: e.activation(out=stB[:, 4:8], in_=stB[:, 4:8], func=AF.Exp, scale=-0.5))
                            op("pool", [sA, stA], [sA], lambda e: e.tensor_tensor(
                                out=sA[:].rearrange("p (h d) -> p h d", h=4), in0=sA[:].rearrange("p (h d) -> p h d", h=4),
                                in1=stA.t[:, 0:4].unsqueeze(2).to_broadcast([128, 4, 64]), op=ALU.mult))
                            op("pool", [sA, GNA], [sA], lambda e: e.tensor_tensor(out=sA[:], in0=sA[:], in1=GNA[:], op=ALU.mult))
                            op("pool", [sA, g_], [y], lambda e: e.tensor_tensor(out=y[:, 0:256], in0=sA[:], in1=g_[:, 0, 0:256], op=ALU.mult))
                            op("dve", [sB, stB], [sB], lambda e: e.tensor_tensor(
                                out=sB[:].rearrange("p (h d) -> p h d", h=4), in0=sB[:].rearrange("p (h d) -> p h d", h=4),
                                in1=stB.t[:, 4:8].unsqueeze(2).to_broadcast([128, 4, 96]), op=ALU.mult))
                            op("dve", [sB, GNB], [sB], lambda e: e.tensor_tensor(out=sB[:], in0=sB[:], in1=GNB[:], op=ALU.mult))
                            op("dve", [sB, g_], [y], lambda e: e.tensor_tensor(out=y[:, 256:640], in0=sB[:], in1=g_[:, 1, :], op=ALU.mult))
                            op("dve", [c_, g_], [y], lambda e: e.tensor_tensor(out=y[:, 640:1024], in0=c_[:], in1=g_[:, 2, :], op=ALU.mult))

                        def o_s2(c):
                            y = c["y"]
                            py = pyt.next()
                            for kc in range(8):
                                op("pe", [y, CB], [py], lambda e: e.transpose(out=py[:, kc, :], in_=y[:, kc * 128:(kc + 1) * 128], identity=IDENT))
                            yT = YT.next()
                            c["yT"] = yT
                            op("act", [py], [yT], lambda e: e.copy(out=yT[:], in_=py[:]))

                        def o_s3(c):
                            yT = c["yT"]
                            u = pu.next()
                            st8 = st8r.next()
                            c.update(u=u, st8=st8)
                            for nb in range(2):
                                for kc in range(8):
                                    op("pe", [yT, WO], [u], lambda e: e.matmul(u[:, nb, :], lhsT=yT[:, kc, :], rhs=WO[:, kc, nb * 512:(nb + 1) * 512],
                                                                              start=(kc == 0), stop=(kc == 7)))
                            for nb in range(2):
                                jk = junk.next()
                                op("act", [u], [jk, st8], lambda e: e.activation(out=jk[:], in_=u[:, nb, :], func=AF.Square,
                                                                               accum_out=st8[:, 8 + nb:9 + nb]))

                        def o_s4(c):
                            ti, u, st8, xt = c["ti"], c["u"], c["st8"], c["xt"]
                            m = MODS[1 if ti < NCTX else 0]
                            r0 = ti * 128
                            big = bigr.next()
                            op("dve", [st8], [st8], lambda e: e.tensor_tensor(out=st8[:, 10:11], in0=st8[:, 8:9], in1=st8[:, 9:10], op=ALU.add))
                            op("dve", [st8], [st8], lambda e: e.tensor_scalar(out=st8[:, 10:11], in0=st8[:, 10:11], scalar1=1.0 / D, scalar2=EPS,
                                                                              op0=ALU.mult, op1=ALU.add))
                            op("act", [st8], [st8], lambda e: e.activation(out=st8[:, 10:11], in_=st8[:, 10:11], func=AF.Ln))
                            op("act", [st8], [st8], lambda e: e.activation(out=st8[:, 10:11], in_=st8[:, 10:11], func=AF.Exp, scale=-0.5))
                            op("dve", [u, st8, m], [big], lambda e: e.scalar_tensor_tensor(
                                out=big[:].rearrange("p (a b) -> p a b", a=2), in0=u[:], scalar=st8[:, 10:11],
                                in1=m[:, 2 * D:3 * D].rearrange("p (a b) -> p a b", a=2), op0=ALU.mult, op1=ALU.mult))
                            xo_ = xo.next()
                            op("dve", [big, xt], [xo_], lambda e: e.tensor_tensor(out=xo_[:], in0=big[:], in1=xt[:], op=ALU.add))

                            def store():
                                if last:
                                    dma("pool", dr, xo_, y_out[(ti - NCTX) * 128:(ti - NCTX + 1) * 128, :], xo_[:], sem_buf=xo_)
                                else:
                                    dma("pool", dr, xo_, XC[r0:r0 + 128, :], xo_[:], sem_buf=xo_)
                            c["store"] = store

                        def o_s5(c):
                            c["store"]()

                        run_skewed([o_s0, o_s1, o_s2, o_s3, o_s4, o_s5], [dict(ti=ti) for ti in tiles], name="O")
                        fw.barrier()
                        fw.release(wst.bufs + oa[0].bufs + oa[1].bufs + ob[0].bufs + ob[1].bufs + oc.bufs + gz.bufs + xr.bufs + xo.bufs)
                fw.barrier()
                fw.release([LB, OML, GNA, GNB, GB])
        fw.barrier()
    return nc


def make_in_maps(inputs, cores):
    x = np.asarray(inputs["x"], np.float32)
    c = np.asarray(inputs["c"], np.float32)
    ctx = np.asarray(inputs["ctx"], np.float32)
    c_ctx = np.asarray(inputs["c_ctx"], np.float32)
    shared = {
        "w_mod": np.ascontiguousarray(inputs["w_mod"], np.float32),
        "b_mod": np.ascontiguousarray(inputs["b_mod"], np.float32),
        "g_pre": np.ascontiguousarray(inputs["g_pre"], np.float32),
        "g_post": np.ascontiguousarray(inputs["g_post"], np.float32),
        "w_in": np.ascontiguousarray(inputs["w_in"], np.float32),
        "w_out": np.ascontiguousarray(inputs["w_out"], np.float32),
        "hgrn_lb": np.ascontiguousarray(np.asarray(inputs["hgrn_lb"], np.float32).reshape(DEPTH, 512)),
        "hgrn_gn": np.ascontiguousarray(inputs["hgrn_gn"], np.float32),
        "gate_b": np.ascontiguousarray(np.asarray(inputs["mlstm_gate_b"], np.float32).reshape(DEPTH, 16)),
        "mlstm_gn": np.ascontiguousarray(inputs["mlstm_gn"], np.float32),
        "rpbt": _rpb_tables(np.asarray(inputs["na_rpb"], np.float32)),
        "rope": _rope_tables().reshape(NT, 128, 384),
        "consts": _consts(),
    }
    maps = []
    for b in cores:
        cT = np.concatenate([c[b].reshape(8, 128).T, c_ctx.reshape(8, 128).T], axis=1)
        m = dict(shared)
        m["x"] = np.ascontiguousarray(x[b])
        m["ctx"] = np.ascontiguousarray(ctx[b])
        m["cT"] = np.ascontiguousarray(cT, np.float32)
        maps.append(m)
    return maps


def kernel(**inputs):
    nc = build()
    maps = make_in_maps(inputs, list(range(8)))
    res = run_bass_kernel_spmd(nc, maps, core_ids=list(range(8)))
    out = np.stack([np.asarray(r["y"], np.float32) for r in res.results], axis=0)
    return out
```

```python
import os
import numpy as np
from contextlib import ExitStack
import concourse.bass as bass
import concourse.mybir as mybir
from concourse.bass_utils import run_bass_kernel_spmd

F32 = mybir.dt.float32
BF16 = mybir.dt.bfloat16
AF = mybir.ActivationFunctionType
ALU = mybir.AluOpType
AX = mybir.AxisListType

D = 1024
NT = 34
NCTX = 2
TOK = NT * 128
PIN = 4752
DEPTH = 2
EPS = 1e-6
NTB = 2944
NTBS = 4352
NTF = 528
NFM = 20
NEG = -30000.0
BWD_ORDER = [1, 0] + list(range(NT - 1, 1, -1))

SEM_CHUNK = 24000


class Buf:
    def __init__(self, t=None, name=""):
        self.t = t
        self.name = name
        self.w = None
        self.r = {}
        self.dsem = None

    def __getitem__(self, k):
        return self.t[k]


class DramTrack:
    def __init__(self):
        self.d = {}

    def get(self, name):
        if name not in self.d:
            self.d[name] = Buf(None, name)
        return self.d[name]


class DmaSem:
    def __init__(self, sem):
        self.sem = sem
        self.n = 0


class FW:
    def __init__(self, nc, es):
        self.nc = nc
        self.es = es
        self.E = {"pe": nc.tensor, "act": nc.scalar, "dve": nc.vector, "pool": nc.gpsimd, "sp": nc.sync}
        self.cnt = {e: 0 for e in self.E}
        self.sems = {e: [] for e in self.E}
        self.seen = {e: {} for e in self.E}
        self.dsems = []
        self.free_dsems = []
        self.bar_sem = es.enter_context(nc.semaphore("barsem"))
        self.bar_n = 0
        self.uid = 0

    def sb(self, es, name, shape, dt):
        self.uid += 1
        t = es.enter_context(self.nc.sbuf_tensor("%s_%d" % (name, self.uid), list(shape), dt))
        return Buf(t, name)

    def ps(self, es, name, shape, dt):
        self.uid += 1
        t = es.enter_context(self.nc.psum_tensor("%s_%d" % (name, self.uid), list(shape), dt))
        return Buf(t, name)

    def get_dsem(self):
        if self.free_dsems:
            return self.free_dsems.pop()
        s = self.es.enter_context(self.nc.semaphore("dsem%d" % len(self.dsems)))
        d = DmaSem(s)
        self.dsems.append(d)
        return d

    def release(self, bufs):
        for b in bufs:
            if b.dsem is not None:
                self.free_dsems.append(b.dsem)
                b.dsem = None

    def _esem(self, e, n):
        k = (n - 1) // SEM_CHUNK
        while len(self.sems[e]) <= k:
            self.sems[e].append(self.es.enter_context(self.nc.semaphore("es_%s_%d" % (e, len(self.sems[e])))))
        return self.sems[e][k], (n - 1) % SEM_CHUNK + 1

    def _wait_tok(self, e, tok):
        key, n = tok
        if key == e and e == "pe":
            return
        if self.seen[e].get(key, 0) >= n:
            return
        self.seen[e][key] = n
        eng = self.E[e]
        if isinstance(key, str):
            sem, val = self._esem(key, n)
            eng.wait_ge(sem, val)
        else:
            eng.wait_ge(key.sem, n)

    def _deps(self, e, reads, writes):
        for b in reads:
            if b.w is not None:
                self._wait_tok(e, b.w)
        for b in writes:
            if b.w is not None:
                self._wait_tok(e, b.w)
            for key, n in list(b.r.items()):
                self._wait_tok(e, (key, n))

    def _mark(self, tok, reads, writes):
        key, n = tok
        for b in writes:
            b.w = tok
            b.r = {}
        for b in reads:
            if b in writes:
                continue
            if b.r.get(key, 0) < n:
                b.r[key] = n

    def op(self, e, reads, writes, fn):
        self._deps(e, reads, writes)
        ins = fn(self.E[e])
        self.cnt[e] += 1
        n = self.cnt[e]
        sem, _ = self._esem(e, n)
        ins.then_inc(sem, 1)
        self._mark((e, n), reads, writes)
        return ins

    def dma(self, q, out_buf, in_buf, out_ap, in_ap, sem_buf=None, **kw):
        if isinstance(out_buf, DramTrack):
            out_buf = out_buf.get(out_ap.name)
        if isinstance(in_buf, DramTrack):
            in_buf = in_buf.get(in_ap.name)
        sb_ = sem_buf if sem_buf is not None else out_buf
        if sb_.dsem is None:
            sb_.dsem = self.get_dsem()
        ds = sb_.dsem
        self._deps(q, [in_buf], [out_buf])
        ins = self.E[q].dma_start(out=out_ap, in_=in_ap, **kw)
        ds.n += 16
        ins.then_inc(ds.sem, 16)
        self._mark((ds, ds.n), [in_buf], [out_buf])
        return ins

    def barrier(self):
        sp = self.E["sp"]
        for e in self.E:
            if e != "sp" and self.cnt[e] > 0:
                self._wait_tok("sp", (e, self.cnt[e]))
        for d in self.dsems:
            if d.n > 0:
                self._wait_tok("sp", (d, d.n))
        self.bar_n += 1
        sp.sem_inc(self.bar_sem, 1)
        for e in self.E:
            if e != "sp":
                self.E[e].wait_ge(self.bar_sem, self.bar_n)
        for e in self.E:
            for e2 in self.E:
                if e2 != e:
                    self.seen[e][e2] = self.cnt[e2]
            for d in self.dsems:
                self.seen[e][d] = d.n


def run_skewed(stages, items, cap=None, name=""):
    n = len(stages)
    if cap is None:
        cap = int(os.environ.get("SKEW_CAP_" + name, os.environ.get("SKEW_CAP", n - 1)))
    cap = min(cap, n - 1)
    for t in range(len(items) + cap):
        for off in range(cap, -1, -1):
            k = t - off
            if 0 <= k < len(items):
                for si in range(n):
                    if min(si, cap) == off:
                        stages[si](items[k])


class Ring:
    def __init__(self, bufs):
        self.bufs = bufs
        self.i = 0

    def next(self):
        b = self.bufs[self.i % len(self.bufs)]
        self.i += 1
        return b


def _consts():
    s = np.arange(128)[:, None]
    t = np.arange(128)[None, :]
    c = np.zeros((128, 6, 128), np.float32)
    c[:, 0] = (s == t)
    c[:, 1] = (s <= t)
    c[:, 2] = (s >= t)
    c[:, 3] = (s > t)
    c[:, 4] = (s < t)
    c[:, 5] = 1.0
    return c.reshape(128, 768)


def _rope_tables():
    tab = np.zeros((NT, 128, 4, 96), np.float32)
    qs = np.float32(96 ** -0.5)
    tab[:NCTX, :, 0, :] = qs
    tab[:NCTX, :, 2, :] = 1.0
    inv = (np.float32(10000.0) ** (-np.arange(0, 48, 2, dtype=np.float32) / np.float32(48))).astype(np.float32)
    T = np.arange(4096)
    pos = [(T // 64).astype(np.float32), (T % 64).astype(np.float32)]
    cos = np.zeros((4096, 2, 2, 24), np.float32)
    sin = np.zeros((4096, 2, 2, 24), np.float32)
    for p in range(2):
        ang = (pos[p][:, None] * inv[None, :]).astype(np.float32)
        cs, sn = np.cos(ang).astype(np.float32), np.sin(ang).astype(np.float32)
        cos[:, p, 0], cos[:, p, 1] = cs, cs
        sin[:, p, 0], sin[:, p, 1] = -sn, sn
    cos = cos.reshape(32, 128, 96)
    sin = sin.reshape(32, 128, 96)
    tab[NCTX:, :, 0] = cos * qs
    tab[NCTX:, :, 1] = sin * qs
    tab[NCTX:, :, 2] = cos
    tab[NCTX:, :, 3] = sin
    return tab


def _rpb_tables(na_rpb):
    L, H = na_rpb.shape[0], na_rpb.shape[1]
    s = np.arange(128)[:, None]
    t = np.arange(128)[None, :]
    kc, qc = s % 64, t % 64
    qr = t // 64
    cs = np.clip(qc - 8, 0, 48)
    band = (kc >= cs) & (kc < cs + 16)
    dc = np.clip(kc - qc + 15, 0, 30)
    out = np.full((L, H, 9, 128, 128), NEG, np.float32)
    for k in range(9):
        o = [-3, -2, -1, 0, 1, 2, 3, -2, 2][k]
        kr = 2 * o + s // 64
        diff = kr - qr
        ok = band & (diff >= -7) & (diff <= 7)
        if k >= 7:
            ok = ok & (diff >= -4) & (diff <= 3)
        dr = np.clip(diff + 7, 0, 14)
        for l in range(L):
            for h in range(H):
                g = na_rpb[l, h][dr, dc]
                out[l, h, k] = np.where(ok, g, np.float32(NEG))
    return out.reshape(L, H * 9, 128, 128)


def _key_tiles(lt):
    if lt <= 1:
        return [(kt, kt - lt + 3) for kt in range(0, 4)]
    if lt >= 30:
        return [(kt, kt - lt + 3) for kt in range(28, 32)]
    res = []
    for o in range(-2, 3):
        k = o + 3
        if o == -2:
            k = 7
        if o == 2:
            k = 8
        res.append((lt + o, k))
    return res


def build(debug=(), nlayers=DEPTH, phases="PABCO"):
    nc = bass.Bass("TRN2", target_bir_lowering=False)

    def din(name, shape, dt=F32):
        return nc.dram_tensor(name, list(shape), dt, kind="ExternalInput").ap()

    def dscr(name, shape, dt=F32):
        kind = "ExternalOutput" if name in debug else "Internal"
        return nc.dram_tensor(name, list(shape), dt, kind=kind).ap()

    x_in = din("x", [4096, D])
    ctx_in = din("ctx", [256, D])
    cT_in = din("cT", [128, 16])
    w_mod = din("w_mod", [DEPTH, D, 3 * D])
    b_mod = din("b_mod", [DEPTH, 3 * D])
    g_pre = din("g_pre", [DEPTH, D])
    g_post = din("g_post", [DEPTH, D])
    w_in = din("w_in", [DEPTH, D, PIN])
    w_out = din("w_out", [DEPTH, D, D])
    hgrn_lb = din("hgrn_lb", [DEPTH, 512])
    hgrn_gn = din("hgrn_gn", [DEPTH, 256])
    gate_b = din("gate_b", [DEPTH, 16])
    mlstm_gn = din("mlstm_gn", [DEPTH, 384])
    rpbt = din("rpbt", [DEPTH, 54, 128, 128])
    rope = din("rope", [NT, 128, 4 * 96])
    consts = din("consts", [128, 768])
    y_out = nc.dram_tensor("y", [4096, D], F32, kind="ExternalOutput").ap()

    XC = dscr("XC", [TOK, D])
    TB = dscr("TB", [TOK, NTB], BF16)
    TF = dscr("TF", [TOK, NTF])
    FM = dscr("FM", [NT, 128, NFM, 128], BF16)
    OA = [dscr("OA%d" % d, [TOK, 256]) for d in range(2)]
    OB = [dscr("OB%d" % d, [TOK, 384]) for d in range(2)]
    OC = dscr("OC", [TOK, 384])
    dr = DramTrack()

    with ExitStack() as es:
        fw = FW(nc, es)
        op, dma = fw.op, fw.dma

        CST = fw.sb(es, "cst", [128, 768], F32)
        dma("sp", CST, dr, CST[:], consts[:, :])
        IDENTF = CST.t[:, 0:128]
        TRI = [CST.t[:, 128:256], CST.t[:, 256:384]]
        TRIS = [CST.t[:, 384:512], CST.t[:, 512:640]]
        ONESF = CST.t[:, 640:768]
        CB = fw.sb(es, "cstb", [128, 384], BF16)
        op("dve", [CST], [CB], lambda e: e.tensor_copy(out=CB[:], in_=CST[:, 0:384]))
        IDENT = CB.t[:, 0:128]
        MASK = [CB.t[:, 128:256], CB.t[:, 256:384]]
        CT = fw.sb(es, "cT", [128, 16], F32)
        dma("sp", CT, dr, CT[:], cT_in[:, :])
        SC = fw.sb(es, "sc", [128, 16], F32)
        op("act", [CT], [SC], lambda e: e.activation(out=SC[:], in_=CT[:], func=AF.Exp, scale=-1.0))
        op("act", [SC], [SC], lambda e: e.activation(out=SC[:], in_=SC[:], func=AF.Ln, bias=1.0))
        op("act", [SC], [SC], lambda e: e.activation(out=SC[:], in_=SC[:], func=AF.Exp, scale=-1.0))
        op("dve", [SC, CT], [SC], lambda e: e.tensor_tensor(out=SC[:], in0=SC[:], in1=CT[:], op=ALU.mult))
        fw.barrier()

        def x_src(l, ti):
            if l == 0:
                if ti < NCTX:
                    return ctx_in[ti * 128:(ti + 1) * 128, :]
                return x_in[(ti - NCTX) * 128:(ti - NCTX + 1) * 128, :]
            return XC[ti * 128:(ti + 1) * 128, :]

        for l in range(nlayers):
            last = (l == DEPTH - 1)
            with ExitStack() as les:
                MODS = [fw.sb(les, "mod%d" % i, [128, 3 * D], F32) for i in range(2)]
                LB = fw.sb(les, "lb", [128, 512], F32)
                OML = fw.sb(les, "oml", [128, 512], F32)
                GNA = fw.sb(les, "gna", [128, 256], F32)
                GNB = fw.sb(les, "gnb", [128, 384], F32)
                GB = fw.sb(les, "gb", [128, 16], F32)
                dma("sp", GNA, dr, GNA[:], hgrn_gn[l:l + 1, :].partition_broadcast(128))
                dma("sp", GNB, dr, GNB[:], mlstm_gn[l:l + 1, :].partition_broadcast(128))
                dma("sp", GB, dr, GB[:], gate_b[l:l + 1, :].partition_broadcast(128))
                if l == 0:
                    op("dve", [], [LB], lambda e: e.memset(LB[:], 0.0))
                else:
                    dma("sp", LB, dr, LB[:], hgrn_lb[1:2, :].partition_broadcast(128))
                    dma("sp", OML, dr, OML[:], hgrn_lb[0:1, :].partition_broadcast(128))
                    op("dve", [LB, OML], [LB], lambda e: e.tensor_sub(out=LB[:], in0=LB[:], in1=OML[:]))
                    op("act", [LB], [LB], lambda e: e.activation(out=LB[:], in_=LB[:], func=AF.Exp, scale=-1.0))
                    op("act", [LB], [LB], lambda e: e.activation(out=LB[:], in_=LB[:], func=AF.Ln, bias=1.0))
                    op("act", [LB], [LB], lambda e: e.activation(out=LB[:], in_=LB[:], func=AF.Exp, scale=-1.0))
                op("dve", [LB], [OML], lambda e: e.tensor_scalar(out=OML[:], in0=LB[:], scalar1=-1.0, scalar2=1.0,
                                                                  op0=ALU.mult, op1=ALU.add))
                win_ = ExitStack()
                if "P" in phases:
                    WIN = fw.sb(win_, "win", [128, 8, PIN], BF16)
                    wst = Ring([fw.sb(win_, "wst%d" % i, [128, 1188], F32) for i in range(3)])

                    def _mk_win(k):
                        kc, q4 = k // 4, k % 4

                        def f():
                            w = wst.next()
                            dma("pool", w, dr, w[:], w_in[l, kc * 128:(kc + 1) * 128, q4 * 1188:(q4 + 1) * 1188])
                            if k % 4 == 0:
                                op("dve", [w], [WIN], lambda e: e.tensor_copy(out=WIN[:, kc, q4 * 1188:(q4 + 1) * 1188], in_=w[:]))
                            else:
                                op("act", [w], [WIN], lambda e: e.copy(out=WIN[:, kc, q4 * 1188:(q4 + 1) * 1188], in_=w[:]))
                        return f
                    win_pre = [_mk_win(k) for k in range(32)]
                with ExitStack() as ms:
                    BM = fw.sb(ms, "bm", [128, 3 * D], F32)
                    GPRE = fw.sb(ms, "gpre", [128, D], F32)
                    GPOST = fw.sb(ms, "gpost", [128, D], F32)
                    dma("sp", BM, dr, BM[:], b_mod[l:l + 1, :].partition_broadcast(128))
                    dma("sp", GPRE, dr, GPRE[:], g_pre[l:l + 1, :].partition_broadcast(128))
                    dma("sp", GPOST, dr, GPOST[:], g_post[l:l + 1, :].partition_broadcast(128))
                    wm = Ring([fw.sb(ms, "wm%d" % i, [128, 8, 512], F32) for i in range(2)])
                    pmod = Ring([fw.ps(ms, "pmod%d" % i, [128, 512], F32) for i in range(4)])
                    for nb in range(6):
                        w = wm.next()
                        dma("sp", w, dr, w[:], w_mod[l, :, nb * 512:(nb + 1) * 512].rearrange("(kc p) n -> p kc n", p=128))
                        for st in range(2):
                            pm = pmod.next()
                            for kc in range(8):
                                op("pe", [SC, w], [pm], lambda e: e.matmul(
                                    pm[:], lhsT=SC[:, st * 8 + kc:st * 8 + kc + 1].to_broadcast([128, 128]),
                                    rhs=w[:, kc, :], start=(kc == 0), stop=(kc == 7)))
                            m = MODS[st]
                            op("dve", [pm, BM], [m], lambda e: e.tensor_tensor(
                                out=m[:, nb * 512:(nb + 1) * 512], in0=pm[:], in1=BM[:, nb * 512:(nb + 1) * 512], op=ALU.add))
                            for _ in range(3):
                                if "P" in phases and win_pre:
                                    win_pre.pop(0)()
                    for st in range(2):
                        m = MODS[st]
                        op("dve", [m, GPRE], [m], lambda e: e.scalar_tensor_tensor(
                            out=m[:, D:2 * D], in0=m[:, D:2 * D], scalar=1.0, in1=GPRE[:], op0=ALU.add, op1=ALU.mult))
                        op("dve", [m, GPOST], [m], lambda e: e.tensor_tensor(
                            out=m[:, 2 * D:3 * D], in0=m[:, 2 * D:3 * D], in1=GPOST[:], op=ALU.mult))
                    while "P" in phases and win_pre:
                        win_pre.pop(0)()
                    fw.barrier()
                    fw.release(wm.bufs + [BM, GPRE, GPOST])

                if "P" in phases:
                    with ExitStack() as ps_:
                        xr = Ring([fw.sb(ps_, "x%d" % i, [128, D], F32) for i in range(2)])
                        rp = Ring([fw.sb(ps_, "rope%d" % i, [128, 4, 96], F32) for i in range(2)])
                        sqr = Ring([fw.sb(ps_, "sq%d" % i, [128, D], F32) for i in range(2)])
                        ssr = Ring([fw.sb(ps_, "ss%d" % i, [128, 4], F32) for i in range(3)])
                        hbr = Ring([fw.sb(ps_, "hb%d" % i, [128, D], BF16) for i in range(2)])
                        hT = Ring([fw.sb(ps_, "hT%d" % i, [128, 8, 128], BF16) for i in range(2)])
                        phT = fw.ps(ps_, "phT", [128, 8, 128], BF16)
                        pj = Ring([fw.ps(ps_, "pj%d" % i, [128, 512], F32) for i in range(5)])
                        ptt = Ring([fw.ps(ps_, "ptt%d" % i, [128, 8, 128], BF16) for i in range(2)])
                        tbs = Ring([fw.sb(ps_, "tbs%d" % i, [128, NTBS], BF16) for i in range(2)])
                        tfs = Ring([fw.sb(ps_, "tfs%d" % i, [128, NTF], F32) for i in range(2)])
                        fms = Ring([fw.sb(ps_, "fms%d" % i, [128, NFM, 128], BF16) for i in range(2)])
                        tmpr = Ring([fw.sb(ps_, "ptmp%d" % i, [128, 512], F32) for i in range(6)])

                        def proj(ps, c0, wd, hTb):
                            for kc in range(8):
                                op("pe", [hTb, WIN], [ps], lambda e: e.matmul(
                                    ps[:, 0:wd], lhsT=hTb[:, kc, :], rhs=WIN[:, kc, c0:c0 + wd], start=(kc == 0), stop=(kc == 7)))

                        def sigm(src_buf, src_ap, wd):
                            a = tmpr.next(); b = tmpr.next()
                            op("act", [src_buf], [a], lambda e: e.activation(out=a[:, 0:wd], in_=src_ap, func=AF.Exp, scale=-1.0))
                            op("act", [a], [b], lambda e: e.activation(out=b[:, 0:wd], in_=a[:, 0:wd], func=AF.Ln, bias=1.0))
                            op("act", [b], [a], lambda e: e.activation(out=a[:, 0:wd], in_=b[:, 0:wd], func=AF.Exp, scale=-1.0))
                            return a, b

                        PST = {}

                        def prologue(ti):
                            st = 1 if ti < NCTX else 0
                            m = MODS[st]
                            xt = xr.next()
                            dma("sp", xt, dr, xt[:], x_src(l, ti))
                            rt = rp.next()
                            dma("sp", rt, dr, rt[:], rope[ti].rearrange("p (a b) -> p a b", a=4))
                            sq = sqr.next(); ss = ssr.next(); hb = hbr.next()
                            op("act", [xt], [sq, ss], lambda e: e.activation(out=sq[:], in_=xt[:], func=AF.Square, accum_out=ss[:, 0:1]))
                            op("dve", [ss], [ss], lambda e: e.tensor_scalar(out=ss[:, 1:2], in0=ss[:, 0:1], scalar1=1.0 / D, scalar2=EPS,
                                                                            op0=ALU.mult, op1=ALU.add))
                            op("act", [ss], [ss], lambda e: e.activation(out=ss[:, 2:3], in_=ss[:, 1:2], func=AF.Ln))
                            op("act", [ss], [ss], lambda e: e.activation(out=ss[:, 3:4], in_=ss[:, 2:3], func=AF.Exp, scale=-0.5))
                            op("dve", [xt, ss, m], [sq], lambda e: e.scalar_tensor_tensor(
                                out=sq[:], in0=xt[:], scalar=ss[:, 3:4], in1=m[:, D:2 * D], op0=ALU.mult, op1=ALU.mult))
                            op("dve", [sq, m], [hb], lambda e: e.tensor_tensor(out=hb[:], in0=sq[:], in1=m[:, 0:D], op=ALU.add))
                            for kc in range(8):
                                op("pe", [hb, CB], [phT], lambda e: e.transpose(out=phT[:, kc, :], in_=hb[:, kc * 128:(kc + 1) * 128], identity=IDENT))
                            hTb = hT.next()
                            op("act", [phT], [hTb], lambda e: e.copy(out=hTb[:], in_=phT[:]))
                            PST[ti] = dict(hTb=hTb, rt=rt)

                        def groups_a(ti):
                            hTb, rt = PST[ti]["hTb"], PST[ti]["rt"]
                            tb = tbs.next()
                            tf = tfs.next()
                            PST[ti].update(tb=tb, tf=tf)
                            ps = pj.next(); proj(ps, 256, 512, hTb)
                            sg, sp_ = sigm(ps, ps[:], 512)
                            if l == 0:
                                op("dve", [sp_], [tf], lambda e: e.tensor_scalar_mul(out=tf[:, 0:512], in0=sp_[:], scalar1=-1.0))
                            else:
                                op("dve", [sg, OML], [sg], lambda e: e.tensor_tensor(out=sg[:], in0=sg[:], in1=OML[:], op=ALU.mult))
                                op("dve", [sg, LB], [sg], lambda e: e.tensor_tensor(out=sg[:], in0=sg[:], in1=LB[:], op=ALU.add))
                                op("act", [sg], [tf], lambda e: e.activation(out=tf[:, 0:512], in_=sg[:], func=AF.Ln))
                            op("dve", [sg], [tb], lambda e: e.tensor_scalar(out=tb[:, 0:512], in0=sg[:], scalar1=-1.0, scalar2=1.0,
                                                                           op0=ALU.mult, op1=ALU.add))
                            ps = pj.next(); proj(ps, 0, 256, hTb)
                            sg, _ = sigm(ps, ps[:, 0:256], 256)
                            op("dve", [ps, sg], [tb], lambda e: e.tensor_tensor(out=tb[:, 2944:3200], in0=ps[:, 0:256], in1=sg[:, 0:256], op=ALU.mult))
                            ps = pj.next(); proj(ps, 768, 512, hTb)
                            op("dve", [ps], [tb], lambda e: e.tensor_copy(out=tb[:, 512:768], in_=ps[:, 0:256]))
                            sg, _ = sigm(ps, ps[:, 256:512], 256)
                            op("dve", [ps, sg], [tb], lambda e: e.tensor_tensor(out=tb[:, 768:1024], in0=ps[:, 256:512], in1=sg[:, 0:256], op=ALU.mult))
                            for (c0, dst, ci) in ((1280, 3200, 0), (1664, 1024, 2)):
                                ps = pj.next(); proj(ps, c0, 384, hTb)
                                t2 = tmpr.next(); t3 = tmpr.next()
                                pv = ps.t[:, 0:384].rearrange("p (h a b j) -> p h a b j", h=4, a=2, b=2)
                                t2v = t2.t[:, 0:384].rearrange("p (h a b j) -> p h a b j", h=4, a=2, b=2)
                                cosb = rt.t[:, ci, :].unsqueeze(1).to_broadcast([128, 4, 96])
                                sinv = rt.t[:, ci + 1, :].rearrange("p (a b j) -> p a b j", a=2, b=2)
                                op("dve", [ps, rt], [t3], lambda e: e.tensor_tensor(
                                    out=t3[:, 0:384].rearrange("p (h d) -> p h d", h=4), in0=ps[:, 0:384].rearrange("p (h d) -> p h d", h=4),
                                    in1=cosb, op=ALU.mult))
                                for b_ in range(2):
                                    op("dve", [ps, rt], [t2], lambda e: e.tensor_tensor(
                                        out=t2v[:, :, :, b_, :], in0=pv[:, :, :, 1 - b_, :],
                                        in1=sinv[:, :, b_, :].unsqueeze(1).to_broadcast([128, 4, 2, 24]), op=ALU.mult))
                                op("pool", [t3, t2], [tb], lambda e: e.tensor_tensor(out=tb[:, dst:dst + 384], in0=t3[:, 0:384], in1=t2[:, 0:384], op=ALU.add))

                        def groups_b(ti):
                            hTb, rt, tb, tf = PST[ti]["hTb"], PST[ti]["rt"], PST[ti]["tb"], PST[ti]["tf"]
                            ps = pj.next(); proj(ps, 2048, 384, hTb)
                            op("act", [ps], [tb], lambda e: e.copy(out=tb[:, 1408:1792], in_=ps[:, 0:384]))
                            ps = pj.next(); proj(ps, 2432, 384, hTb)
                            sgo, _ = sigm(ps, ps[:, 0:384], 384)
                            ps = pj.next(); proj(ps, 2816, 384, hTb)
                            sgz, _ = sigm(ps, ps[:, 0:384], 384)
                            op("dve", [ps, sgz], [sgz], lambda e: e.tensor_tensor(out=sgz[:, 0:384], in0=ps[:, 0:384], in1=sgz[:, 0:384], op=ALU.mult))
                            op("pool", [sgo, sgz], [tb], lambda e: e.tensor_tensor(out=tb[:, 1792:2176], in0=sgo[:, 0:384], in1=sgz[:, 0:384], op=ALU.mult))
                            ps = pj.next(); proj(ps, 3200, 16, hTb)
                            op("dve", [ps, GB], [tf], lambda e: e.tensor_tensor(out=tf[:, 512:528], in0=ps[:, 0:16], in1=GB[:], op=ALU.add))
                            _, spg = sigm(tf, tf[:, 520:528], 8)
                            op("dve", [spg], [tf], lambda e: e.tensor_scalar_mul(out=tf[:, 520:528], in0=spg[:, 0:8], scalar1=-1.0))
                            ps = pj.next(); proj(ps, 3216, 384, hTb)
                            op("act", [ps], [tb], lambda e: e.activation(out=tb[:, 3584:3968], in_=ps[:, 0:384], func=AF.Copy, scale=0.125))
                            ps = pj.next(); proj(ps, 3600, 384, hTb)
                            op("dve", [ps], [tb], lambda e: e.tensor_copy(out=tb[:, 3968:4352], in_=ps[:, 0:384]))
                            ps = pj.next(); proj(ps, 3984, 384, hTb)
                            op("act", [ps], [tb], lambda e: e.copy(out=tb[:, 2176:2560], in_=ps[:, 0:384]))
                            ps = pj.next(); proj(ps, 4368, 384, hTb)
                            sg, _ = sigm(ps, ps[:, 0:384], 384)
                            op("dve", [ps, sg], [tb], lambda e: e.tensor_tensor(out=tb[:, 2560:2944], in0=ps[:, 0:384], in1=sg[:, 0:384], op=ALU.mult))

                        def tail(ti):
                            tb, tf = PST[ti]["tb"], PST[ti]["tf"]
                            fm = fms.next()
                            srcA = [2944, 3072, 0, 128, 256, 384]
                            srcC = [3584, 3712, 3840, 3968, 4096, 4224]
                            pt_ = ptt.next()
                            for j, c0 in enumerate(srcA):
                                op("pe", [tb, CB], [pt_], lambda e: e.transpose(out=pt_[:, j, :], in_=tb[:, c0:c0 + 128], identity=IDENT))
                            op("dve", [pt_], [fm], lambda e: e.tensor_copy(out=fm[:, 0:6, :], in_=pt_[:, 0:6, :]))
                            pt_ = ptt.next()
                            for j, c0 in enumerate(srcC):
                                op("pe", [tb, CB], [pt_], lambda e: e.transpose(out=pt_[:, j, :], in_=tb[:, c0:c0 + 128], identity=IDENT))
                            op("act", [pt_], [fm], lambda e: e.copy(out=fm[:, 6:12, :], in_=pt_[:, 0:6, :]))
                            pt_ = ptt.next()
                            for j in range(8):
                                c0 = (3200 if j < 4 else 1024) + (j % 4) * 96
                                op("pe", [tb, CB], [pt_], lambda e: e.transpose(out=pt_[0:96, j, :], in_=tb[:, c0:c0 + 96], identity=IDENT))
                            op("dve", [pt_], [fm], lambda e: e.tensor_copy(out=fm[0:96, 12:20, :], in_=pt_[0:96, 0:8, :]))
                            del PST[ti]

                            def stores():
                                dma("pool", dr, tb, TB[ti * 128:(ti + 1) * 128, :], tb[:, 0:NTB], sem_buf=tb)
                                dma("pool", dr, tf, TF[ti * 128:(ti + 1) * 128, :], tf[:], sem_buf=tf)
                                dma("pool", dr, fm, FM[ti], fm[:], sem_buf=fm)
                            return stores

                        prologue(0)
                        for ti in range(NT):
                            if ti + 1 < NT:
                                prologue(ti + 1)
                            groups_a(ti)
                            pst = tail(ti - 1) if ti > 0 else None
                            groups_b(ti)
                            if pst is not None:
                                pst()
                        tail(NT - 1)()
                        fw.barrier()
                        fw.release(xr.bufs + rp.bufs + tbs.bufs + tfs.bufs + fms.bufs)
                if "P" in phases:
                    fw.release(wst.bufs)
                win_.close()

                if "A" in phases:
                    with ExitStack() as as_:
                        LA = int(os.environ.get("LA_A", 2))

                        def ring(name, shape, dt, n=None, ps=False):
                            if n is None:
                                n = LA + 1
                            return Ring([(fw.ps if ps else fw.sb)(as_, "%s%d" % (name, i), shape, dt) for i in range(n)])
                        St = []
                        for d in range(2):
                            S = fw.sb(as_, "S%d" % d, [128, 2, 64], F32)
                            Sb = fw.sb(as_, "Sb%d" % d, [128, 2, 128], BF16)
                            op("dve", [], [S], lambda e: e.memset(S[:], 0.0))
                            op("dve", [], [Sb], lambda e: e.memset(Sb[:], 0.0))
                            R_ = dict(S=S, Sb=Sb,
                                      lf=ring("alf%d" % d, [128, 256], F32), qT=ring("aqT%d" % d, [128, 2, 128], BF16),
                                      kT=ring("akT%d" % d, [128, 2, 128], BF16), kt=ring("akt%d" % d, [128, 256], BF16),
                                      vt=ring("avt%d" % d, [128, 256], BF16), ost=ring("aos%d" % d, [128, 256], F32, 3),
                                      bT=ring("abT%d" % d, [128, 2, 128], F32), E1=ring("aE1%d" % d, [128, 2, 128], F32),
                                      Es=ring("aEs%d" % d, [128, 2, 128], F32), tmp=ring("atm%d" % d, [128, 2, 128], F32),
                                      rr=ring("arr%d" % d, [128, 2, 8], F32), edk=ring("aed%d" % d, [128, 256], F32),
                                      kd=ring("akd%d" % d, [128, 256], BF16), Qf=ring("aQf%d" % d, [128, 2, 128], BF16),
                                      Qs=ring("aQs%d" % d, [128, 2, 2, 128], BF16), ATm=ring("aAT%d" % d, [128, 4, 128], BF16),
                                      Ek=[ring("aEk%d_%d" % (d, i), [128, 2, 128], F32) for i in range(4)],
                                      Kv=[ring("aKv%d_%d" % (d, i), [128, 2, 128], BF16) for i in range(4)])
                            for i in range(4):
                                for b in R_["Kv"][i].bufs:
                                    op("dve", [], [b], lambda e: e.memset(b[:], 0.0))
                            for b in R_["rr"].bufs + R_["Qs"].bufs:
                                op("dve", [], [b], lambda e: e.memset(b[:], 0.0))
                            St.append(R_)
                        pbT = ring("apbT", [128, 2, 128], F32, 1, ps=True)
                        prs = ring("aprs", [128, 256], F32, 1, ps=True)
                        pAT = ring("apAT", [128, 4, 128], F32, 2, ps=True)
                        pO = ring("apO", [128, 256], F32, 2, ps=True)
                        pU = ring("apU", [128, 2, 128], F32, 1, ps=True)

                        def a_s0(c):
                            ti, d = c["ti"], c["d"]
                            R_ = St[d]
                            lf = R_["lf"].next(); qT = R_["qT"].next(); kT = R_["kT"].next(); kt = R_["kt"].next(); vt = R_["vt"].next()
                            r0 = ti * 128
                            dma("sp", lf, dr, lf[:], TF[r0:r0 + 128, d * 256:(d + 1) * 256])
                            dma("sp", qT, dr, qT[:], FM[ti, :, 0:2, :])
                            dma("sp", kT, dr, kT[:], FM[ti, :, 2 + 2 * d:4 + 2 * d, :])
                            dma("sp", kt, dr, kt[:], TB[r0:r0 + 128, d * 256:(d + 1) * 256])
                            dma("sp", vt, dr, vt[:], TB[r0:r0 + 128, 512:768])
                            pb = pbT.next()
                            for pt in range(2):
                                op("pe", [lf, CST], [pb], lambda e: e.matmul(pb[:, pt, :], lhsT=lf[:, pt * 128:(pt + 1) * 128], rhs=TRI[d], start=True, stop=True))
                            bT = R_["bT"].next()
                            op("act", [pb], [bT], lambda e: e.copy(out=bT[:], in_=pb[:]))
                            pr = prs.next()
                            op("pe", [lf, CST], [pr], lambda e: e.matmul(pr[:], lhsT=TRIS[d], rhs=lf[:], start=True, stop=True))
                            edk = R_["edk"].next()
                            op("act", [pr], [edk], lambda e: e.activation(out=edk[:], in_=pr[:], func=AF.Exp))
                            c.update(qT=qT, kT=kT, kt=kt, vt=vt, bT=bT, edk=edk)

                        def a_s1(c):
                            d, bT, edk, kt = c["d"], c["bT"], c["edk"], c["kt"]
                            R_ = St[d]
                            kd = R_["kd"].next()
                            op("pool", [edk, kt], [kd], lambda e: e.tensor_tensor(out=kd[:], in0=edk[:], in1=kt[:], op=ALU.mult))
                            E1 = R_["E1"].next()
                            op("act", [bT], [E1], lambda e: e.activation(out=E1[:], in_=bT[:], func=AF.Exp))
                            rr = R_["rr"].next()
                            if d == 0:
                                src = bT.t[:, :, 31:127:32]; dn = rr.t[:, :, 1:4]; dp = rr.t[:, :, 5:8]
                            else:
                                src = bT.t[:, :, 32:128:32]; dn = rr.t[:, :, 0:3]; dp = rr.t[:, :, 4:7]
                            op("dve", [bT], [rr], lambda e: e.tensor_scalar_mul(out=dn, in0=src, scalar1=-1.0))
                            op("dve", [bT], [rr], lambda e: e.tensor_copy(out=dp, in_=src))
                            tmp = R_["tmp"].next()
                            op("dve", [bT, rr], [tmp], lambda e: e.tensor_tensor(
                                out=tmp[:].rearrange("p a (i j) -> p a i j", i=4), in0=bT[:].rearrange("p a (i j) -> p a i j", i=4),
                                in1=rr.t[:, :, 0:4].unsqueeze(3).to_broadcast([128, 2, 4, 32]), op=ALU.add))
                            c.update(kd=kd, E1=E1, rr=rr, tmp=tmp)

                        def a_s2(c):
                            d, bT, E1, rr, tmp, qT = c["d"], c["bT"], c["E1"], c["rr"], c["tmp"], c["qT"]
                            R_ = St[d]
                            Qf = R_["Qf"].next()
                            op("pool", [E1, qT], [Qf], lambda e: e.tensor_tensor(out=Qf[:], in0=E1[:], in1=qT[:], op=ALU.mult))
                            Es = R_["Es"].next()
                            op("act", [tmp], [Es], lambda e: e.activation(out=Es[:], in_=tmp[:], func=AF.Exp))
                            Eks = []
                            for i in range(4):
                                lo, hi = (0, 32 * (i + 1)) if d == 0 else (32 * i, 128)
                                Ek = R_["Ek"][i].next()
                                for pt in range(2):
                                    op("act", [bT, rr], [Ek], lambda e: e.activation(
                                        out=Ek[:, pt, lo:hi], in_=bT[:, pt, lo:hi], func=AF.Exp, bias=rr[:, pt, 4 + i:5 + i], scale=-1.0))
                                Eks.append(Ek)
                            c.update(Qf=Qf, Es=Es, Eks=Eks)

                        def a_s3(c):
                            d, Es, Eks, qT, kT = c["d"], c["Es"], c["Eks"], c["qT"], c["kT"]
                            R_ = St[d]
                            Qs = R_["Qs"].next()
                            for hl in range(2):
                                op("dve", [Es, qT], [Qs], lambda e: e.tensor_tensor(
                                    out=Qs[64 * hl:64 * hl + 64, hl, :, :], in0=Es[64 * hl:64 * hl + 64, :, :], in1=qT[64 * hl:64 * hl + 64, :, :], op=ALU.mult))
                            Kvs = []
                            for i in range(4):
                                lo, hi = (0, 32 * (i + 1)) if d == 0 else (32 * i, 128)
                                Ek = Eks[i]
                                Kv = R_["Kv"][i].next()
                                op("pool" if i % 2 == 1 else "dve", [Ek, kT], [Kv],
                                   lambda e: e.tensor_tensor(out=Kv[:, :, lo:hi], in0=Ek[:, :, lo:hi], in1=kT[:, :, lo:hi], op=ALU.mult))
                                Kvs.append(Kv)
                            c.update(Qs=Qs, Kvs=Kvs)

                        def a_s4(c):
                            d, Kvs, Qs = c["d"], c["Kvs"], c["Qs"]
                            R_ = St[d]
                            pa = pAT.next()
                            for h in range(4):
                                pt = h // 2
                                for i in range(4):
                                    Kv = Kvs[i]
                                    op("pe", [Kv, Qs], [pa], lambda e: e.matmul(
                                        pa[:, h, 32 * i:32 * i + 32], lhsT=Kv[:, pt, :], rhs=Qs[:, h % 2, pt, 32 * i:32 * i + 32],
                                        start=True, stop=True))
                            ATm = R_["ATm"].next()
                            op("dve", [pa, CB], [ATm], lambda e: e.tensor_tensor(
                                out=ATm[:], in0=pa[:], in1=MASK[d].unsqueeze(1).to_broadcast([128, 4, 128]), op=ALU.mult))
                            c["ATm"] = ATm

                        def a_s5(c):
                            ti, d, vt, kd, E1, Qf, ATm = c["ti"], c["d"], c["vt"], c["kd"], c["E1"], c["Qf"], c["ATm"]
                            R_ = St[d]
                            S, Sb = R_["S"], R_["Sb"]
                            r0 = ti * 128
                            po = pO.next()
                            for pt in range(2):
                                op("pe", [Qf, Sb], [po], lambda e: e.matmul(po[:, 128 * pt:128 * pt + 128], lhsT=Qf[:, pt, :], rhs=Sb[:, pt, :],
                                                                            start=True, stop=False, skip_group_check=True))
                                for hl in range(2):
                                    h = 2 * pt + hl
                                    op("pe", [ATm, vt], [po], lambda e: e.matmul(po[:, 64 * h:64 * h + 64], lhsT=ATm[:, h, :], rhs=vt[:, 64 * h:64 * h + 64],
                                                                                start=False, stop=(hl == 1), skip_group_check=True))
                            pu = pU.next()
                            for pt in range(2):
                                op("pe", [kd, vt], [pu], lambda e: e.matmul(pu[:, pt, :], lhsT=kd[:, pt * 128:(pt + 1) * 128], rhs=vt[:, pt * 128:(pt + 1) * 128],
                                                                           start=True, stop=True))
                            col = 127 if d == 0 else 0
                            for pt in range(2):
                                for hl in range(2):
                                    bs = 64 * hl
                                    op("dve", [S, E1, pu], [S], lambda e: e.scalar_tensor_tensor(
                                        out=S[bs:bs + 64, pt, :], in0=S[bs:bs + 64, pt, :], scalar=E1[bs:bs + 64, pt, col:col + 1],
                                        in1=pu[bs:bs + 64, pt, bs:bs + 64], op0=ALU.mult, op1=ALU.add))
                            for hl in range(2):
                                bs = 64 * hl
                                op("act", [S], [Sb], lambda e: e.copy(out=Sb[bs:bs + 64, :, bs:bs + 64], in_=S[bs:bs + 64, :, :]))
                            ost = R_["ost"].next()
                            op("act", [po], [ost], lambda e: e.copy(out=ost[:], in_=po[:]))
                            c["store"] = lambda: dma("pool", dr, ost, OA[d][r0:r0 + 128, :], ost[:], sem_buf=ost)

                        def a_s6(c):
                            c["store"]()

                        seq = []
                        for j in range(NT):
                            seq += [dict(ti=j, d=0), dict(ti=BWD_ORDER[j], d=1)]
                        run_skewed([a_s0, a_s1, a_s2, a_s3, a_s4, a_s5, a_s6], seq, name="A")
                        fw.barrier()
                        for R_ in St:
                            for k_ in ("lf", "qT", "kT", "kt", "vt", "ost"):
                                fw.release(R_[k_].bufs)

                bc_ = ExitStack()
                c_pre = []
                if "C" in phases:
                    KT = fw.sb(bc_, "cKT", [128, NT, 3, 128], BF16)
                    VX = fw.sb(bc_, "cVX", [128, NT, 6, 65], BF16)
                    BIAS = fw.sb(bc_, "cBias", [128, 54, 128], BF16)
                    bst = Ring([fw.sb(bc_, "cbst%d" % i, [128, 6, 128], F32) for i in range(2)])
                    op("pool", [], [VX], lambda e: e.memset(VX[:, :, :, 64:65], 1.0))

                    def _mk_bias(g_):
                        def f():
                            b = bst.next()
                            dma("sp", b, dr, b[:], rpbt[l, g_ * 6:(g_ + 1) * 6].rearrange("k s t -> s k t"))
                            op("act", [b], [BIAS], lambda e: e.copy(out=BIAS[:, g_ * 6:(g_ + 1) * 6, :], in_=b[:]))
                        return f

                    def _mk_kv(ti):
                        def f():
                            dma("sp", KT, dr, KT[:, ti, :, :], FM[ti, :, 9:12, :])
                            dma("sp", VX, dr, VX[:, ti, :, 0:64],
                                TB[ti * 128:(ti + 1) * 128, 2176:2560].rearrange("p (h d) -> p h d", h=6))
                        return f
                    c_pre = [_mk_bias(g_) for g_ in range(9)] + [_mk_kv(ti) for ti in range(NT)]

                if "B" in phases:
                    with ExitStack() as bs_:
                        def ring(name, shape, dt, n=3, ps=False):
                            return Ring([(fw.ps if ps else fw.sb)(bs_, "%s%d" % (name, i), shape, dt) for i in range(n)])
                        St = []
                        for d in range(2):
                            C = fw.sb(bs_, "C%d" % d, [128, 4, 97], F32)
                            Cb = fw.sb(bs_, "Cb%d" % d, [128, 4, 97], BF16)
                            op("dve", [], [C], lambda e: e.memset(C[:], 0.0))
                            op("dve", [], [Cb], lambda e: e.memset(Cb[:], 0.0))
                            R_ = dict(C=C, Cb=Cb, g=ring("bg%d" % d, [128, 16], F32), qT=ring("bqT%d" % d, [128, 4, 128], BF16),
                                      kT=ring("bkT%d" % d, [128, 4, 128], BF16), kt=ring("bkt%d" % d, [128, 384], BF16),
                                      vx=ring("bvx%d" % d, [128, 4, 97], BF16), X=ring("bX%d" % d, [128, 16], F32),
                                      E=ring("bE%d" % d, [128, 16], F32), kh=ring("bkh%d" % d, [128, 384], BF16),
                                      ST=ring("bST%d" % d, [128, 4, 128], BF16), u=ring("bu%d" % d, [128, 4, 97], F32),
                                      dd=ring("bdd%d" % d, [128, 8], F32), ost=ring("bos%d" % d, [128, 384], F32, 3))
                            for b in R_["vx"].bufs:
                                op("dve", [], [b], lambda e: e.memset(b[:], 1.0))
                            St.append(R_)
                        pg = ring("bpg", [128, 16], F32, 2, ps=True)
                        psc = ring("bpsc", [128, 4, 128], F32, 2, ps=True)
                        pout = ring("bpout", [128, 4, 128], F32, 2, ps=True)
                        pdu = ring("bpdu", [128, 4, 128], F32, 1, ps=True)

                        def b_s0(c):
                            ti, d = c["ti"], c["d"]
                            R_ = St[d]
                            if c_pre:
                                c_pre.pop(0)()
                            g = R_["g"].next(); qT = R_["qT"].next(); kT = R_["kT"].next(); kt = R_["kt"].next(); vx = R_["vx"].next()
                            r0 = ti * 128
                            dma("sp", g, dr, g[:], TF[r0:r0 + 128, 512:528])
                            dma("sp", qT, dr, qT[:], FM[ti, :, 12:16, :])
                            dma("sp", kT, dr, kT[:], FM[ti, :, 16:20, :])
                            dma("sp", kt, dr, kt[:], TB[r0:r0 + 128, 1024:1408])
                            dma("sp", vx, dr, vx[:, :, 0:96], TB[r0:r0 + 128, 1408:1792].rearrange("p (h d) -> p h d", h=4))
                            ig = g.t[:, 4 * d:4 * d + 4]
                            lfd = g.t[:, 8 + 4 * d:12 + 4 * d]
                            p_ = pg.next()
                            op("pe", [g, CST], [p_], lambda e: e.matmul(p_[:, 0:4], lhsT=TRI[d], rhs=lfd, start=True, stop=True))
                            op("pe", [g, CST], [p_], lambda e: e.matmul(p_[:, 4:8], lhsT=TRIS[d], rhs=lfd, start=True, stop=True))
                            op("pe", [g, CST], [p_], lambda e: e.matmul(p_[:, 8:12], lhsT=ONESF, rhs=lfd, start=True, stop=True))
                            sc = psc.next()
                            for h in range(4):
                                op("pe", [kT, qT], [sc], lambda e: e.matmul(sc[:, h, :], lhsT=kT[0:96, h, :], rhs=qT[0:96, h, :], start=True, stop=True))
                            X = R_["X"].next()
                            op("dve", [g, p_], [X], lambda e: e.tensor_tensor(out=X[:, 0:4], in0=ig, in1=p_[:, 0:4], op=ALU.subtract))
                            op("dve", [g, p_], [X], lambda e: e.tensor_tensor(out=X[:, 4:8], in0=ig, in1=p_[:, 4:8], op=ALU.add))
                            op("dve", [p_], [X], lambda e: e.tensor_copy(out=X[:, 8:12], in_=p_[:, 0:4]))
                            op("dve", [p_], [X], lambda e: e.tensor_copy(out=X[:, 12:16], in_=p_[:, 8:12]))
                            c.update(qT=qT, kt=kt, vx=vx, X=X, sc=sc)

                        def b_s1(c):
                            d, kt, X = c["d"], c["kt"], c["X"]
                            R_ = St[d]
                            E = R_["E"].next()
                            op("act", [X], [E], lambda e: e.activation(out=E[:], in_=X[:], func=AF.Exp))
                            kh = R_["kh"].next()
                            op("pool", [kt, E], [kh], lambda e: e.tensor_tensor(
                                out=kh[:].rearrange("p (h d) -> p h d", h=4), in0=kt[:].rearrange("p (h d) -> p h d", h=4),
                                in1=E.t[:, 4:8].unsqueeze(2).to_broadcast([128, 4, 96]), op=ALU.mult))
                            c.update(E=E, kh=kh)

                        def b_s2(c):
                            d, sc, E = c["d"], c["sc"], c["E"]
                            R_ = St[d]
                            ST = R_["ST"].next()
                            for h in range(4):
                                op("dve", [sc, E, CB], [ST], lambda e: e.scalar_tensor_tensor(
                                    out=ST[:, h, :], in0=sc[:, h, :], scalar=E[:, h:h + 1], in1=MASK[d], op0=ALU.mult, op1=ALU.mult))
                            c.update(ST=ST)

                        def b_s3(c):
                            d, qT, vx, E, kh, ST = c["d"], c["qT"], c["vx"], c["E"], c["kh"], c["ST"]
                            R_ = St[d]
                            C, Cb = R_["C"], R_["Cb"]
                            po = pout.next()
                            for h in range(4):
                                op("pe", [ST, vx], [po], lambda e: e.matmul(po[:, h, 0:97], lhsT=ST[:, h, :], rhs=vx[:, h, :], start=True, stop=False))
                                op("pe", [qT, Cb], [po], lambda e: e.matmul(po[:, h, 0:97], lhsT=qT[0:96, h, :], rhs=Cb[0:96, h, :], start=False, stop=True))
                            pd = pdu.next()
                            for h in range(4):
                                op("pe", [kh, vx], [pd], lambda e: e.matmul(pd[0:96, h, 0:97], lhsT=kh[:, 96 * h:96 * h + 96], rhs=vx[:, h, :], start=True, stop=True))
                            op("dve", [C, E], [C], lambda e: e.tensor_tensor(
                                out=C[0:96, :, :], in0=C[0:96, :, :], in1=E.t[0:96, 12:16].unsqueeze(2).to_broadcast([96, 4, 97]), op=ALU.mult))
                            op("dve", [C, pd], [C], lambda e: e.tensor_tensor(out=C[0:96, :, :], in0=C[0:96, :, :], in1=pd[0:96, :, 0:97], op=ALU.add))
                            op("act", [C], [Cb], lambda e: e.copy(out=Cb[0:96, :, :], in_=C[0:96, :, :]))
                            c.update(po=po)

                        def b_s4(c):
                            ti, d, po, E = c["ti"], c["d"], c["po"], c["E"]
                            R_ = St[d]
                            r0 = ti * 128
                            u = R_["u"].next()
                            op("dve", [po, E], [u], lambda e: e.tensor_tensor(
                                out=u[:], in0=po[:, :, 0:97], in1=E.t[:, 8:12].unsqueeze(2).to_broadcast([128, 4, 97]), op=ALU.mult))
                            dd = R_["dd"].next()
                            op("dve", [u], [dd], lambda e: e.tensor_scalar_max(out=dd[:, 0:4], in0=u[:, :, 96], scalar1=1.0))
                            op("dve", [u, dd], [dd], lambda e: e.scalar_tensor_tensor(out=dd[:, 4:8], in0=u[:, :, 96], scalar=-1.0, in1=dd[:, 0:4],
                                                                                     op0=ALU.mult, op1=ALU.max))
                            op("dve", [dd], [dd], lambda e: e.reciprocal(out=dd[:, 0:4], in_=dd[:, 4:8]))
                            ost = R_["ost"].next()
                            op("pool", [u, dd], [ost], lambda e: e.tensor_tensor(
                                out=ost[:].rearrange("p (h d) -> p h d", h=4), in0=u[:, :, 0:96],
                                in1=dd.t[:, 0:4].unsqueeze(2).to_broadcast([128, 4, 96]), op=ALU.mult))
                            c["store"] = lambda: dma("pool", dr, ost, OB[d][r0:r0 + 128, :], ost[:], sem_buf=ost)

                        def b_s5(c):
                            c["store"]()

                        seq = []
                        for j in range(NT):
                            seq += [dict(ti=j, d=0), dict(ti=BWD_ORDER[j], d=1)]
                        run_skewed([b_s0, b_s1, b_s2, b_s3, b_s4, b_s5], seq, name="B")
                        fw.barrier()
                        for R_ in St:
                            for k_ in ("g", "qT", "kT", "kt", "vx", "ost"):
                                fw.release(R_[k_].bufs)

                if "C" in phases:
                    with ExitStack() as cs_:
                        while c_pre:
                            c_pre.pop(0)()
                        qr_ = Ring([fw.sb(cs_, "cq%d" % i, [128, 3, 128], BF16) for i in range(2)])
                        qzr = Ring([fw.sb(cs_, "cqz%d" % i, [128, 6, 128], BF16) for i in range(3)])
                        for b in qzr.bufs:
                            op("dve", [], [b], lambda e: e.memset(b[:], 0.0))
                        PTr = Ring([fw.sb(cs_, "cPT%d" % i, [128, 7, 128], BF16) for i in range(3)])
                        osr = Ring([fw.sb(cs_, "cos%d" % i, [128, 384], F32) for i in range(3)])
                        rrr = Ring([fw.sb(cs_, "crr%d" % i, [128, 6], F32) for i in range(2)])
                        psct = Ring([fw.ps(cs_, "cps%d" % i, [128, 8, 128], F32) for i in range(3)])
                        pcout = Ring([fw.ps(cs_, "cpo%d" % i, [128, 6, 65], F32) for i in range(2)])
                        tiles = list(range(NT)) if not last else list(range(NCTX, NT))
                        TS = {}

                        def c_keys(ti):
                            if ti < NCTX:
                                return [(0, None), (1, None)]
                            return [(NCTX + kt, kind) for kt, kind in _key_tiles(ti - NCTX)] + [(0, None), (1, None)]

                        def c_s0(c):
                            ti, h = c["ti"], c["h"]
                            if h == 0:
                                q = qr_.next()
                                dma("sp", q, dr, q[:], FM[ti, :, 6:9, :])
                                qz = qzr.next()
                                op("pool", [q], [qz], lambda e: e.tensor_copy(out=qz[0:64, 0:6:2, :], in_=q[0:64, :, :]))
                                op("pool", [q], [qz], lambda e: e.tensor_copy(out=qz[64:128, 1:6:2, :], in_=q[64:128, :, :]))
                                TS[ti] = dict(qz=qz, po=pcout.next())
                            qz = TS[ti]["qz"]
                            c["po"] = TS[ti]["po"]
                            keys = c_keys(ti)
                            pt = h // 2
                            sc = psct.next()
                            for j, (kt, kind) in enumerate(keys):
                                op("pe", [KT, qz], [sc], lambda e: e.matmul(sc[:, j, :], lhsT=KT[:, kt, pt, :], rhs=qz[:, h, :],
                                                                           start=True, stop=(kind is None)))
                                if kind is not None:
                                    op("pe", [CB, BIAS], [sc], lambda e: e.matmul(sc[:, j, :], lhsT=IDENT, rhs=BIAS[:, h * 9 + kind, :], start=False, stop=True))
                            c["sc"] = sc

                        def c_s1(c):
                            nk = len(c_keys(c["ti"]))
                            sc = c["sc"]
                            PT = PTr.next()
                            op("act", [sc], [PT], lambda e: e.activation(out=PT[:, 0:nk, :], in_=sc[:, 0:nk, :], func=AF.Exp))
                            c["PT"] = PT

                        def c_s2(c):
                            ti, h, PT, po = c["ti"], c["h"], c["PT"], c["po"]
                            keys = c_keys(ti)
                            nk = len(keys)
                            for j, (kt, kind) in enumerate(keys):
                                op("pe", [PT, VX], [po], lambda e: e.matmul(po[:, h, :], lhsT=PT[:, j, :], rhs=VX[:, kt, h, :], start=(j == 0), stop=(j == nk - 1)))
                            if h == 5:
                                rr = rrr.next()
                                op("dve", [po], [rr], lambda e: e.reciprocal(out=rr[:], in_=po[:, :, 64]))
                                ost = osr.next()
                                op("dve", [po, rr], [ost], lambda e: e.tensor_tensor(
                                    out=ost[:].rearrange("p (h d) -> p h d", h=6), in0=po[:, :, 0:64],
                                    in1=rr[:].unsqueeze(2).to_broadcast([128, 6, 64]), op=ALU.mult))
                                c["store"] = lambda: dma("pool", dr, ost, OC[ti * 128:(ti + 1) * 128, :], ost[:], sem_buf=ost)

                        def c_s3(c):
                            if "store" in c:
                                c["store"]()

                        run_skewed([c_s0, c_s1, c_s2, c_s3], [dict(ti=ti, h=h) for ti in tiles for h in range(6)], name="C")
                        fw.barrier()
                        fw.release([KT, VX] + bst.bufs + qr_.bufs + osr.bufs)
                bc_.close()

                if "O" in phases:
                    with ExitStack() as os_:
                        WO = fw.sb(os_, "wo", [128, 8, D], BF16)
                        wst = Ring([fw.sb(os_, "wost%d" % i, [128, D], F32) for i in range(2)])
                        for kc in range(8):
                            w = wst.next()
                            dma("sp", w, dr, w[:], w_out[l, kc * 128:(kc + 1) * 128, :])
                            op("dve", [w], [WO], lambda e: e.tensor_copy(out=WO[:, kc, :], in_=w[:]))

                        def ring(name, shape, dt, n=2, ps=False):
                            return Ring([(fw.ps if ps else fw.sb)(os_, "%s%d" % (name, i), shape, dt) for i in range(n)])
                        oa = [ring("ooa%d" % d, [128, 256], F32) for d in range(2)]
                        ob = [ring("oob%d" % d, [128, 384], F32) for d in range(2)]
                        oc = ring("ooc", [128, 384], F32, 3)
                        gz = ring("ogz", [128, 3, 384], BF16, 3)
                        xr = ring("ox", [128, D], F32, 6)
                        xo = ring("oxo", [128, D], F32, 3)
                        Y = ring("oY", [128, D], BF16, 3)
                        YT = ring("oYT", [128, 8, 128], BF16, 3)
                        sAr = ring("osA", [128, 256], F32, 3)
                        sBr = ring("osB", [128, 384], F32, 3)
                        tAr = ring("otA", [128, 256], BF16, 2)
                        tBr = ring("otB", [128, 384], F32, 2)
                        st8r = ring("ost8", [128, 16], F32, 3)
                        stAr = ring("ostA", [128, 16], F32, 3)
                        stBr = ring("ostB", [128, 16], F32, 3)
                        bigr = ring("obig", [128, D], F32, 2)
                        junk = ring("ojunk", [128, 512], BF16, 2)
                        pyt = ring("opyt", [128, 8, 128], BF16, 2, ps=True)
                        pu = ring("opu", [128, 2, 512], F32, 3, ps=True)
                        tiles = list(range(NT)) if not last else list(range(NCTX, NT))

                        def o_s0(c):
                            ti = c["ti"]
                            r0 = ti * 128
                            a0, a1 = oa[0].next(), oa[1].next()
                            b0, b1 = ob[0].next(), ob[1].next()
                            c_ = oc.next(); g_ = gz.next(); xt = xr.next()
                            dma("sp", a0, dr, a0[:], OA[0][r0:r0 + 128, :]); dma("sp", a1, dr, a1[:], OA[1][r0:r0 + 128, :])
                            dma("sp", b0, dr, b0[:], OB[0][r0:r0 + 128, :]); dma("sp", b1, dr, b1[:], OB[1][r0:r0 + 128, :])
                            dma("sp", c_, dr, c_[:], OC[r0:r0 + 128, :])
                            dma("sp", g_, dr, g_[:, 0, 0:256], TB[r0:r0 + 128, 768:1024])
                            dma("sp", g_, dr, g_[:, 1, :], TB[r0:r0 + 128, 1792:2176])
                            dma("sp", g_, dr, g_[:, 2, :], TB[r0:r0 + 128, 2560:2944])
                            dma("sp", xt, dr, xt[:], x_src(l, ti))
                            sA = sAr.next(); sB = sBr.next(); tA = tAr.next(); tB = tBr.next(); stA = stAr.next(); stB = stBr.next()
                            c.update(c_=c_, g_=g_, xt=xt, sA=sA, sB=sB, stA=stA, stB=stB)
                            op("pool", [a0, a1], [sA], lambda e: e.tensor_tensor(out=sA[:], in0=a0[:], in1=a1[:], op=ALU.add))
                            for h in range(4):
                                op("act", [sA], [tA, stA], lambda e: e.activation(out=tA[:, 64 * h:64 * h + 64], in_=sA[:, 64 * h:64 * h + 64],
                                                                                   func=AF.Square, accum_out=stA[:, h:h + 1]))
                            op("pool", [stA], [stA], lambda e: e.tensor_scalar(out=stA[:, 0:4], in0=stA[:, 0:4], scalar1=1.0 / 64, scalar2=64.0 * EPS,
                                                                              op0=ALU.mult, op1=ALU.add))
                            op("dve", [b0, b1], [sB], lambda e: e.tensor_tensor(out=sB[:], in0=b0[:], in1=b1[:], op=ALU.add))
                            op("dve", [sB], [tB], lambda e: e.tensor_tensor(out=tB[:], in0=sB[:], in1=sB[:], op=ALU.mult))
                            op("dve", [tB], [stB], lambda e: e.tensor_reduce(out=stB[:, 4:8], in_=tB[:].rearrange("p (h d) -> p h d", h=4),
                                                                            axis=AX.X, op=ALU.add))
                            op("dve", [stB], [stB], lambda e: e.tensor_scalar(out=stB[:, 4:8], in0=stB[:, 4:8], scalar1=1.0 / 96, scalar2=EPS,
                                                                              op0=ALU.mult, op1=ALU.add))

                        def o_s1(c):
                            c_, g_, sA, sB, stA, stB = c["c_"], c["g_"], c["sA"], c["sB"], c["stA"], c["stB"]
                            y = Y.next()
                            c["y"] = y
                            op("act", [stA], [stA], lambda e: e.activation(out=stA[:, 0:4], in_=stA[:, 0:4], func=AF.Ln))
                            op("act", [stA], [stA], lambda e: e.activation(out=stA[:, 0:4], in_=stA[:, 0:4], func=AF.Exp, scale=-0.5))
                            op("act", [stB], [stB], lambda e: e.activation(out=stB[:, 4:8], in_=stB[:, 4:8], func=AF.Ln))
                            op("act", [stB], [stB], lambda e: e.activation(out=stB[:, 4:8], in_=stB[:, 4:8], func=AF.Exp, scale=-0.5))
                            op("pool", [sA, stA], [sA], lambda e: e.tensor_tensor(
                                out=sA[:].rearrange("p (h d) -> p h d", h=4), in0=sA[:].rearrange("p (h d) -> p h d", h=4),
                                in1=stA.t[:, 0:4].unsqueeze(2).to_broadcast([128, 4, 64]), op=ALU.mult))
                            op("pool", [sA, GNA], [sA], lambda e: e.tensor_tensor(out=sA[:], in0=sA[:], in1=GNA[:], op=ALU.mult))
                            op("pool", [sA, g_], [y], lambda e: e.tensor_tensor(out=y[:, 0:256], in0=sA[:], in1=g_[:, 0, 0:256], op=ALU.mult))
                            op("dve", [sB, stB], [sB], lambda e: e.tensor_tensor(
                                out=sB[:].rearrange("p (h d) -> p h d", h=4), in0=sB[:].rearrange("p (h d) -> p h d", h=4),
                                in1=stB.t[:, 4:8].unsqueeze(2).to_broadcast([128, 4, 96]), op=ALU.mult))
                            op("dve", [sB, GNB], [sB], lambda e: e.tensor_tensor(out=sB[:], in0=sB[:], in1=GNB[:], op=ALU.mult))
                            op("dve", [sB, g_], [y], lambda e: e.tensor_tensor(out=y[:, 256:640], in0=sB[:], in1=g_[:, 1, :], op=ALU.mult))
                            op("dve", [c_, g_], [y], lambda e: e.tensor_tensor(out=y[:, 640:1024], in0=c_[:], in1=g_[:, 2, :], op=ALU.mult))

                        def o_s2(c):
                            y = c["y"]
                            py = pyt.next()
                            for kc in range(8):
                                op("pe", [y, CB], [py], lambda e: e.transpose(out=py[:, kc, :], in_=y[:, kc * 128:(kc + 1) * 128], identity=IDENT))
                            yT = YT.next()
                            c["yT"] = yT
                            op("act", [py], [yT], lambda e: e.copy(out=yT[:], in_=py[:]))

                        def o_s3(c):
                            yT = c["yT"]
                            u = pu.next()
                            st8 = st8r.next()
                            c.update(u=u, st8=st8)
                            for nb in range(2):
                                for kc in range(8):
                                    op("pe", [yT, WO], [u], lambda e: e.matmul(u[:, nb, :], lhsT=yT[:, kc, :], rhs=WO[:, kc, nb * 512:(nb + 1) * 512],
                                                                              start=(kc == 0), stop=(kc == 7)))
                            for nb in range(2):
                                jk = junk.next()
                                op("act", [u], [jk, st8], lambda e: e.activation(out=jk[:], in_=u[:, nb, :], func=AF.Square,
                                                                               accum_out=st8[:, 8 + nb:9 + nb]))

                        def o_s4(c):
                            ti, u, st8, xt = c["ti"], c["u"], c["st8"], c["xt"]
                            m = MODS[1 if ti < NCTX else 0]
                            r0 = ti * 128
                            big = bigr.next()
                            op("dve", [st8], [st8], lambda e: e.tensor_tensor(out=st8[:, 10:11], in0=st8[:, 8:9], in1=st8[:, 9:10], op=ALU.add))
                            op("dve", [st8], [st8], lambda e: e.tensor_scalar(out=st8[:, 10:11], in0=st8[:, 10:11], scalar1=1.0 / D, scalar2=EPS,
                                                                              op0=ALU.mult, op1=ALU.add))
                            op("act", [st8], [st8], lambda e: e.activation(out=st8[:, 10:11], in_=st8[:, 10:11], func=AF.Ln))
                            op("act", [st8], [st8], lambda e: e.activation(out=st8[:, 10:11], in_=st8[:, 10:11], func=AF.Exp, scale=-0.5))
                            op("dve", [u, st8, m], [big], lambda e: e.scalar_tensor_tensor(
                                out=big[:].rearrange("p (a b) -> p a b", a=2), in0=u[:], scalar=st8[:, 10:11],
                                in1=m[:, 2 * D:3 * D].rearrange("p (a b) -> p a b", a=2), op0=ALU.mult, op1=ALU.mult))
                            xo_ = xo.next()
                            op("dve", [big, xt], [xo_], lambda e: e.tensor_tensor(out=xo_[:], in0=big[:], in1=xt[:], op=ALU.add))

                            def store():
                                if last:
                                    dma("pool", dr, xo_, y_out[(ti - NCTX) * 128:(ti - NCTX + 1) * 128, :], xo_[:], sem_buf=xo_)
                                else:
                                    dma("pool", dr, xo_, XC[r0:r0 + 128, :], xo_[:], sem_buf=xo_)
                            c["store"] = store

                        def o_s5(c):
                            c["store"]()

                        run_skewed([o_s0, o_s1, o_s2, o_s3, o_s4, o_s5], [dict(ti=ti) for ti in tiles], name="O")
                        fw.barrier()
                        fw.release(wst.bufs + oa[0].bufs + oa[1].bufs + ob[0].bufs + ob[1].bufs + oc.bufs + gz.bufs + xr.bufs + xo.bufs)
                fw.barrier()
                fw.release([LB, OML, GNA, GNB, GB])
        fw.barrier()
    return nc


def make_in_maps(inputs, cores):
    x = np.asarray(inputs["x"], np.float32)
    c = np.asarray(inputs["c"], np.float32)
    ctx = np.asarray(inputs["ctx"], np.float32)
    c_ctx = np.asarray(inputs["c_ctx"], np.float32)
    shared = {
        "w_mod": np.ascontiguousarray(inputs["w_mod"], np.float32),
        "b_mod": np.ascontiguousarray(inputs["b_mod"], np.float32),
        "g_pre": np.ascontiguousarray(inputs["g_pre"], np.float32),
        "g_post": np.ascontiguousarray(inputs["g_post"], np.float32),
        "w_in": np.ascontiguousarray(inputs["w_in"], np.float32),
        "w_out": np.ascontiguousarray(inputs["w_out"], np.float32),
        "hgrn_lb": np.ascontiguousarray(np.asarray(inputs["hgrn_lb"], np.float32).reshape(DEPTH, 512)),
        "hgrn_gn": np.ascontiguousarray(inputs["hgrn_gn"], np.float32),
        "gate_b": np.ascontiguousarray(np.asarray(inputs["mlstm_gate_b"], np.float32).reshape(DEPTH, 16)),
        "mlstm_gn": np.ascontiguousarray(inputs["mlstm_gn"], np.float32),
        "rpbt": _rpb_tables(np.asarray(inputs["na_rpb"], np.float32)),
        "rope": _rope_tables().reshape(NT, 128, 384),
        "consts": _consts(),
    }
    maps = []
    for b in cores:
        cT = np.concatenate([c[b].reshape(8, 128).T, c_ctx.reshape(8, 128).T], axis=1)
        m = dict(shared)
        m["x"] = np.ascontiguousarray(x[b])
        m["ctx"] = np.ascontiguousarray(ctx[b])
        m["cT"] = np.ascontiguousarray(cT, np.float32)
        maps.append(m)
    return maps


def kernel(**inputs):
    nc = build()
    maps = make_in_maps(inputs, list(range(8)))
    res = run_bass_kernel_spmd(nc, maps, core_ids=list(range(8)))
    out = np.stack([np.asarray(r["y"], np.float32) for r in res.results], axis=0)
    return out
```

```python
import os
import numpy as np
from contextlib import ExitStack
import concourse.bass as bass
import concourse.mybir as mybir
from concourse.bass_utils import run_bass_kernel_spmd

F32 = mybir.dt.float32
BF16 = mybir.dt.bfloat16
AF = mybir.ActivationFunctionType
ALU = mybir.AluOpType
AX = mybir.AxisListType

D = 1024
NT = 34
NCTX = 2
TOK = NT * 128
PIN = 4752
DEPTH = 2
EPS = 1e-6
NTB = 2944
NTBS = 4352
NTF = 528
NFM = 20
NEG = -30000.0
BWD_ORDER = [1, 0] + list(range(NT - 1, 1, -1))

SEM_CHUNK = 24000


class Buf:
    def __init__(self, t=None, name=""):
        self.t = t
        self.name = name
        self.w = None
        self.r = {}
        self.dsem = None

    def __getitem__(self, k):
        return self.t[k]


class DramTrack:
    def __init__(self):
        self.d = {}

    def get(self, name):
        if name not in self.d:
            self.d[name] = Buf(None, name)
        return self.d[name]


class DmaSem:
    def __init__(self, sem):
        self.sem = sem
        self.n = 0


class FW:
    def __init__(self, nc, es):
        self.nc = nc
        self.es = es
        self.E = {"pe": nc.tensor, "act": nc.scalar, "dve": nc.vector, "pool": nc.gpsimd, "sp": nc.sync}
        self.cnt = {e: 0 for e in self.E}
        self.sems = {e: [] for e in self.E}
        self.seen = {e: {} for e in self.E}
        self.dsems = []
        self.free_dsems = []
        self.bar_sem = es.enter_context(nc.semaphore("barsem"))
        self.bar_n = 0
        self.uid = 0

    def sb(self, es, name, shape, dt):
        self.uid += 1
        t = es.enter_context(self.nc.sbuf_tensor("%s_%d" % (name, self.uid), list(shape), dt))
        return Buf(t, name)

    def ps(self, es, name, shape, dt):
        self.uid += 1
        t = es.enter_context(self.nc.psum_tensor("%s_%d" % (name, self.uid), list(shape), dt))
        return Buf(t, name)

    def get_dsem(self):
        if self.free_dsems:
            return self.free_dsems.pop()
        s = self.es.enter_context(self.nc.semaphore("dsem%d" % len(self.dsems)))
        d = DmaSem(s)
        self.dsems.append(d)
        return d

    def release(self, bufs):
        for b in bufs:
            if b.dsem is not None:
                self.free_dsems.append(b.dsem)
                b.dsem = None

    def _esem(self, e, n):
        k = (n - 1) // SEM_CHUNK
        while len(self.sems[e]) <= k:
            self.sems[e].append(self.es.enter_context(self.nc.semaphore("es_%s_%d" % (e, len(self.sems[e])))))
        return self.sems[e][k], (n - 1) % SEM_CHUNK + 1

    def _wait_tok(self, e, tok):
        key, n = tok
        if key == e and e == "pe":
            return
        if self.seen[e].get(key, 0) >= n:
            return
        self.seen[e][key] = n
        eng = self.E[e]
        if isinstance(key, str):
            sem, val = self._esem(key, n)
            eng.wait_ge(sem, val)
        else:
            eng.wait_ge(key.sem, n)

    def _deps(self, e, reads, writes):
        for b in reads:
            if b.w is not None:
                self._wait_tok(e, b.w)
        for b in writes:
            if b.w is not None:
                self._wait_tok(e, b.w)
            for key, n in list(b.r.items()):
                self._wait_tok(e, (key, n))

    def _mark(self, tok, reads, writes):
        key, n = tok
        for b in writes:
            b.w = tok
            b.r = {}
        for b in reads:
            if b in writes:
                continue
            if b.r.get(key, 0) < n:
                b.r[key] = n

    def op(self, e, reads, writes, fn):
        self._deps(e, reads, writes)
        ins = fn(self.E[e])
        self.cnt[e] += 1
        n = self.cnt[e]
        sem, _ = self._esem(e, n)
        ins.then_inc(sem, 1)
        self._mark((e, n), reads, writes)
        return ins

    def dma(self, q, out_buf, in_buf, out_ap, in_ap, sem_buf=None, **kw):
        if isinstance(out_buf, DramTrack):
            out_buf = out_buf.get(out_ap.name)
        if isinstance(in_buf, DramTrack):
            in_buf = in_buf.get(in_ap.name)
        sb_ = sem_buf if sem_buf is not None else out_buf
        if sb_.dsem is None:
            sb_.dsem = self.get_dsem()
        ds = sb_.dsem
        self._deps(q, [in_buf], [out_buf])
        ins = self.E[q].dma_start(out=out_ap, in_=in_ap, **kw)
        ds.n += 16
        ins.then_inc(ds.sem, 16)
        self._mark((ds, ds.n), [in_buf], [out_buf])
        return ins

    def barrier(self):
        sp = self.E["sp"]
        for e in self.E:
            if e != "sp" and self.cnt[e] > 0:
                self._wait_tok("sp", (e, self.cnt[e]))
        for d in self.dsems:
            if d.n > 0:
                self._wait_tok("sp", (d, d.n))
        self.bar_n += 1
        sp.sem_inc(self.bar_sem, 1)
        for e in self.E:
            if e != "sp":
                self.E[e].wait_ge(self.bar_sem, self.bar_n)
        for e in self.E:
            for e2 in self.E:
                if e2 != e:
                    self.seen[e][e2] = self.cnt[e2]
            for d in self.dsems:
                self.seen[e][d] = d.n


def run_skewed(stages, items, cap=None, name=""):
    n = len(stages)
    if cap is None:
        cap = int(os.environ.get("SKEW_CAP_" + name, os.environ.get("SKEW_CAP", n - 1)))
    cap = min(cap, n - 1)
    for t in range(len(items) + cap):
        for off in range(cap, -1, -1):
            k = t - off
            if 0 <= k < len(items):
                for si in range(n):
                    if min(si, cap) == off:
                        stages[si](items[k])


class Ring:
    def __init__(self, bufs):
        self.bufs = bufs
        self.i = 0

    def next(self):
        b = self.bufs[self.i % len(self.bufs)]
        self.i += 1
        return b


def _consts():
    s = np.arange(128)[:, None]
    t = np.arange(128)[None, :]
    c = np.zeros((128, 6, 128), np.float32)
    c[:, 0] = (s == t)
    c[:, 1] = (s <= t)
    c[:, 2] = (s >= t)
    c[:, 3] = (s > t)
    c[:, 4] = (s < t)
    c[:, 5] = 1.0
    return c.reshape(128, 768)


def _rope_tables():
    tab = np.zeros((NT, 128, 4, 96), np.float32)
    qs = np.float32(96 ** -0.5)
    tab[:NCTX, :, 0, :] = qs
    tab[:NCTX, :, 2, :] = 1.0
    inv = (np.float32(10000.0) ** (-np.arange(0, 48, 2, dtype=np.float32) / np.float32(48))).astype(np.float32)
    T = np.arange(4096)
    pos = [(T // 64).astype(np.float32), (T % 64).astype(np.float32)]
    cos = np.zeros((4096, 2, 2, 24), np.float32)
    sin = np.zeros((4096, 2, 2, 24), np.float32)
    for p in range(2):
        ang = (pos[p][:, None] * inv[None, :]).astype(np.float32)
        cs, sn = np.cos(ang).astype(np.float32), np.sin(ang).astype(np.float32)
        cos[:, p, 0], cos[:, p, 1] = cs, cs
        sin[:, p, 0], sin[:, p, 1] = -sn, sn
    cos = cos.reshape(32, 128, 96)
    sin = sin.reshape(32, 128, 96)
    tab[NCTX:, :, 0] = cos * qs
    tab[NCTX:, :, 1] = sin * qs
    tab[NCTX:, :, 2] = cos
    tab[NCTX:, :, 3] = sin
    return tab


def _rpb_tables(na_rpb):
    L, H = na_rpb.shape[0], na_rpb.shape[1]
    s = np.arange(128)[:, None]
    t = np.arange(128)[None, :]
    kc, qc = s % 64, t % 64
    qr = t // 64
    cs = np.clip(qc - 8, 0, 48)
    band = (kc >= cs) & (kc < cs + 16)
    dc = np.clip(kc - qc + 15, 0, 30)
    out = np.full((L, H, 9, 128, 128), NEG, np.float32)
    for k in range(9):
        o = [-3, -2, -1, 0, 1, 2, 3, -2, 2][k]
        kr = 2 * o + s // 64
        diff = kr - qr
        ok = band & (diff >= -7) & (diff <= 7)
        if k >= 7:
            ok = ok & (diff >= -4) & (diff <= 3)
        dr = np.clip(diff + 7, 0, 14)
        for l in range(L):
            for h in range(H):
                g = na_rpb[l, h][dr, dc]
                out[l, h, k] = np.where(ok, g, np.float32(NEG))
    return out.reshape(L, H * 9, 128, 128)


def _key_tiles(lt):
    if lt <= 1:
        return [(kt, kt - lt + 3) for kt in range(0, 4)]
    if lt >= 30:
        return [(kt, kt - lt + 3) for kt in range(28, 32)]
    res = []
    for o in range(-2, 3):
        k = o + 3
        if o == -2:
            k = 7
        if o == 2:
            k = 8
        res.append((lt + o, k))
    return res


def build(debug=(), nlayers=DEPTH, phases="PABCO"):
    nc = bass.Bass("TRN2", target_bir_lowering=False)

    def din(name, shape, dt=F32):
        return nc.dram_tensor(name, list(shape), dt, kind="ExternalInput").ap()

    def dscr(name, shape, dt=F32):
        kind = "ExternalOutput" if name in debug else "Internal"
        return nc.dram_tensor(name, list(shape), dt, kind=kind).ap()

    x_in = din("x", [4096, D])
    ctx_in = din("ctx", [256, D])
    cT_in = din("cT", [128, 16])
    w_mod = din("w_mod", [DEPTH, D, 3 * D])
    b_mod = din("b_mod", [DEPTH, 3 * D])
    g_pre = din("g_pre", [DEPTH, D])
    g_post = din("g_post", [DEPTH, D])
    w_in = din("w_in", [DEPTH, D, PIN])
    w_out = din("w_out", [DEPTH, D, D])
    hgrn_lb = din("hgrn_lb", [DEPTH, 512])
    hgrn_gn = din("hgrn_gn", [DEPTH, 256])
    gate_b = din("gate_b", [DEPTH, 16])
    mlstm_gn = din("mlstm_gn", [DEPTH, 384])
    rpbt = din("rpbt", [DEPTH, 54, 128, 128])
    rope = din("rope", [NT, 128, 4 * 96])
    consts = din("consts", [128, 768])
    y_out = nc.dram_tensor("y", [4096, D], F32, kind="ExternalOutput").ap()

    XC = dscr("XC", [TOK, D])
    TB = dscr("TB", [TOK, NTB], BF16)
    TF = dscr("TF", [TOK, NTF])
    FM = dscr("FM", [NT, 128, NFM, 128], BF16)
    OA = [dscr("OA%d" % d, [TOK, 256]) for d in range(2)]
    OB = [dscr("OB%d" % d, [TOK, 384]) for d in range(2)]
    OC = dscr("OC", [TOK, 384])
    dr = DramTrack()

    with ExitStack() as es:
        fw = FW(nc, es)
        op, dma = fw.op, fw.dma

        CST = fw.sb(es, "cst", [128, 768], F32)
        dma("sp", CST, dr, CST[:], consts[:, :])
        IDENTF = CST.t[:, 0:128]
        TRI = [CST.t[:, 128:256], CST.t[:, 256:384]]
        TRIS = [CST.t[:, 384:512], CST.t[:, 512:640]]
        ONESF = CST.t[:, 640:768]
        CB = fw.sb(es, "cstb", [128, 384], BF16)
        op("dve", [CST], [CB], lambda e: e.tensor_copy(out=CB[:], in_=CST[:, 0:384]))
        IDENT = CB.t[:, 0:128]
        MASK = [CB.t[:, 128:256], CB.t[:, 256:384]]
        CT = fw.sb(es, "cT", [128, 16], F32)
        dma("sp", CT, dr, CT[:], cT_in[:, :])
        SC = fw.sb(es, "sc", [128, 16], F32)
        op("act", [CT], [SC], lambda e: e.activation(out=SC[:], in_=CT[:], func=AF.Exp, scale=-1.0))
        op("act", [SC], [SC], lambda e: e.activation(out=SC[:], in_=SC[:], func=AF.Ln, bias=1.0))
        op("act", [SC], [SC], lambda e: e.activation(out=SC[:], in_=SC[:], func=AF.Exp, scale=-1.0))
        op("dve", [SC, CT], [SC], lambda e: e.tensor_tensor(out=SC[:], in0=SC[:], in1=CT[:], op=ALU.mult))
        fw.barrier()

        def x_src(l, ti):
            if l == 0:
                if ti < NCTX:
                    return ctx_in[ti * 128:(ti + 1) * 128, :]
                return x_in[(ti - NCTX) * 128:(ti - NCTX + 1) * 128, :]
            return XC[ti * 128:(ti + 1) * 128, :]

        for l in range(nlayers):
            last = (l == DEPTH - 1)
            with ExitStack() as les:
                MODS = [fw.sb(les, "mod%d" % i, [128, 3 * D], F32) for i in range(2)]
                LB = fw.sb(les, "lb", [128, 512], F32)
                OML = fw.sb(les, "oml", [128, 512], F32)
                GNA = fw.sb(les, "gna", [128, 256], F32)
                GNB = fw.sb(les, "gnb", [128, 384], F32)
                GB = fw.sb(les, "gb", [128, 16], F32)
                dma("sp", GNA, dr, GNA[:], hgrn_gn[l:l + 1, :].partition_broadcast(128))
                dma("sp", GNB, dr, GNB[:], mlstm_gn[l:l + 1, :].partition_broadcast(128))
                dma("sp", GB, dr, GB[:], gate_b[l:l + 1, :].partition_broadcast(128))
                if l == 0:
                    op("dve", [], [LB], lambda e: e.memset(LB[:], 0.0))
                else:
                    dma("sp", LB, dr, LB[:], hgrn_lb[1:2, :].partition_broadcast(128))
                    dma("sp", OML, dr, OML[:], hgrn_lb[0:1, :].partition_broadcast(128))
                    op("dve", [LB, OML], [LB], lambda e: e.tensor_sub(out=LB[:], in0=LB[:], in1=OML[:]))
                    op("act", [LB], [LB], lambda e: e.activation(out=LB[:], in_=LB[:], func=AF.Exp, scale=-1.0))
                    op("act", [LB], [LB], lambda e: e.activation(out=LB[:], in_=LB[:], func=AF.Ln, bias=1.0))
                    op("act", [LB], [LB], lambda e: e.activation(out=LB[:], in_=LB[:], func=AF.Exp, scale=-1.0))
                op("dve", [LB], [OML], lambda e: e.tensor_scalar(out=OML[:], in0=LB[:], scalar1=-1.0, scalar2=1.0,
                                                                  op0=ALU.mult, op1=ALU.add))
                win_ = ExitStack()
                if "P" in phases:
                    WIN = fw.sb(win_, "win", [128, 8, PIN], BF16)
                    wst = Ring([fw.sb(win_, "wst%d" % i, [128, 1188], F32) for i in range(3)])

                    def _mk_win(k):
                        kc, q4 = k // 4, k % 4

                        def f():
                            w = wst.next()
                            dma("pool", w, dr, w[:], w_in[l, kc * 128:(kc + 1) * 128, q4 * 1188:(q4 + 1) * 1188])
                            if k % 4 == 0:
                                op("dve", [w], [WIN], lambda e: e.tensor_copy(out=WIN[:, kc, q4 * 1188:(q4 + 1) * 1188], in_=w[:]))
                            else:
                                op("act", [w], [WIN], lambda e: e.copy(out=WIN[:, kc, q4 * 1188:(q4 + 1) * 1188], in_=w[:]))
                        return f
                    win_pre = [_mk_win(k) for k in range(32)]
                with ExitStack() as ms:
                    BM = fw.sb(ms, "bm", [128, 3 * D], F32)
                    GPRE = fw.sb(ms, "gpre", [128, D], F32)
                    GPOST = fw.sb(ms, "gpost", [128, D], F32)
                    dma("sp", BM, dr, BM[:], b_mod[l:l + 1, :].partition_broadcast(128))
                    dma("sp", GPRE, dr, GPRE[:], g_pre[l:l + 1, :].partition_broadcast(128))
                    dma("sp", GPOST, dr, GPOST[:], g_post[l:l + 1, :].partition_broadcast(128))
                    wm = Ring([fw.sb(ms, "wm%d" % i, [128, 8, 512], F32) for i in range(2)])
                    pmod = Ring([fw.ps(ms, "pmod%d" % i, [128, 512], F32) for i in range(4)])
                    for nb in range(6):
                        w = wm.next()
                        dma("sp", w, dr, w[:], w_mod[l, :, nb * 512:(nb + 1) * 512].rearrange("(kc p) n -> p kc n", p=128))
                        for st in range(2):
                            pm = pmod.next()
                            for kc in range(8):
                                op("pe", [SC, w], [pm], lambda e: e.matmul(
                                    pm[:], lhsT=SC[:, st * 8 + kc:st * 8 + kc + 1].to_broadcast([128, 128]),
                                    rhs=w[:, kc, :], start=(kc == 0), stop=(kc == 7)))
                            m = MODS[st]
                            op("dve", [pm, BM], [m], lambda e: e.tensor_tensor(
                                out=m[:, nb * 512:(nb + 1) * 512], in0=pm[:], in1=BM[:, nb * 512:(nb + 1) * 512], op=ALU.add))
                            for _ in range(3):
                                if "P" in phases and win_pre:
                                    win_pre.pop(0)()
                    for st in range(2):
                        m = MODS[st]
                        op("dve", [m, GPRE], [m], lambda e: e.scalar_tensor_tensor(
                            out=m[:, D:2 * D], in0=m[:, D:2 * D], scalar=1.0, in1=GPRE[:], op0=ALU.add, op1=ALU.mult))
                        op("dve", [m, GPOST], [m], lambda e: e.tensor_tensor(
                            out=m[:, 2 * D:3 * D], in0=m[:, 2 * D:3 * D], in1=GPOST[:], op=ALU.mult))
                    while "P" in phases and win_pre:
                        win_pre.pop(0)()
                    fw.barrier()
                    fw.release(wm.bufs + [BM, GPRE, GPOST])

                if "P" in phases:
                    with ExitStack() as ps_:
                        xr = Ring([fw.sb(ps_, "x%d" % i, [128, D], F32) for i in range(2)])
                        rp = Ring([fw.sb(ps_, "rope%d" % i, [128, 4, 96], F32) for i in range(2)])
                        sqr = Ring([fw.sb(ps_, "sq%d" % i, [128, D], F32) for i in range(2)])
                        ssr = Ring([fw.sb(ps_, "ss%d" % i, [128, 4], F32) for i in range(3)])
                        hbr = Ring([fw.sb(ps_, "hb%d" % i, [128, D], BF16) for i in range(2)])
                        hT = Ring([fw.sb(ps_, "hT%d" % i, [128, 8, 128], BF16) for i in range(2)])
                        phT = fw.ps(ps_, "phT", [128, 8, 128], BF16)
                        pj = Ring([fw.ps(ps_, "pj%d" % i, [128, 512], F32) for i in range(5)])
                        ptt = Ring([fw.ps(ps_, "ptt%d" % i, [128, 8, 128], BF16) for i in range(2)])
                        tbs = Ring([fw.sb(ps_, "tbs%d" % i, [128, NTBS], BF16) for i in range(2)])
                        tfs = Ring([fw.sb(ps_, "tfs%d" % i, [128, NTF], F32) for i in range(2)])
                        fms = Ring([fw.sb(ps_, "fms%d" % i, [128, NFM, 128], BF16) for i in range(2)])
                        tmpr = Ring([fw.sb(ps_, "ptmp%d" % i, [128, 512], F32) for i in range(6)])

                        def proj(ps, c0, wd, hTb):
                            for kc in range(8):
                                op("pe", [hTb, WIN], [ps], lambda e: e.matmul(
                                    ps[:, 0:wd], lhsT=hTb[:, kc, :], rhs=WIN[:, kc, c0:c0 + wd], start=(kc == 0), stop=(kc == 7)))

                        def sigm(src_buf, src_ap, wd):
                            a = tmpr.next(); b = tmpr.next()
                            op("act", [src_buf], [a], lambda e: e.activation(out=a[:, 0:wd], in_=src_ap, func=AF.Exp, scale=-1.0))
                            op("act", [a], [b], lambda e: e.activation(out=b[:, 0:wd], in_=a[:, 0:wd], func=AF.Ln, bias=1.0))
                            op("act", [b], [a], lambda e: e.activation(out=a[:, 0:wd], in_=b[:, 0:wd], func=AF.Exp, scale=-1.0))
                            return a, b

                        PST = {}

                        def prologue(ti):
                            st = 1 if ti < NCTX else 0
                            m = MODS[st]
                            xt = xr.next()
                            dma("sp", xt, dr, xt[:], x_src(l, ti))
                            rt = rp.next()
                            dma("sp", rt, dr, rt[:], rope[ti].rearrange("p (a b) -> p a b", a=4))
                            sq = sqr.next(); ss = ssr.next(); hb = hbr.next()
                            op("act", [xt], [sq, ss], lambda e: e.activation(out=sq[:], in_=xt[:], func=AF.Square, accum_out=ss[:, 0:1]))
                            op("dve", [ss], [ss], lambda e: e.tensor_scalar(out=ss[:, 1:2], in0=ss[:, 0:1], scalar1=1.0 / D, scalar2=EPS,
                                                                            op0=ALU.mult, op1=ALU.add))
                            op("act", [ss], [ss], lambda e: e.activation(out=ss[:, 2:3], in_=ss[:, 1:2], func=AF.Ln))
                            op("act", [ss], [ss], lambda e: e.activation(out=ss[:, 3:4], in_=ss[:, 2:3], func=AF.Exp, scale=-0.5))
                            op("dve", [xt, ss, m], [sq], lambda e: e.scalar_tensor_tensor(
                                out=sq[:], in0=xt[:], scalar=ss[:, 3:4], in1=m[:, D:2 * D], op0=ALU.mult, op1=ALU.mult))
                            op("dve", [sq, m], [hb], lambda e: e.tensor_tensor(out=hb[:], in0=sq[:], in1=m[:, 0:D], op=ALU.add))
                            for kc in range(8):
                                op("pe", [hb, CB], [phT], lambda e: e.transpose(out=phT[:, kc, :], in_=hb[:, kc * 128:(kc + 1) * 128], identity=IDENT))
                            hTb = hT.next()
                            op("act", [phT], [hTb], lambda e: e.copy(out=hTb[:], in_=phT[:]))
                            PST[ti] = dict(hTb=hTb, rt=rt)

                        def groups_a(ti):
                            hTb, rt = PST[ti]["hTb"], PST[ti]["rt"]
                            tb = tbs.next()
                            tf = tfs.next()
                            PST[ti].update(tb=tb, tf=tf)
                            ps = pj.next(); proj(ps, 256, 512, hTb)
                            sg, sp_ = sigm(ps, ps[:], 512)
                            if l == 0:
                                op("dve", [sp_], [tf], lambda e: e.tensor_scalar_mul(out=tf[:, 0:512], in0=sp_[:], scalar1=-1.0))
                            else:
                                op("dve", [sg, OML], [sg], lambda e: e.tensor_tensor(out=sg[:], in0=sg[:], in1=OML[:], op=ALU.mult))
                                op("dve", [sg, LB], [sg], lambda e: e.tensor_tensor(out=sg[:], in0=sg[:], in1=LB[:], op=ALU.add))
                                op("act", [sg], [tf], lambda e: e.activation(out=tf[:, 0:512], in_=sg[:], func=AF.Ln))
                            op("dve", [sg], [tb], lambda e: e.tensor_scalar(out=tb[:, 0:512], in0=sg[:], scalar1=-1.0, scalar2=1.0,
                                                                           op0=ALU.mult, op1=ALU.add))
                            ps = pj.next(); proj(ps, 0, 256, hTb)
                            sg, _ = sigm(ps, ps[:, 0:256], 256)
                            op("dve", [ps, sg], [tb], lambda e: e.tensor_tensor(out=tb[:, 2944:3200], in0=ps[:, 0:256], in1=sg[:, 0:256], op=ALU.mult))
                            ps = pj.next(); proj(ps, 768, 512, hTb)
                            op("dve", [ps], [tb], lambda e: e.tensor_copy(out=tb[:, 512:768], in_=ps[:, 0:256]))
                            sg, _ = sigm(ps, ps[:, 256:512], 256)
                            op("dve", [ps, sg], [tb], lambda e: e.tensor_tensor(out=tb[:, 768:1024], in0=ps[:, 256:512], in1=sg[:, 0:256], op=ALU.mult))
                            for (c0, dst, ci) in ((1280, 3200, 0), (1664, 1024, 2)):
                                ps = pj.next(); proj(ps, c0, 384, hTb)
                                t2 = tmpr.next(); t3 = tmpr.next()
                                pv = ps.t[:, 0:384].rearrange("p (h a b j) -> p h a b j", h=4, a=2, b=2)
                                t2v = t2.t[:, 0:384].rearrange("p (h a b j) -> p h a b j", h=4, a=2, b=2)
                                cosb = rt.t[:, ci, :].unsqueeze(1).to_broadcast([128, 4, 96])
                                sinv = rt.t[:, ci + 1, :].rearrange("p (a b j) -> p a b j", a=2, b=2)
                                op("dve", [ps, rt], [t3], lambda e: e.tensor_tensor(
                                    out=t3[:, 0:384].rearrange("p (h d) -> p h d", h=4), in0=ps[:, 0:384].rearrange("p (h d) -> p h d", h=4),
                                    in1=cosb, op=ALU.mult))
                                for b_ in range(2):
                                    op("dve", [ps, rt], [t2], lambda e: e.tensor_tensor(
                                        out=t2v[:, :, :, b_, :], in0=pv[:, :, :, 1 - b_, :],
                                        in1=sinv[:, :, b_, :].unsqueeze(1).to_broadcast([128, 4, 2, 24]), op=ALU.mult))
                                op("pool", [t3, t2], [tb], lambda e: e.tensor_tensor(out=tb[:, dst:dst + 384], in0=t3[:, 0:384], in1=t2[:, 0:384], op=ALU.add))

                        def groups_b(ti):
                            hTb, rt, tb, tf = PST[ti]["hTb"], PST[ti]["rt"], PST[ti]["tb"], PST[ti]["tf"]
                            ps = pj.next(); proj(ps, 2048, 384, hTb)
                            op("act", [ps], [tb], lambda e: e.copy(out=tb[:, 1408:1792], in_=ps[:, 0:384]))
                            ps = pj.next(); proj(ps, 2432, 384, hTb)
                            sgo, _ = sigm(ps, ps[:, 0:384], 384)
                            ps = pj.next(); proj(ps, 2816, 384, hTb)
                            sgz, _ = sigm(ps, ps[:, 0:384], 384)
                            op("dve", [ps, sgz], [sgz], lambda e: e.tensor_tensor(out=sgz[:, 0:384], in0=ps[:, 0:384], in1=sgz[:, 0:384], op=ALU.mult))
                            op("pool", [sgo, sgz], [tb], lambda e: e.tensor_tensor(out=tb[:, 1792:2176], in0=sgo[:, 0:384], in1=sgz[:, 0:384], op=ALU.mult))
                            ps = pj.next(); proj(ps, 3200, 16, hTb)
                            op("dve", [ps, GB], [tf], lambda e: e.tensor_tensor(out=tf[:, 512:528], in0=ps[:, 0:16], in1=GB[:], op=ALU.add))
                            _, spg = sigm(tf, tf[:, 520:528], 8)
                            op("dve", [spg], [tf], lambda e: e.tensor_scalar_mul(out=tf[:, 520:528], in0=spg[:, 0:8], scalar1=-1.0))
                            ps = pj.next(); proj(ps, 3216, 384, hTb)
                            op("act", [ps], [tb], lambda e: e.activation(out=tb[:, 3584:3968], in_=ps[:, 0:384], func=AF.Copy, scale=0.125))
                            ps = pj.next(); proj(ps, 3600, 384, hTb)
                            op("dve", [ps], [tb], lambda e: e.tensor_copy(out=tb[:, 3968:4352], in_=ps[:, 0:384]))
                            ps = pj.next(); proj(ps, 3984, 384, hTb)
                            op("act", [ps], [tb], lambda e: e.copy(out=tb[:, 2176:2560], in_=ps[:, 0:384]))
                            ps = pj.next(); proj(ps, 4368, 384, hTb)
                            sg, _ = sigm(ps, ps[:, 0:384], 384)
                            op("dve", [ps, sg], [tb], lambda e: e.tensor_tensor(out=tb[:, 2560:2944], in0=ps[:, 0:384], in1=sg[:, 0:384], op=ALU.mult))

                        def tail(ti):
                            tb, tf = PST[ti]["tb"], PST[ti]["tf"]
                            fm = fms.next()
                            srcA = [2944, 3072, 0, 128, 256, 384]
                            srcC = [3584, 3712, 3840, 3968, 4096, 4224]
                            pt_ = ptt.next()
                            for j, c0 in enumerate(srcA):
                                op("pe", [tb, CB], [pt_], lambda e: e.transpose(out=pt_[:, j, :], in_=tb[:, c0:c0 + 128], identity=IDENT))
                            op("dve", [pt_], [fm], lambda e: e.tensor_copy(out=fm[:, 0:6, :], in_=pt_[:, 0:6, :]))
                            pt_ = ptt.next()
                            for j, c0 in enumerate(srcC):
                                op("pe", [tb, CB], [pt_], lambda e: e.transpose(out=pt_[:, j, :], in_=tb[:, c0:c0 + 128], identity=IDENT))
                            op("act", [pt_], [fm], lambda e: e.copy(out=fm[:, 6:12, :], in_=pt_[:, 0:6, :]))
                            pt_ = ptt.next()
                            for j in range(8):
                                c0 = (3200 if j < 4 else 1024) + (j % 4) * 96
                                op("pe", [tb, CB], [pt_], lambda e: e.transpose(out=pt_[0:96, j, :], in_=tb[:, c0:c0 + 96], identity=IDENT))
                            op("dve", [pt_], [fm], lambda e: e.tensor_copy(out=fm[0:96, 12:20, :], in_=pt_[0:96, 0:8, :]))
                            del PST[ti]

                            def stores():
                                dma("pool", dr, tb, TB[ti * 128:(ti + 1) * 128, :], tb[:, 0:NTB], sem_buf=tb)
                                dma("pool", dr, tf, TF[ti * 128:(ti + 1) * 128, :], tf[:], sem_buf=tf)
                                dma("pool", dr, fm, FM[ti], fm[:], sem_buf=fm)
                            return stores

                        prologue(0)
                        for ti in range(NT):
                            if ti + 1 < NT:
                                prologue(ti + 1)
                            groups_a(ti)
                            pst = tail(ti - 1) if ti > 0 else None
                            groups_b(ti)
                            if pst is not None:
                                pst()
                        tail(NT - 1)()
                        fw.barrier()
                        fw.release(xr.bufs + rp.bufs + tbs.bufs + tfs.bufs + fms.bufs)
                if "P" in phases:
                    fw.release(wst.bufs)
                win_.close()

                if "A" in phases:
                    with ExitStack() as as_:
                        LA = int(os.environ.get("LA_A", 2))

                        def ring(name, shape, dt, n=None, ps=False):
                            if n is None:
                                n = LA + 1
                            return Ring([(fw.ps if ps else fw.sb)(as_, "%s%d" % (name, i), shape, dt) for i in range(n)])
                        St = []
                        for d in range(2):
                            S = fw.sb(as_, "S%d" % d, [128, 2, 64], F32)
                            Sb = fw.sb(as_, "Sb%d" % d, [128, 2, 128], BF16)
                            op("dve", [], [S], lambda e: e.memset(S[:], 0.0))
                            op("dve", [], [Sb], lambda e: e.memset(Sb[:], 0.0))
                            R_ = dict(S=S, Sb=Sb,
                                      lf=ring("alf%d" % d, [128, 256], F32), qT=ring("aqT%d" % d, [128, 2, 128], BF16),
                                      kT=ring("akT%d" % d, [128, 2, 128], BF16), kt=ring("akt%d" % d, [128, 256], BF16),
                                      vt=ring("avt%d" % d, [128, 256], BF16), ost=ring("aos%d" % d, [128, 256], F32, 3),
                                      bT=ring("abT%d" % d, [128, 2, 128], F32), E1=ring("aE1%d" % d, [128, 2, 128], F32),
                                      Es=ring("aEs%d" % d, [128, 2, 128], F32), tmp=ring("atm%d" % d, [128, 2, 128], F32),
                                      rr=ring("arr%d" % d, [128, 2, 8], F32), edk=ring("aed%d" % d, [128, 256], F32),
                                      kd=ring("akd%d" % d, [128, 256], BF16), Qf=ring("aQf%d" % d, [128, 2, 128], BF16),
                                      Qs=ring("aQs%d" % d, [128, 2, 2, 128], BF16), ATm=ring("aAT%d" % d, [128, 4, 128], BF16),
                                      Ek=[ring("aEk%d_%d" % (d, i), [128, 2, 128], F32) for i in range(4)],
                                      Kv=[ring("aKv%d_%d" % (d, i), [128, 2, 128], BF16) for i in range(4)])
                            for i in range(4):
                                for b in R_["Kv"][i].bufs:
                                    op("dve", [], [b], lambda e: e.memset(b[:], 0.0))
                            for b in R_["rr"].bufs + R_["Qs"].bufs:
                                op("dve", [], [b], lambda e: e.memset(b[:], 0.0))
                            St.append(R_)
                        pbT = ring("apbT", [128, 2, 128], F32, 1, ps=True)
                        prs = ring("aprs", [128, 256], F32, 1, ps=True)
                        pAT = ring("apAT", [128, 4, 128], F32, 2, ps=True)
                        pO = ring("apO", [128, 256], F32, 2, ps=True)
                        pU = ring("apU", [128, 2, 128], F32, 1, ps=True)

                        def a_s0(c):
                            ti, d = c["ti"], c["d"]
                            R_ = St[d]
                            lf = R_["lf"].next(); qT = R_["qT"].next(); kT = R_["kT"].next(); kt = R_["kt"].next(); vt = R_["vt"].next()
                            r0 = ti * 128
                            dma("sp", lf, dr, lf[:], TF[r0:r0 + 128, d * 256:(d + 1) * 256])
                            dma("sp", qT, dr, qT[:], FM[ti, :, 0:2, :])
                            dma("sp", kT, dr, kT[:], FM[ti, :, 2 + 2 * d:4 + 2 * d, :])
                            dma("sp", kt, dr, kt[:], TB[r0:r0 + 128, d * 256:(d + 1) * 256])
                            dma("sp", vt, dr, vt[:], TB[r0:r0 + 128, 512:768])
                            pb = pbT.next()
                            for pt in range(2):
                                op("pe", [lf, CST], [pb], lambda e: e.matmul(pb[:, pt, :], lhsT=lf[:, pt * 128:(pt + 1) * 128], rhs=TRI[d], start=True, stop=True))
                            bT = R_["bT"].next()
                            op("act", [pb], [bT], lambda e: e.copy(out=bT[:], in_=pb[:]))
                            pr = prs.next()
                            op("pe", [lf, CST], [pr], lambda e: e.matmul(pr[:], lhsT=TRIS[d], rhs=lf[:], start=True, stop=True))
                            edk = R_["edk"].next()
                            op("act", [pr], [edk], lambda e: e.activation(out=edk[:], in_=pr[:], func=AF.Exp))
                            c.update(qT=qT, kT=kT, kt=kt, vt=vt, bT=bT, edk=edk)

                        def a_s1(c):
                            d, bT, edk, kt = c["d"], c["bT"], c["edk"], c["kt"]
                            R_ = St[d]
                            kd = R_["kd"].next()
                            op("pool", [edk, kt], [kd], lambda e: e.tensor_tensor(out=kd[:], in0=edk[:], in1=kt[:], op=ALU.mult))
                            E1 = R_["E1"].next()
                            op("act", [bT], [E1], lambda e: e.activation(out=E1[:], in_=bT[:], func=AF.Exp))
                            rr = R_["rr"].next()
                            if d == 0:
                                src = bT.t[:, :, 31:127:32]; dn = rr.t[:, :, 1:4]; dp = rr.t[:, :, 5:8]
                            else:
                                src = bT.t[:, :, 32:128:32]; dn = rr.t[:, :, 0:3]; dp = rr.t[:, :, 4:7]
                            op("dve", [bT], [rr], lambda e: e.tensor_scalar_mul(out=dn, in0=src, scalar1=-1.0))
                            op("dve", [bT], [rr], lambda e: e.tensor_copy(out=dp, in_=src))
                            tmp = R_["tmp"].next()
                            op("dve", [bT, rr], [tmp], lambda e: e.tensor_tensor(
                                out=tmp[:].rearrange("p a (i j) -> p a i j", i=4), in0=bT[:].rearrange("p a (i j) -> p a i j", i=4),
                                in1=rr.t[:, :, 0:4].unsqueeze(3).to_broadcast([128, 2, 4, 32]), op=ALU.add))
                            c.update(kd=kd, E1=E1, rr=rr, tmp=tmp)

                        def a_s2(c):
                            d, bT, E1, rr, tmp, qT = c["d"], c["bT"], c["E1"], c["rr"], c["tmp"], c["qT"]
                            R_ = St[d]
                            Qf = R_["Qf"].next()
                            op("pool", [E1, qT], [Qf], lambda e: e.tensor_tensor(out=Qf[:], in0=E1[:], in1=qT[:], op=ALU.mult))
                            Es = R_["Es"].next()
                            op("act", [tmp], [Es], lambda e: e.activation(out=Es[:], in_=tmp[:], func=AF.Exp))
                            Eks = []
                            for i in range(4):
                                lo, hi = (0, 32 * (i + 1)) if d == 0 else (32 * i, 128)
                                Ek = R_["Ek"][i].next()
                                for pt in range(2):
                                    op("act", [bT, rr], [Ek], lambda e: e.activation(
                                        out=Ek[:, pt, lo:hi], in_=bT[:, pt, lo:hi], func=AF.Exp, bias=rr[:, pt, 4 + i:5 + i], scale=-1.0))
                                Eks.append(Ek)
                            c.update(Qf=Qf, Es=Es, Eks=Eks)

                        def a_s3(c):
                            d, Es, Eks, qT, kT = c["d"], c["Es"], c["Eks"], c["qT"], c["kT"]
                            R_ = St[d]
                            Qs = R_["Qs"].next()
                            for hl in range(2):
                                op("dve", [Es, qT], [Qs], lambda e: e.tensor_tensor(
                                    out=Qs[64 * hl:64 * hl + 64, hl, :, :], in0=Es[64 * hl:64 * hl + 64, :, :], in1=qT[64 * hl:64 * hl + 64, :, :], op=ALU.mult))
                            Kvs = []
                            for i in range(4):
                                lo, hi = (0, 32 * (i + 1)) if d == 0 else (32 * i, 128)
                                Ek = Eks[i]
                                Kv = R_["Kv"][i].next()
                                op("pool" if i % 2 == 1 else "dve", [Ek, kT], [Kv],
                                   lambda e: e.tensor_tensor(out=Kv[:, :, lo:hi], in0=Ek[:, :, lo:hi], in1=kT[:, :, lo:hi], op=ALU.mult))
                                Kvs.append(Kv)
                            c.update(Qs=Qs, Kvs=Kvs)

                        def a_s4(c):
                            d, Kvs, Qs = c["d"], c["Kvs"], c["Qs"]
                            R_ = St[d]
                            pa = pAT.next()
                            for h in range(4):
                                pt = h // 2
                                for i in range(4):
                                    Kv = Kvs[i]
                                    op("pe", [Kv, Qs], [pa], lambda e: e.matmul(
                                        pa[:, h, 32 * i:32 * i + 32], lhsT=Kv[:, pt, :], rhs=Qs[:, h % 2, pt, 32 * i:32 * i + 32],
                                        start=True, stop=True))
                            ATm = R_["ATm"].next()
                            op("dve", [pa, CB], [ATm], lambda e: e.tensor_tensor(
                                out=ATm[:], in0=pa[:], in1=MASK[d].unsqueeze(1).to_broadcast([128, 4, 128]), op=ALU.mult))
                            c["ATm"] = ATm

                        def a_s5(c):
                            ti, d, vt, kd, E1, Qf, ATm = c["ti"], c["d"], c["vt"], c["kd"], c["E1"], c["Qf"], c["ATm"]
                            R_ = St[d]
                            S, Sb = R_["S"], R_["Sb"]
                            r0 = ti * 128
                            po = pO.next()
                            for pt in range(2):
                                op("pe", [Qf, Sb], [po], lambda e: e.matmul(po[:, 128 * pt:128 * pt + 128], lhsT=Qf[:, pt, :], rhs=Sb[:, pt, :],
                                                                            start=True, stop=False, skip_group_check=True))
                                for hl in range(2):
                                    h = 2 * pt + hl
                                    op("pe", [ATm, vt], [po], lambda e: e.matmul(po[:, 64 * h:64 * h + 64], lhsT=ATm[:, h, :], rhs=vt[:, 64 * h:64 * h + 64],
                                                                                start=False, stop=(hl == 1), skip_group_check=True))
                            pu = pU.next()
                            for pt in range(2):
                                op("pe", [kd, vt], [pu], lambda e: e.matmul(pu[:, pt, :], lhsT=kd[:, pt * 128:(pt + 1) * 128], rhs=vt[:, pt * 128:(pt + 1) * 128],
                                                                           start=True, stop=True))
                            col = 127 if d == 0 else 0
                            for pt in range(2):
                                for hl in range(2):
                                    bs = 64 * hl
                                    op("dve", [S, E1, pu], [S], lambda e: e.scalar_tensor_tensor(
                                        out=S[bs:bs + 64, pt, :], in0=S[bs:bs + 64, pt, :], scalar=E1[bs:bs + 64, pt, col:col + 1],
                                        in1=pu[bs:bs + 64, pt, bs:bs + 64], op0=ALU.mult, op1=ALU.add))
                            for hl in range(2):
                                bs = 64 * hl
                                op("act", [S], [Sb], lambda e: e.copy(out=Sb[bs:bs + 64, :, bs:bs + 64], in_=S[bs:bs + 64, :, :]))
                            ost = R_["ost"].next()
                            op("act", [po], [ost], lambda e: e.copy(out=ost[:], in_=po[:]))
                            c["store"] = lambda: dma("pool", dr, ost, OA[d][r0:r0 + 128, :], ost[:], sem_buf=ost)

                        def a_s6(c):
                            c["store"]()

                        seq = []
                        for j in range(NT):
                            seq += [dict(ti=j, d=0), dict(ti=BWD_ORDER[j], d=1)]
                        run_skewed([a_s0, a_s1, a_s2, a_s3, a_s4, a_s5, a_s6], seq, name="A")
                        fw.barrier()
                        for R_ in St:
                            for k_ in ("lf", "qT", "kT", "kt", "vt", "ost"):
                                fw.release(R_[k_].bufs)

                bc_ = ExitStack()
                c_pre = []
                if "C" in phases:
                    KT = fw.sb(bc_, "cKT", [128, NT, 3, 128], BF16)
                    VX = fw.sb(bc_, "cVX", [128, NT, 6, 65], BF16)
                    BIAS = fw.sb(bc_, "cBias", [128, 54, 128], BF16)
                    bst = Ring([fw.sb(bc_, "cbst%d" % i, [128, 6, 128], F32) for i in range(2)])
                    op("pool", [], [VX], lambda e: e.memset(VX[:, :, :, 64:65], 1.0))

                    def _mk_bias(g_):
                        def f():
                            b = bst.next()
                            dma("sp", b, dr, b[:], rpbt[l, g_ * 6:(g_ + 1) * 6].rearrange("k s t -> s k t"))
                            op("act", [b], [BIAS], lambda e: e.copy(out=BIAS[:, g_ * 6:(g_ + 1) * 6, :], in_=b[:]))
                        return f

                    def _mk_kv(ti):
                        def f():
                            dma("sp", KT, dr, KT[:, ti, :, :], FM[ti, :, 9:12, :])
                            dma("sp", VX, dr, VX[:, ti, :, 0:64],
                                TB[ti * 128:(ti + 1) * 128, 2176:2560].rearrange("p (h d) -> p h d", h=6))
                        return f
                    c_pre = [_mk_bias(g_) for g_ in range(9)] + [_mk_kv(ti) for ti in range(NT)]

                if "B" in phases:
                    with ExitStack() as bs_:
                        def ring(name, shape, dt, n=3, ps=False):
                            return Ring([(fw.ps if ps else fw.sb)(bs_, "%s%d" % (name, i), shape, dt) for i in range(n)])
                        St = []
                        for d in range(2):
                            C = fw.sb(bs_, "C%d" % d, [128, 4, 97], F32)
                            Cb = fw.sb(bs_, "Cb%d" % d, [128, 4, 97], BF16)
                            op("dve", [], [C], lambda e: e.memset(C[:], 0.0))
                            op("dve", [], [Cb], lambda e: e.memset(Cb[:], 0.0))
                            R_ = dict(C=C, Cb=Cb, g=ring("bg%d" % d, [128, 16], F32), qT=ring("bqT%d" % d, [128, 4, 128], BF16),
                                      kT=ring("bkT%d" % d, [128, 4, 128], BF16), kt=ring("bkt%d" % d, [128, 384], BF16),
                                      vx=ring("bvx%d" % d, [128, 4, 97], BF16), X=ring("bX%d" % d, [128, 16], F32),
                                      E=ring("bE%d" % d, [128, 16], F32), kh=ring("bkh%d" % d, [128, 384], BF16),
                                      ST=ring("bST%d" % d, [128, 4, 128], BF16), ST0=ring("bST0%d" % d, [128, 4, 128], BF16, 2), u=ring("bu%d" % d, [128, 4, 97], F32),
                                      dd=ring("bdd%d" % d, [128, 8], F32), ost=ring("bos%d" % d, [128, 384], F32, 3))
                            for b in R_["vx"].bufs:
                                op("dve", [], [b], lambda e: e.memset(b[:], 1.0))
                            St.append(R_)
                        pg = ring("bpg", [128, 16], F32, 2, ps=True)
                        psc = ring("bpsc", [128, 4, 128], F32, 2, ps=True)
                        pout = ring("bpout", [128, 4, 128], F32, 2, ps=True)
                        pdu = ring("bpdu", [128, 4, 128], F32, 1, ps=True)

                        def b_s0(c):
                            ti, d = c["ti"], c["d"]
                            R_ = St[d]
                            if c_pre:
                                c_pre.pop(0)()
                            g = R_["g"].next(); qT = R_["qT"].next(); kT = R_["kT"].next(); kt = R_["kt"].next(); vx = R_["vx"].next()
                            r0 = ti * 128
                            dma("sp", g, dr, g[:], TF[r0:r0 + 128, 512:528])
                            dma("sp", qT, dr, qT[:], FM[ti, :, 12:16, :])
                            dma("sp", kT, dr, kT[:], FM[ti, :, 16:20, :])
                            dma("sp", kt, dr, kt[:], TB[r0:r0 + 128, 1024:1408])
                            dma("sp", vx, dr, vx[:, :, 0:96], TB[r0:r0 + 128, 1408:1792].rearrange("p (h d) -> p h d", h=4))
                            ig = g.t[:, 4 * d:4 * d + 4]
                            lfd = g.t[:, 8 + 4 * d:12 + 4 * d]
                            p_ = pg.next()
                            op("pe", [g, CST], [p_], lambda e: e.matmul(p_[:, 0:4], lhsT=TRI[d], rhs=lfd, start=True, stop=True))
                            op("pe", [g, CST], [p_], lambda e: e.matmul(p_[:, 4:8], lhsT=TRIS[d], rhs=lfd, start=True, stop=True))
                            op("pe", [g, CST], [p_], lambda e: e.matmul(p_[:, 8:12], lhsT=ONESF, rhs=lfd, start=True, stop=True))
                            sc = psc.next()
                            for h in range(4):
                                op("pe", [kT, qT], [sc], lambda e: e.matmul(sc[:, h, :], lhsT=kT[0:96, h, :], rhs=qT[0:96, h, :], start=True, stop=True))
                            X = R_["X"].next()
                            op("dve", [g, p_], [X], lambda e: e.tensor_tensor(out=X[:, 0:4], in0=ig, in1=p_[:, 0:4], op=ALU.subtract))
                            op("dve", [g, p_], [X], lambda e: e.tensor_tensor(out=X[:, 4:8], in0=ig, in1=p_[:, 4:8], op=ALU.add))
                            c.update(qT=qT, kt=kt, vx=vx, X=X, sc=sc, p_=p_)

                        def b_s1(c):
                            d, kt, X = c["d"], c["kt"], c["X"]
                            R_ = St[d]
                            E = R_["E"].next()
                            p_ = c["p_"]
                            op("act", [X], [E], lambda e: e.activation(out=E[:, 0:8], in_=X[:, 0:8], func=AF.Exp))
                            op("act", [p_], [E], lambda e: e.activation(out=E[:, 8:12], in_=p_[:, 0:4], func=AF.Exp))
                            op("act", [p_], [E], lambda e: e.activation(out=E[:, 12:16], in_=p_[:, 8:12], func=AF.Exp))
                            kh = R_["kh"].next()
                            op("pool", [kt, E], [kh], lambda e: e.tensor_tensor(
                                out=kh[:].rearrange("p (h d) -> p h d", h=4), in0=kt[:].rearrange("p (h d) -> p h d", h=4),
                                in1=E.t[:, 4:8].unsqueeze(2).to_broadcast([128, 4, 96]), op=ALU.mult))
                            c.update(E=E, kh=kh)

                        def b_s2(c):
                            d, sc, E = c["d"], c["sc"], c["E"]
                            R_ = St[d]
                            ST = R_["ST"].next()
                            S0_ = R_["ST0"].next()
                            for h in range(4):
                                op("act", [sc, E], [S0_], lambda e: e.activation(out=S0_[:, h, :], in_=sc[:, h, :], func=AF.Copy, scale=E[:, h:h + 1]))
                            op("pool", [S0_, CB], [ST], lambda e: e.tensor_tensor(
                                out=ST[:], in0=S0_[:], in1=MASK[d].unsqueeze(1).to_broadcast([128, 4, 128]), op=ALU.mult))
                            c.update(ST=ST)

                        def b_s3(c):
                            d, qT, vx, E, kh, ST = c["d"], c["qT"], c["vx"], c["E"], c["kh"], c["ST"]
                            R_ = St[d]
                            C, Cb = R_["C"], R_["Cb"]
                            po = pout.next()
                            for h in range(4):
                                op("pe", [ST, vx], [po], lambda e: e.matmul(po[:, h, 0:97], lhsT=ST[:, h, :], rhs=vx[:, h, :], start=True, stop=False))
                                op("pe", [qT, Cb], [po], lambda e: e.matmul(po[:, h, 0:97], lhsT=qT[0:96, h, :], rhs=Cb[0:96, h, :], start=False, stop=True))
                            pd = pdu.next()
                            for h in range(4):
                                op("pe", [kh, vx], [pd], lambda e: e.matmul(pd[0:96, h, 0:97], lhsT=kh[:, 96 * h:96 * h + 96], rhs=vx[:, h, :], start=True, stop=True))
                            op("dve", [C, E], [C], lambda e: e.tensor_tensor(
                                out=C[0:96, :, :], in0=C[0:96, :, :], in1=E.t[0:96, 12:16].unsqueeze(2).to_broadcast([96, 4, 97]), op=ALU.mult))
                            op("dve", [C, pd], [C], lambda e: e.tensor_tensor(out=C[0:96, :, :], in0=C[0:96, :, :], in1=pd[0:96, :, 0:97], op=ALU.add))
                            op("dve", [C], [Cb], lambda e: e.tensor_copy(out=Cb[0:96, :, :], in_=C[0:96, :, :]))
                            c.update(po=po)

                        def b_s4(c):
                            ti, d, po, E = c["ti"], c["d"], c["po"], c["E"]
                            R_ = St[d]
                            r0 = ti * 128
                            u = R_["u"].next()
                            for h in range(4):
                                op("act", [po, E], [u], lambda e: e.activation(out=u[:, h, :], in_=po[:, h, 0:97], func=AF.Copy, scale=E[:, 8 + h:9 + h]))
                            dd = R_["dd"].next()
                            op("dve", [u], [dd], lambda e: e.tensor_scalar_max(out=dd[:, 0:4], in0=u[:, :, 96], scalar1=1.0))
                            op("dve", [u, dd], [dd], lambda e: e.scalar_tensor_tensor(out=dd[:, 4:8], in0=u[:, :, 96], scalar=-1.0, in1=dd[:, 0:4],
                                                                                     op0=ALU.mult, op1=ALU.max))
                            op("dve", [dd], [dd], lambda e: e.reciprocal(out=dd[:, 0:4], in_=dd[:, 4:8]))
                            ost = R_["ost"].next()
                            op("pool", [u, dd], [ost], lambda e: e.tensor_tensor(
                                out=ost[:].rearrange("p (h d) -> p h d", h=4), in0=u[:, :, 0:96],
                                in1=dd.t[:, 0:4].unsqueeze(2).to_broadcast([128, 4, 96]), op=ALU.mult))
                            c["store"] = lambda: dma("pool", dr, ost, OB[d][r0:r0 + 128, :], ost[:], sem_buf=ost)

                        def b_s5(c):
                            c["store"]()

                        seq = []
                        for j in range(NT):
                            seq += [dict(ti=j, d=0), dict(ti=BWD_ORDER[j], d=1)]
                        run_skewed([b_s0, b_s1, b_s2, b_s3, b_s4, b_s5], seq, name="B")
                        fw.barrier()
                        for R_ in St:
                            for k_ in ("g", "qT", "kT", "kt", "vx", "ost"):
                                fw.release(R_[k_].bufs)

                if "C" in phases:
                    with ExitStack() as cs_:
                        while c_pre:
                            c_pre.pop(0)()
                        qr_ = Ring([fw.sb(cs_, "cq%d" % i, [128, 3, 128], BF16) for i in range(2)])
                        qzr = Ring([fw.sb(cs_, "cqz%d" % i, [128, 6, 128], BF16) for i in range(3)])
                        for b in qzr.bufs:
                            op("dve", [], [b], lambda e: e.memset(b[:], 0.0))
                        PTr = Ring([fw.sb(cs_, "cPT%d" % i, [128, 7, 128], BF16) for i in range(3)])
                        osr = Ring([fw.sb(cs_, "cos%d" % i, [128, 384], F32) for i in range(3)])
                        rrr = Ring([fw.sb(cs_, "crr%d" % i, [128, 6], F32) for i in range(2)])
                        psct = Ring([fw.ps(cs_, "cps%d" % i, [128, 8, 128], F32) for i in range(3)])
                        pcout = Ring([fw.ps(cs_, "cpo%d" % i, [128, 6, 65], F32) for i in range(2)])
                        tiles = list(range(NT)) if not last else list(range(NCTX, NT))
                        TS = {}

                        def c_keys(ti):
                            if ti < NCTX:
                                return [(0, None), (1, None)]
                            return [(NCTX + kt, kind) for kt, kind in _key_tiles(ti - NCTX)] + [(0, None), (1, None)]

                        def c_s0(c):
                            ti, h = c["ti"], c["h"]
                            if h == 0:
                                q = qr_.next()
                                dma("sp", q, dr, q[:], FM[ti, :, 6:9, :])
                                qz = qzr.next()
                                op("pool", [q], [qz], lambda e: e.tensor_copy(out=qz[0:64, 0:6:2, :], in_=q[0:64, :, :]))
                                op("pool", [q], [qz], lambda e: e.tensor_copy(out=qz[64:128, 1:6:2, :], in_=q[64:128, :, :]))
                                TS[ti] = dict(qz=qz, po=pcout.next())
                            qz = TS[ti]["qz"]
                            c["po"] = TS[ti]["po"]
                            keys = c_keys(ti)
                            pt = h // 2
                            sc = psct.next()
                            for j, (kt, kind) in enumerate(keys):
                                op("pe", [KT, qz], [sc], lambda e: e.matmul(sc[:, j, :], lhsT=KT[:, kt, pt, :], rhs=qz[:, h, :],
                                                                           start=True, stop=(kind is None)))
                                if kind is not None:
                                    op("pe", [CB, BIAS], [sc], lambda e: e.matmul(sc[:, j, :], lhsT=IDENT, rhs=BIAS[:, h * 9 + kind, :], start=False, stop=True))
                            c["sc"] = sc

                        def c_s1(c):
                            nk = len(c_keys(c["ti"]))
                            sc = c["sc"]
                            PT = PTr.next()
                            op("act", [sc], [PT], lambda e: e.activation(out=PT[:, 0:nk, :], in_=sc[:, 0:nk, :], func=AF.Exp))
                            c["PT"] = PT

                        def c_s2(c):
                            ti, h, PT, po = c["ti"], c["h"], c["PT"], c["po"]
                            keys = c_keys(ti)
                            nk = len(keys)
                            for j, (kt, kind) in enumerate(keys):
                                op("pe", [PT, VX], [po], lambda e: e.matmul(po[:, h, :], lhsT=PT[:, j, :], rhs=VX[:, kt, h, :], start=(j == 0), stop=(j == nk - 1)))
                            if h == 5:
                                rr = rrr.next()
                                op("dve", [po], [rr], lambda e: e.reciprocal(out=rr[:], in_=po[:, :, 64]))
                                ost = osr.next()
                                op("dve", [po, rr], [ost], lambda e: e.tensor_tensor(
                                    out=ost[:].rearrange("p (h d) -> p h d", h=6), in0=po[:, :, 0:64],
                                    in1=rr[:].unsqueeze(2).to_broadcast([128, 6, 64]), op=ALU.mult))
                                c["store"] = lambda: dma("pool", dr, ost, OC[ti * 128:(ti + 1) * 128, :], ost[:], sem_buf=ost)

                        def c_s3(c):
                            if "store" in c:
                                c["store"]()

                        run_skewed([c_s0, c_s1, c_s2, c_s3], [dict(ti=ti, h=h) for ti in tiles for h in range(6)], name="C")
                        fw.barrier()
                        fw.release([KT, VX] + bst.bufs + qr_.bufs + osr.bufs)
                bc_.close()

                if "O" in phases:
                    with ExitStack() as os_:
                        WO = fw.sb(os_, "wo", [128, 8, D], BF16)
                        wst = Ring([fw.sb(os_, "wost%d" % i, [128, D], F32) for i in range(2)])
                        for kc in range(8):
                            w = wst.next()
                            dma("sp", w, dr, w[:], w_out[l, kc * 128:(kc + 1) * 128, :])
                            op("dve", [w], [WO], lambda e: e.tensor_copy(out=WO[:, kc, :], in_=w[:]))

                        def ring(name, shape, dt, n=2, ps=False):
                            return Ring([(fw.ps if ps else fw.sb)(os_, "%s%d" % (name, i), shape, dt) for i in range(n)])
                        oa = [ring("ooa%d" % d, [128, 256], F32) for d in range(2)]
                        ob = [ring("oob%d" % d, [128, 384], F32) for d in range(2)]
                        oc = ring("ooc", [128, 384], F32, 3)
                        gz = ring("ogz", [128, 3, 384], BF16, 3)
                        xr = ring("ox", [128, D], F32, 6)
                        xo = ring("oxo", [128, D], F32, 3)
                        Y = ring("oY", [128, D], BF16, 3)
                        YT = ring("oYT", [128, 8, 128], BF16, 3)
                        sAr = ring("osA", [128, 256], F32, 3)
                        sBr = ring("osB", [128, 384], F32, 3)
                        tAr = ring("otA", [128, 256], BF16, 2)
                        tBr = ring("otB", [128, 384], F32, 2)
                        st8r = ring("ost8", [128, 16], F32, 3)
                        stAr = ring("ostA", [128, 16], F32, 3)
                        stBr = ring("ostB", [128, 16], F32, 3)
                        bigr = ring("obig", [128, D], F32, 2)
                        junk = ring("ojunk", [128, 512], BF16, 2)
                        pyt = ring("opyt", [128, 8, 128], BF16, 2, ps=True)
                        pu = ring("opu", [128, 2, 512], F32, 3, ps=True)
                        tiles = list(range(NT)) if not last else list(range(NCTX, NT))

                        def o_s0(c):
                            ti = c["ti"]
                            r0 = ti * 128
                            a0, a1 = oa[0].next(), oa[1].next()
                            b0, b1 = ob[0].next(), ob[1].next()
                            c_ = oc.next(); g_ = gz.next(); xt = xr.next()
                            dma("sp", a0, dr, a0[:], OA[0][r0:r0 + 128, :]); dma("sp", a1, dr, a1[:], OA[1][r0:r0 + 128, :])
                            dma("sp", b0, dr, b0[:], OB[0][r0:r0 + 128, :]); dma("sp", b1, dr, b1[:], OB[1][r0:r0 + 128, :])
                            dma("sp", c_, dr, c_[:], OC[r0:r0 + 128, :])
                            dma("sp", g_, dr, g_[:, 0, 0:256], TB[r0:r0 + 128, 768:1024])
                            dma("sp", g_, dr, g_[:, 1, :], TB[r0:r0 + 128, 1792:2176])
                            dma("sp", g_, dr, g_[:, 2, :], TB[r0:r0 + 128, 2560:2944])
                            dma("sp", xt, dr, xt[:], x_src(l, ti))
                            sA = sAr.next(); sB = sBr.next(); tA = tAr.next(); tB = tBr.next(); stA = stAr.next(); stB = stBr.next()
                            c.update(c_=c_, g_=g_, xt=xt, sA=sA, sB=sB, stA=stA, stB=stB)
                            op("pool", [a0, a1], [sA], lambda e: e.tensor_tensor(out=sA[:], in0=a0[:], in1=a1[:], op=ALU.add))
                            for h in range(4):
                                op("act", [sA], [tA, stA], lambda e: e.activation(out=tA[:, 64 * h:64 * h + 64], in_=sA[:, 64 * h:64 * h + 64],
                                                                                   func=AF.Square, accum_out=stA[:, h:h + 1]))
                            op("pool", [stA], [stA], lambda e: e.tensor_scalar(out=stA[:, 0:4], in0=stA[:, 0:4], scalar1=1.0 / 64, scalar2=64.0 * EPS,
                                                                              op0=ALU.mult, op1=ALU.add))
                            op("dve", [b0, b1], [sB], lambda e: e.tensor_tensor(out=sB[:], in0=b0[:], in1=b1[:], op=ALU.add))
                            op("dve", [sB], [tB], lambda e: e.tensor_tensor(out=tB[:], in0=sB[:], in1=sB[:], op=ALU.mult))
                            op("dve", [tB], [stB], lambda e: e.tensor_reduce(out=stB[:, 4:8], in_=tB[:].rearrange("p (h d) -> p h d", h=4),
                                                                            axis=AX.X, op=ALU.add))
                            op("dve", [stB], [stB], lambda e: e.tensor_scalar(out=stB[:, 4:8], in0=stB[:, 4:8], scalar1=1.0 / 96, scalar2=EPS,
                                                                              op0=ALU.mult, op1=ALU.add))

                        def o_s1(c):
                            c_, g_, sA, sB, stA, stB = c["c_"], c["g_"], c["sA"], c["sB"], c["stA"], c["stB"]
                            y = Y.next()
                            c["y"] = y
                            op("act", [stA], [stA], lambda e: e.activation(out=stA[:, 0:4], in_=stA[:, 0:4], func=AF.Ln))
                            op("act", [stA], [stA], lambda e: e.activation(out=stA[:, 0:4], in_=stA[:, 0:4], func=AF.Exp, scale=-0.5))
                            op("act", [stB], [stB], lambda e: e.activation(out=stB[:, 4:8], in_=stB[:, 4:8], func=AF.Ln))
                            op("act", [stB], [stB], lambda e: e.activation(out=stB[:, 4:8], in_=stB[:, 4:8], func=AF.Exp, scale=-0.5))
                            op("pool", [sA, stA], [sA], lambda e: e.tensor_tensor(
                                out=sA[:].rearrange("p (h d) -> p h d", h=4), in0=sA[:].rearrange("p (h d) -> p h d", h=4),
                                in1=stA.t[:, 0:4].unsqueeze(2).to_broadcast([128, 4, 64]), op=ALU.mult))
                            op("pool", [sA, GNA], [sA], lambda e: e.tensor_tensor(out=sA[:], in0=sA[:], in1=GNA[:], op=ALU.mult))
                            op("pool", [sA, g_], [y], lambda e: e.tensor_tensor(out=y[:, 0:256], in0=sA[:], in1=g_[:, 0, 0:256], op=ALU.mult))
                            op("dve", [sB, stB], [sB], lambda e: e.tensor_tensor(
                                out=sB[:].rearrange("p (h d) -> p h d", h=4), in0=sB[:].rearrange("p (h d) -> p h d", h=4),
                                in1=stB.t[:, 4:8].unsqueeze(2).to_broadcast([128, 4, 96]), op=ALU.mult))
                            op("dve", [sB, GNB], [sB], lambda e: e.tensor_tensor(out=sB[:], in0=sB[:], in1=GNB[:], op=ALU.mult))
                            op("dve", [sB, g_], [y], lambda e: e.tensor_tensor(out=y[:, 256:640], in0=sB[:], in1=g_[:, 1, :], op=ALU.mult))
                            op("dve", [c_, g_], [y], lambda e: e.tensor_tensor(out=y[:, 640:1024], in0=c_[:], in1=g_[:, 2, :], op=ALU.mult))

                        def o_s2(c):
                            y = c["y"]
                            py = pyt.next()
                            for kc in range(8):
                                op("pe", [y, CB], [py], lambda e: e.transpose(out=py[:, kc, :], in_=y[:, kc * 128:(kc + 1) * 128], identity=IDENT))
                            yT = YT.next()
                            c["yT"] = yT
                            op("act", [py], [yT], lambda e: e.copy(out=yT[:], in_=py[:]))

                        def o_s3(c):
                            yT = c["yT"]
                            u = pu.next()
                            st8 = st8r.next()
                            c.update(u=u, st8=st8)
                            for nb in range(2):
                                for kc in range(8):
                                    op("pe", [yT, WO], [u], lambda e: e.matmul(u[:, nb, :], lhsT=yT[:, kc, :], rhs=WO[:, kc, nb * 512:(nb + 1) * 512],
                                                                              start=(kc == 0), stop=(kc == 7)))
                            for nb in range(2):
                                jk = junk.next()
                                op("act", [u], [jk, st8], lambda e: e.activation(out=jk[:], in_=u[:, nb, :], func=AF.Square,
                                                                               accum_out=st8[:, 8 + nb:9 + nb]))

                        def o_s4(c):
                            ti, u, st8, xt = c["ti"], c["u"], c["st8"], c["xt"]
                            m = MODS[1 if ti < NCTX else 0]
                            r0 = ti * 128
                            big = bigr.next()
                            op("dve", [st8], [st8], lambda e: e.tensor_tensor(out=st8[:, 10:11], in0=st8[:, 8:9], in1=st8[:, 9:10], op=ALU.add))
                            op("dve", [st8], [st8], lambda e: e.tensor_scalar(out=st8[:, 10:11], in0=st8[:, 10:11], scalar1=1.0 / D, scalar2=EPS,
                                                                              op0=ALU.mult, op1=ALU.add))
                            op("act", [st8], [st8], lambda e: e.activation(out=st8[:, 10:11], in_=st8[:, 10:11], func=AF.Ln))
                            op("act", [st8], [st8], lambda e: e.activation(out=st8[:, 10:11], in_=st8[:, 10:11], func=AF.Exp, scale=-0.5))
                            op("dve", [u, st8, m], [big], lambda e: e.scalar_tensor_tensor(
                                out=big[:].rearrange("p (a b) -> p a b", a=2), in0=u[:], scalar=st8[:, 10:11],
                                in1=m[:, 2 * D:3 * D].rearrange("p (a b) -> p a b", a=2), op0=ALU.mult, op1=ALU.mult))
                            xo_ = xo.next()
                            op("dve", [big, xt], [xo_], lambda e: e.tensor_tensor(out=xo_[:], in0=big[:], in1=xt[:], op=ALU.add))

                            def store():
                                if last:
                                    dma("pool", dr, xo_, y_out[(ti - NCTX) * 128:(ti - NCTX + 1) * 128, :], xo_[:], sem_buf=xo_)
                                else:
                                    dma("pool", dr, xo_, XC[r0:r0 + 128, :], xo_[:], sem_buf=xo_)
                            c["store"] = store

                        def o_s5(c):
                            c["store"]()

                        run_skewed([o_s0, o_s1, o_s2, o_s3, o_s4, o_s5], [dict(ti=ti) for ti in tiles], name="O")
                        fw.barrier()
                        fw.release(wst.bufs + oa[0].bufs + oa[1].bufs + ob[0].bufs + ob[1].bufs + oc.bufs + gz.bufs + xr.bufs + xo.bufs)
                fw.barrier()
                fw.release([LB, OML, GNA, GNB, GB])
        fw.barrier()
    return nc


def make_in_maps(inputs, cores):
    x = np.asarray(inputs["x"], np.float32)
    c = np.asarray(inputs["c"], np.float32)
    ctx = np.asarray(inputs["ctx"], np.float32)
    c_ctx = np.asarray(inputs["c_ctx"], np.float32)
    shared = {
        "w_mod": np.ascontiguousarray(inputs["w_mod"], np.float32),
        "b_mod": np.ascontiguousarray(inputs["b_mod"], np.float32),
        "g_pre": np.ascontiguousarray(inputs["g_pre"], np.float32),
        "g_post": np.ascontiguousarray(inputs["g_post"], np.float32),
        "w_in": np.ascontiguousarray(inputs["w_in"], np.float32),
        "w_out": np.ascontiguousarray(inputs["w_out"], np.float32),
        "hgrn_lb": np.ascontiguousarray(np.asarray(inputs["hgrn_lb"], np.float32).reshape(DEPTH, 512)),
        "hgrn_gn": np.ascontiguousarray(inputs["hgrn_gn"], np.float32),
        "gate_b": np.ascontiguousarray(np.asarray(inputs["mlstm_gate_b"], np.float32).reshape(DEPTH, 16)),
        "mlstm_gn": np.ascontiguousarray(inputs["mlstm_gn"], np.float32),
        "rpbt": _rpb_tables(np.asarray(inputs["na_rpb"], np.float32)),
        "rope": _rope_tables().reshape(NT, 128, 384),
        "consts": _consts(),
    }
    maps = []
    for b in cores:
        cT = np.concatenate([c[b].reshape(8, 128).T, c_ctx.reshape(8, 128).T], axis=1)
        m = dict(shared)
        m["x"] = np.ascontiguousarray(x[b])
        m["ctx"] = np.ascontiguousarray(ctx[b])
        m["cT"] = np.ascontiguousarray(cT, np.float32)
        maps.append(m)
    return maps


def kernel(**inputs):
    nc = build()
    maps = make_in_maps(inputs, list(range(8)))
    res = run_bass_kernel_spmd(nc, maps, core_ids=list(range(8)))
    out = np.stack([np.asarray(r["y"], np.float32) for r in res.results], axis=0)
    return out
```
